# Optimizing a Trainium2 kernel written in Bass

```python
import math
import jax
import jax.numpy as jnp
from jax import lax
import numpy as np

D_MODEL = 2048
BATCH = 8
SEQ = 2048
DEPTH = 2

GRID_W = 64
CTX_LEN = 256
EXPAND = 2
MIX_WIDTH = EXPAND * D_MODEL

HEAD_DIM = 128
ATTN_WIDTH = MIX_WIDTH // 2
N_Q_HEADS = ATTN_WIDTH // HEAD_DIM
N_KV_HEADS = N_Q_HEADS // 4
Q_PER_KV = N_Q_HEADS // N_KV_HEADS
KV_WIDTH = N_KV_HEADS * HEAD_DIM
Q_BLOCK = 128
ROPE_THETA = 10000.0
ROPE_AXIS_DIM = HEAD_DIM // 2

SSD_WIDTH = MIX_WIDTH - ATTN_WIDTH
SSD_HEAD_DIM = 64
N_SSD_HEADS = SSD_WIDTH // SSD_HEAD_DIM
N_SSD_GROUPS = 4
HEADS_PER_GROUP = N_SSD_HEADS // N_SSD_GROUPS
D_STATE = 128
D_CONV = 5
SSD_CHUNK = 128
CONV_CH = SSD_WIDTH + 2 * N_SSD_GROUPS * D_STATE

CTX_COL0 = 2 * ATTN_WIDTH + SSD_WIDTH
EV_SPLITS = [ATTN_WIDTH, 2 * ATTN_WIDTH, CTX_COL0, CTX_COL0 + KV_WIDTH, CTX_COL0 + 2 * KV_WIDTH, CTX_COL0 + 2 * KV_WIDTH + CONV_CH]
CTX_SPLITS = [KV_WIDTH, 2 * KV_WIDTH, 2 * KV_WIDTH + CONV_CH]
EV_IN_COLS = CTX_COL0 + 2 * KV_WIDTH + CONV_CH + 2 * N_SSD_HEADS

FOURIER_WIDTH = MIX_WIDTH
N_FOURIER_GROUPS = 8
FOURIER_GROUP = FOURIER_WIDTH // N_FOURIER_GROUPS

N_EVEN = (DEPTH + 1) // 2
N_ODD = DEPTH // 2
RMS_EPS = 1e-6

kernel_name = 'hybrid_dit_gqa_ssd_fourier'


def rms_norm(x, w):
    xf = x.astype(jnp.float32)
    y = xf * lax.rsqrt(jnp.mean(xf * xf, axis=-1, keepdims=True) + RMS_EPS)
    return (y * w.astype(jnp.float32)).astype(x.dtype)


def ada_mod(cond, ada_w, ada_b):
    return jax.nn.silu(cond) @ ada_w + ada_b


def modulate(x, norm_w, mod):
    shift, scale, gate = jnp.split(mod, 3, axis=-1)
    h = rms_norm(x, norm_w) * (1 + scale[:, None]) + shift[:, None]
    return h, gate[:, None]


def heads(t, n, d):
    return t.reshape(t.shape[0], t.shape[1], n, d)


def axial_rope_angles(rows):
    pos = jnp.arange(rows * GRID_W)
    row = (pos // GRID_W).astype(jnp.float32)
    col = (pos % GRID_W).astype(jnp.float32)
    inv_freq = ROPE_THETA ** (-jnp.arange(0, ROPE_AXIS_DIM, 2, dtype=jnp.float32) / ROPE_AXIS_DIM)
    return row[:, None] * inv_freq, col[:, None] * inv_freq


def rotate_axis(x, ang):
    cos = jnp.cos(ang)[None, :, None, :].astype(x.dtype)
    sin = jnp.sin(ang)[None, :, None, :].astype(x.dtype)
    x1, x2 = jnp.split(x, 2, axis=-1)
    return jnp.concatenate([x1 * cos - x2 * sin, x1 * sin + x2 * cos], axis=-1)


def apply_axial_rope(x, ang_row, ang_col):
    return jnp.concatenate([rotate_axis(x[..., :ROPE_AXIS_DIM], ang_row),
                            rotate_axis(x[..., ROPE_AXIS_DIM:], ang_col)], axis=-1)


def attend(q, k, v):
    s = jnp.einsum('bqgrd,bkgd->bgrqk', q.astype(jnp.float32), k.astype(jnp.float32)) * (HEAD_DIM ** -0.5)
    p = jax.nn.softmax(s, axis=-1)
    return jnp.einsum('bgrqk,bkgd->bqgrd', p, v.astype(jnp.float32)).astype(v.dtype)


def latent_attention(q, k_all, v_all):
    b, s = q.shape[:2]
    nblk = s // Q_BLOCK
    qb = q.reshape(b, nblk, Q_BLOCK, N_KV_HEADS, Q_PER_KV, HEAD_DIM).swapaxes(0, 1)
    out = lax.map(lambda qi: attend(qi, k_all, v_all), qb)
    return out.swapaxes(0, 1).reshape(b, s, ATTN_WIDTH)


def depthwise_conv(x, w, bias):
    y = lax.conv_general_dilated(x, w[:, None, :].astype(x.dtype), window_strides=(1,),
                                 padding=[(D_CONV // 2, D_CONV // 2)],
                                 dimension_numbers=('NWC', 'WIO', 'NWC'),
                                 feature_group_count=x.shape[-1])
    return jax.nn.silu(y + bias)


def segsum(a):
    t = a.shape[-1]
    a_rep = jnp.broadcast_to(a[..., :, None], a.shape + (t,))
    strict = jnp.tril(jnp.ones((t, t), dtype=bool), -1)
    s = jnp.cumsum(jnp.where(strict, a_rep, 0.0), axis=-2)
    return jnp.where(jnp.tril(jnp.ones((t, t), dtype=bool)), s, -jnp.inf)


def _to_chunks(X, dA, Bm):
    b, L = X.shape[:2]
    nc = L // SSD_CHUNK
    Xc = X.reshape(b, nc, SSD_CHUNK, N_SSD_GROUPS, HEADS_PER_GROUP, SSD_HEAD_DIM)
    Ac = dA.reshape(b, nc, SSD_CHUNK, N_SSD_GROUPS, HEADS_PER_GROUP).transpose(0, 3, 4, 1, 2)
    Bc = Bm.reshape(b, nc, SSD_CHUNK, N_SSD_GROUPS, D_STATE)
    return Xc, Ac, Bc


def _chunk_boundary_states(Xc, A_cs, Bc, h0):
    decay_to_end = jnp.exp(A_cs[..., -1:] - A_cs)
    local = jnp.einsum('bclgn,bgrcl,bclgrp->bcgrpn', Bc, decay_to_end, Xc)
    local = jnp.concatenate([h0[:, None], local], axis=1)
    totals = jnp.pad(A_cs[..., -1], ((0, 0), (0, 0), (0, 0), (1, 0)))
    decay_chunk = jnp.exp(segsum(totals))
    states = jnp.einsum('bgrzc,bcgrpn->bzgrpn', decay_chunk, local)
    return states[:, :-1], states[:, -1]


def ssd_scan(X, dA, Bm, Cm, h0):
    b, L = X.shape[:2]
    Xc, Ac, Bc = _to_chunks(X, dA, Bm)
    Cc = Cm.reshape(Bc.shape)
    A_cs = jnp.cumsum(Ac, axis=-1)
    cb = jnp.einsum('bclgn,bcsgn->bgcls', Cc, Bc)
    y_diag = jnp.einsum('bgrcls,bcsgrp->bclgrp', cb[:, :, None] * jnp.exp(segsum(Ac)), Xc)
    states_in, h_final = _chunk_boundary_states(Xc, A_cs, Bc, h0)
    y_off = jnp.einsum('bclgn,bcgrpn,bgrcl->bclgrp', Cc, states_in, jnp.exp(A_cs))
    return (y_diag + y_off).reshape(b, L, N_SSD_HEADS, SSD_HEAD_DIM), h_final


def ssd_final_state(X, dA, Bm, h0):
    Xc, Ac, Bc = _to_chunks(X, dA, Bm)
    return _chunk_boundary_states(Xc, jnp.cumsum(Ac, axis=-1), Bc, h0)[1]


def ssd_prep(xbc, dt, a_log, dt_bias):
    b, L = xbc.shape[:2]
    xs, bm, cm = jnp.split(xbc.astype(jnp.float32), [SSD_WIDTH, SSD_WIDTH + N_SSD_GROUPS * D_STATE], axis=-1)
    xs = heads(xs, N_SSD_HEADS, SSD_HEAD_DIM)
    bm = heads(bm, N_SSD_GROUPS, D_STATE)
    cm = heads(cm, N_SSD_GROUPS, D_STATE)
    dtp = jax.nn.softplus(dt.astype(jnp.float32).reshape(b, L, 2, N_SSD_HEADS) + dt_bias.astype(jnp.float32))
    dA = dtp * (-jnp.exp(a_log.astype(jnp.float32)))
    return xs, bm, cm, dtp, dA


def _direction(xs, bm, cm, dtp, dA, d):
    X, A = xs * dtp[:, :, d, :, None], dA[:, :, d]
    if d == 1:
        return jnp.flip(X, 1), jnp.flip(A, 1), jnp.flip(bm, 1), jnp.flip(cm, 1)
    return X, A, bm, cm


def _even_out(attn, g, y, xs, z, d_skip, ssd_norm, w_out):
    b, L = g.shape[:2]
    a = attn.reshape(b, L, ATTN_WIDTH) * jax.nn.silu(g)
    y = (y + xs * d_skip.astype(jnp.float32)[:, None]).reshape(b, L, SSD_WIDTH) * jax.nn.silu(z.astype(jnp.float32))
    yg = y.reshape(b, L, N_SSD_GROUPS, SSD_WIDTH // N_SSD_GROUPS)
    yg = yg * lax.rsqrt(jnp.mean(yg * yg, axis=-1, keepdims=True) + RMS_EPS)
    y = (yg.reshape(b, L, SSD_WIDTH) * ssd_norm.astype(jnp.float32)).astype(a.dtype)
    return jnp.concatenate([a, y], axis=-1) @ w_out


def even_layer(x, ctx, c, c_ctx, ang_row, ang_col, norm_w, ada_w, ada_b, w_in, q_norm, k_norm,
               conv_w, conv_b, a_log, dt_bias, d_skip, ssd_norm, w_out, update_ctx):
    b = x.shape[0]
    hx, gate_x = modulate(x, norm_w, ada_mod(c, ada_w, ada_b))
    hc, gate_c = modulate(ctx, norm_w, ada_mod(c_ctx[None], ada_w, ada_b))
    zero_state = jnp.zeros((b, N_SSD_GROUPS, HEADS_PER_GROUP, SSD_HEAD_DIM, D_STATE), jnp.float32)

    if update_ctx:
        qc, gc, zc, kc, vc, xbc_c, dt_c = jnp.split(hc @ w_in, EV_SPLITS, axis=-1)
    else:
        kc, vc, xbc_c, dt_c = jnp.split(hc @ w_in[:, CTX_COL0:], CTX_SPLITS, axis=-1)
    kc = rms_norm(heads(kc, N_KV_HEADS, HEAD_DIM), k_norm)
    vc = heads(vc, N_KV_HEADS, HEAD_DIM)
    xs_c, bm_c, cm_c, dtp_c, dA_c = ssd_prep(depthwise_conv(xbc_c, conv_w, conv_b), dt_c, a_log, dt_bias)
    dirs_c = [_direction(xs_c, bm_c, cm_c, dtp_c, dA_c, d) for d in (0, 1)]
    if update_ctx:
        yf_c, hf_c = ssd_scan(*dirs_c[0], zero_state)
        yb_c, hb_c = ssd_scan(*dirs_c[1], zero_state)
        qc = rms_norm(heads(qc, N_Q_HEADS, HEAD_DIM), q_norm)
        attn_c = attend(qc.reshape(b, qc.shape[1], N_KV_HEADS, Q_PER_KV, HEAD_DIM), kc, vc)
        out_c = _even_out(attn_c, gc, yf_c + jnp.flip(yb_c, 1), xs_c, zc, d_skip, ssd_norm, w_out)
        ctx = ctx + gate_c * out_c
    else:
        hf_c = ssd_final_state(dirs_c[0][0], dirs_c[0][1], dirs_c[0][2], zero_state)
        hb_c = ssd_final_state(dirs_c[1][0], dirs_c[1][1], dirs_c[1][2], zero_state)

    n_lat = x.shape[1]
    q, g, z, k, v, xbc, dt = jnp.split(hx @ w_in, EV_SPLITS, axis=-1)
    q = apply_axial_rope(rms_norm(heads(q, N_Q_HEADS, HEAD_DIM), q_norm), ang_row, ang_col)
    k = apply_axial_rope(rms_norm(heads(k, N_KV_HEADS, HEAD_DIM), k_norm), ang_row, ang_col)
    v = heads(v, N_KV_HEADS, HEAD_DIM)
    attn = latent_attention(q.reshape(b, n_lat, N_KV_HEADS, Q_PER_KV, HEAD_DIM),
                            jnp.concatenate([kc, k], axis=1), jnp.concatenate([vc, v], axis=1))
    xs, bm, cm, dtp, dA = ssd_prep(depthwise_conv(xbc, conv_w, conv_b), dt, a_log, dt_bias)
    y_f, _ = ssd_scan(*_direction(xs, bm, cm, dtp, dA, 0), hf_c)
    y_b, _ = ssd_scan(*_direction(xs, bm, cm, dtp, dA, 1), hb_c)
    out = _even_out(attn, g, y_f + jnp.flip(y_b, 1), xs, z, d_skip, ssd_norm, w_out)
    return x + gate_x * out, ctx


def fourier_branch(h, w_in, w_out):
    u, z = jnp.split(h @ w_in, 2, axis=-1)
    b, L = u.shape[:2]
    ug = u.astype(jnp.float32).reshape(b, L, N_FOURIER_GROUPS, FOURIER_GROUP)
    f = jnp.fft.fft2(ug, axes=(1, 3), norm='ortho').real.reshape(b, L, FOURIER_WIDTH)
    return (f.astype(h.dtype) * jax.nn.silu(z)) @ w_out


def odd_layer(x, ctx, c, c_ctx, norm_w, ada_w, ada_b, w_in, w_out, update_ctx):
    hx, gate_x = modulate(x, norm_w, ada_mod(c, ada_w, ada_b))
    x = x + gate_x * fourier_branch(hx, w_in, w_out)
    if update_ctx:
        hc, gate_c = modulate(ctx, norm_w, ada_mod(c_ctx[None], ada_w, ada_b))
        ctx = ctx + gate_c * fourier_branch(hc, w_in, w_out)
    return x, ctx


def setup_inputs(seed: int = 0) -> dict:
    key = jax.random.key(seed)
    ks = jax.random.split(key, 24)
    f32 = jnp.float32

    def nrm(k, shape, fan_in):
        return jax.random.normal(k, shape, f32) * fan_in ** -0.5

    def gain(k, shape):
        return 1.0 + 0.1 * jax.random.normal(k, shape, f32)

    dt0 = jnp.exp(jax.random.uniform(ks[12], (N_EVEN, 2, N_SSD_HEADS), f32) * (math.log(0.1) - math.log(0.001)) + math.log(0.001))
    return {
        'x': jax.random.normal(ks[0], (BATCH, SEQ, D_MODEL), f32),
        'c': jax.random.normal(ks[1], (BATCH, D_MODEL), f32),
        'ctx': jax.random.normal(ks[2], (BATCH, CTX_LEN, D_MODEL), f32),
        'c_ctx': jax.random.normal(ks[3], (D_MODEL,), f32),
        'ev_norm_w': gain(ks[4], (N_EVEN, D_MODEL)),
        'ev_ada_w': nrm(ks[5], (N_EVEN, D_MODEL, 3 * D_MODEL), D_MODEL),
        'ev_ada_b': 0.02 * jax.random.normal(ks[6], (N_EVEN, 3 * D_MODEL), f32),
        'ev_w_in': nrm(ks[7], (N_EVEN, D_MODEL, EV_IN_COLS), D_MODEL),
        'ev_q_norm': gain(ks[8], (N_EVEN, HEAD_DIM)),
        'ev_k_norm': gain(ks[9], (N_EVEN, HEAD_DIM)),
        'ev_conv_w': nrm(ks[10], (N_EVEN, D_CONV, CONV_CH), D_CONV),
        'ev_conv_b': 0.02 * jax.random.normal(ks[11], (N_EVEN, CONV_CH), f32),
        'ev_a_log': jnp.log(jax.random.uniform(ks[13], (N_EVEN, 2, N_SSD_HEADS), f32, 1.0, 16.0)),
        'ev_dt_bias': dt0 + jnp.log(-jnp.expm1(-dt0)),
        'ev_d_skip': gain(ks[14], (N_EVEN, N_SSD_HEADS)),
        'ev_ssd_norm': gain(ks[15], (N_EVEN, SSD_WIDTH)),
        'ev_w_out': nrm(ks[16], (N_EVEN, MIX_WIDTH, D_MODEL), MIX_WIDTH),
        'od_norm_w': gain(ks[17], (N_ODD, D_MODEL)),
        'od_ada_w': nrm(ks[18], (N_ODD, D_MODEL, 3 * D_MODEL), D_MODEL),
        'od_ada_b': 0.02 * jax.random.normal(ks[19], (N_ODD, 3 * D_MODEL), f32),
        'od_w_in': nrm(ks[20], (N_ODD, D_MODEL, 2 * FOURIER_WIDTH), D_MODEL),
        'od_w_out': nrm(ks[21], (N_ODD, FOURIER_WIDTH, D_MODEL), FOURIER_WIDTH),
    }


def reference(x, c, ctx, c_ctx, ev_norm_w, ev_ada_w, ev_ada_b, ev_w_in, ev_q_norm, ev_k_norm,
              ev_conv_w, ev_conv_b, ev_a_log, ev_dt_bias, ev_d_skip, ev_ssd_norm, ev_w_out,
              od_norm_w, od_ada_w, od_ada_b, od_w_in, od_w_out):
    rows = x.shape[1] // GRID_W
    ang_row, ang_col = axial_rope_angles(rows)
    for i in range(DEPTH):
        j = i // 2
        ctx_read_later = any(l % 2 == 0 for l in range(i + 1, DEPTH))
        if i % 2 == 0:
            x, ctx = even_layer(x, ctx, c, c_ctx, ang_row, ang_col, ev_norm_w[j], ev_ada_w[j], ev_ada_b[j],
                                ev_w_in[j], ev_q_norm[j], ev_k_norm[j], ev_conv_w[j], ev_conv_b[j],
                                ev_a_log[j], ev_dt_bias[j], ev_d_skip[j], ev_ssd_norm[j], ev_w_out[j],
                                ctx_read_later)
        else:
            x, ctx = odd_layer(x, ctx, c, c_ctx, od_norm_w[j], od_ada_w[j], od_ada_b[j],
                               od_w_in[j], od_w_out[j], ctx_read_later)
    return x
```

```python
import heapq
import numpy as np
import concourse.bass as bass
import concourse.mybir as mybir
from concourse.bass_utils import run_bass_kernel_spmd

F32 = mybir.dt.float32
BF16 = mybir.dt.bfloat16
AF = mybir.ActivationFunctionType
ALU = mybir.AluOpType
AX = mybir.AxisListType

ENGS = ("pe", "act", "dve", "pool", "sp")
DEFAULT_COST = {"pe": 0.25, "act": 0.7, "dve": 0.7, "pool": 3.0, "sp": 0.05}


class Buf:
    __slots__ = ("name", "w", "r")

    def __init__(self, name=""):
        self.name = name
        self.w = None
        self.r = []


class Sched:
    def __init__(self, nc, n_dma_sems=48):
        self.nc = nc
        self.eng = {"pe": nc.tensor, "act": nc.scalar, "dve": nc.vector,
                    "pool": nc.gpsimd, "sp": nc.sync}
        self.ops = []
        self.bufs = set()
        self.n_dma_sems = n_dma_sems
        self.ndma = 0
        self.pos = {e: 0 for e in ENGS}
        self.seen = {e: {f: -1 for f in ENGS} for e in ENGS}
        self.seen_dma = {e: set() for e in ENGS}
        self.total = {e: 0 for e in ENGS}
        self.reorder = True
        self.nrec = 0

    def op(self, eng, fn, reads=(), writes=(), dma=False, cost=None, lat=None):
        if cost is None:
            cost = DEFAULT_COST[eng]
            if dma:
                cost = 1.0 if eng == "pool" else 0.1
        rec = dict(eng=eng, fn=fn, dma=dma, deps={}, cost=cost, lat=lat, i=self.nrec, sig=False)
        self.nrec += 1
        deps = rec["deps"]
        for b in reads:
            if b.w is not None:
                deps[b.w["i"]] = (b.w, True)
        for b in writes:
            if b.w is not None and b.w["i"] not in deps:
                deps[b.w["i"]] = (b.w, False)
            for r in b.r:
                if r["i"] not in deps:
                    deps[r["i"]] = (r, False)
        for b in writes:
            b.w = rec
            b.r = []
            self.bufs.add(b)
        for b in reads:
            b.r.append(rec)
            self.bufs.add(b)
        self.ops.append(rec)
        return rec

    def pe(self, fn, r=(), w=(), c=None):
        return self.op("pe", fn, r, w, cost=c)

    def act(self, fn, r=(), w=(), c=None):
        return self.op("act", fn, r, w, cost=c)

    def dve(self, fn, r=(), w=(), c=None):
        return self.op("dve", fn, r, w, cost=c)

    def pool(self, fn, r=(), w=(), c=None):
        return self.op("pool", fn, r, w, cost=c)

    def dma(self, q, out, in_, r=(), w=(), slow=False, lat=4.5):
        if slow:
            return self.op(q, lambda e: e.dma_start(out=out, in_=in_, allow_slow_non_contiguous=True), r, w, dma=True, lat=lat)
        return self.op(q, lambda e: e.dma_start(out=out, in_=in_), r, w, dma=True, lat=lat)

    def barrier(self):
        pass

    def begin(self, stack):
        nc = self.nc
        self.csem = {e: stack.enter_context(nc.semaphore("c_" + e)) for e in ENGS}
        self.dsem = [stack.enter_context(nc.semaphore("d_%d" % i)) for i in range(self.n_dma_sems)]
        self.cnt = {e: 0 for e in ENGS}
        self.snap = {}

    def _schedule(self, batch):
        if not self.reorder:
            return list(batch)
        n = len(batch)
        ids = {o["i"]: k for k, o in enumerate(batch)}
        nwait = [0] * n
        users = [[] for _ in range(n)]
        for k, o in enumerate(batch):
            for pi in o["deps"]:
                if pi in ids:
                    nwait[k] += 1
                    users[ids[pi]].append(k)
        ready_t = [0.0] * n
        fut = {e: [] for e in ENGS}
        av = {e: [] for e in ENGS}
        free = {e: 0.0 for e in ENGS}
        for k, o in enumerate(batch):
            if nwait[k] == 0:
                heapq.heappush(fut[o["eng"]], (0.0, k))
        order = []
        done = 0
        while done < n:
            best = None
            for e in ENGS:
                f = fut[e]
                a = av[e]
                while f and f[0][0] <= free[e]:
                    heapq.heappush(a, heapq.heappop(f)[1])
                if a:
                    cand = (free[e], a[0], e, True)
                elif f:
                    cand = (f[0][0], f[0][1], e, False)
                else:
                    continue
                if best is None or cand[:2] < best[:2]:
                    best = cand
            t, k, e, from_av = best
            if from_av:
                heapq.heappop(av[e])
            else:
                heapq.heappop(fut[e])
            o = batch[k]
            start = max(free[e], ready_t[k])
            free[e] = start + o["cost"]
            fin = free[e] if not o["dma"] else start + o["cost"] + (o["lat"] or 3.0)
            order.append(o)
            done += 1
            for u in users[k]:
                nwait[u] -= 1
                if fin > ready_t[u]:
                    ready_t[u] = fin
                if nwait[u] == 0:
                    heapq.heappush(fut[batch[u]["eng"]], (ready_t[u], u))
        self.sim_time = max(free.values())
        return order

    def flush(self):
        batch = self.ops
        self.ops = []
        N = self.n_dma_sems
        order = self._schedule(batch)
        for o in order:
            if o["dma"]:
                o["did"] = self.ndma
                self.ndma += 1
            else:
                o["pos"] = self.pos[o["eng"]]
                self.pos[o["eng"]] += 1
        last_real = {}
        for o in order:
            eng = o["eng"]
            seen = self.seen[eng]
            sd = self.seen_dma[eng]
            need_c = {}
            need_d = set()
            for (p, is_raw) in o["deps"].values():
                if p["dma"]:
                    if p["did"] not in sd:
                        need_d.add(p["did"])
                else:
                    f = p["eng"]
                    if f == eng and not o["dma"]:
                        if eng == "pe":
                            continue
                    if p["pos"] > seen[f] and p["pos"] > need_c.get(f, (-1, None))[0]:
                        need_c[f] = (p["pos"], p)
            if o["dma"] and o["did"] >= N and (o["did"] - N) not in sd:
                need_d.add(o["did"] - N)
            for f, (ps, p) in need_c.items():
                p["sig"] = True
                if ps > seen[f]:
                    seen[f] = ps
                for g, j in self.snap[(f, ps)].items():
                    if j > seen[g]:
                        seen[g] = j
            for d in need_d:
                sd.add(d)
            if len(sd) > 6 * N:
                lo = self.ndma - 3 * N
                self.seen_dma[eng] = set(x for x in sd if x >= lo)
            o["wc"] = [p for (_, p) in need_c.values()]
            o["wd"] = sorted(need_d)
            if not o["dma"]:
                self.snap[(eng, o["pos"])] = dict(seen)
                last_real[eng] = o
        for o in last_real.values():
            o["sig"] = True
        for o in order:
            if not o["dma"] and o["sig"]:
                self.cnt[o["eng"]] += 1
                o["sigval"] = self.cnt[o["eng"]]
        for o in order:
            e = self.eng[o["eng"]]
            for p in o["wc"]:
                e.wait_ge(self.csem[p["eng"]], p["sigval"])
            for d in o["wd"]:
                e.wait_ge(self.dsem[d % N], 16 * (d // N + 1))
            ins = o["fn"](e)
            if o["dma"]:
                ins.then_inc(self.dsem[o["did"] % N], 16)
            elif o["sig"]:
                ins.then_inc(self.csem[o["eng"]], 1)
            o["fn"] = None
            self.total[o["eng"]] += 1
        dlo = max(0, self.ndma - N)
        for en in ENGS:
            e = self.eng[en]
            for f, o in last_real.items():
                e.wait_ge(self.csem[f], o["sigval"])
                self.seen[en][f] = max(self.seen[en][f], o["pos"])
            for d in range(dlo, self.ndma):
                e.wait_ge(self.dsem[d % N], 16 * (d // N + 1))
        self.eng["pe"].nop()
        for en in ENGS:
            self.seen_dma[en] = set(range(dlo, self.ndma))
        for b in self.bufs:
            b.w = None
            b.r = []
        self.bufs = set()
        self.snap = {}
        self.stats = dict(cnt=dict(self.cnt), total=dict(self.total), ndma=self.ndma)

import contextlib
import math
import ml_dtypes

D = 2048
SEQ = 2048
CTX = 256
NTOK = SEQ + CTX
NT = NTOK // 128
EV_COLS = 10304
EPS = 1e-6

C_Q, C_G, C_Z, C_K, C_V, C_XBC, C_DT = 0, 2048, 4096, 6144, 6656, 7168, 10240


def host_consts():
    c = {}
    c["ident_f"] = np.eye(128, dtype=np.float32)
    c["ident_b"] = np.eye(128, dtype=np.float32).astype(ml_dtypes.bfloat16)
    c["ones_f"] = np.ones((128, 128), np.float32)
    c["ones_b"] = np.ones((128, 128), np.float32).astype(ml_dtypes.bfloat16)
    s = np.arange(128)
    tf = (s[:, None] <= s[None, :]).astype(np.float32)
    tb = (s[:, None] >= s[None, :]).astype(np.float32)
    wf = (s[:, None] > s[None, :]).astype(np.float32)
    wb = (s[:, None] < s[None, :]).astype(np.float32)
    c["tri"] = np.stack([tf, tb, wf, wb], axis=1).copy()
    pos = np.arange(SEQ)
    row = (pos // 64).astype(np.float32)
    col = (pos % 64).astype(np.float32)
    inv = (np.float32(10000.0) ** (-np.arange(0, 64, 2, dtype=np.float32) / np.float32(64))).astype(np.float32)
    ang_row = (row[:, None] * inv).astype(np.float32)
    ang_col = (col[:, None] * inv).astype(np.float32)
    cosT = np.zeros((128, SEQ), np.float32)
    sinT = np.zeros((128, SEQ), np.float32)
    RT = np.zeros((128, 128), np.float32)
    for p in range(128):
        ax, i = p // 64, p % 64
        f = i % 32
        ang = ang_row[:, f] if ax == 0 else ang_col[:, f]
        cosT[p] = np.cos(ang.astype(np.float32))
        sinT[p] = np.sin(ang.astype(np.float32))
        if i < 32:
            partner, sgn = p + 32, -1.0
        else:
            partner, sgn = p - 32, 1.0
        RT[partner, p] = sgn
    c["ropecs"] = np.stack([cosT, sinT], axis=1).copy()
    c["RT"] = RT.astype(ml_dtypes.bfloat16)
    n = np.arange(512)
    angc = 2.0 * np.pi * ((n[:, None] * n[None, :]) % 512) / 512.0
    cc = np.cos(angc) / 1024.0
    sc = np.sin(angc) / 1024.0
    dc = np.stack([cc.reshape(4, 128, 512), sc.reshape(4, 128, 512)], axis=0)
    c["dftc"] = np.ascontiguousarray(dc.transpose(2, 0, 1, 3)).astype(ml_dtypes.bfloat16)
    kk = np.arange(1024)
    tabs = []
    for par in range(2):
        tt = 2 * np.arange(1024) + par
        ang = 2.0 * np.pi * ((tt[:, None] * kk[None, :]) % 2048) / 2048.0
        tabs.append(np.stack([np.cos(ang).reshape(8, 128, 1024), (-np.sin(ang)).reshape(8, 128, 1024)], axis=0))
    dl = np.stack(tabs, axis=0)
    c["dftl"] = np.ascontiguousarray(dl.transpose(3, 0, 1, 2, 4)).astype(ml_dtypes.bfloat16)
    return c


CONST_SHAPES = {
    "ident_f": ([128, 128], "f"), "ident_b": ([128, 128], "b"), "ones_f": ([128, 128], "f"),
    "ones_b": ([128, 128], "b"), "tri": ([128, 4, 128], "f"), "ropecs": ([128, 2, 2048], "f"),
    "RT": ([128, 128], "b"), "dftc": ([128, 2, 4, 512], "b"), "dftl": ([128, 2, 2, 8, 1024], "b"),
}

IN_SHAPES = {
    "x": [SEQ, D], "c": [D], "ctx": [CTX, D], "c_ctx": [D],
    "ev_norm_w": [D], "ev_ada_w": [D, 3 * D], "ev_ada_b": [3 * D], "ev_w_in": [D, EV_COLS],
    "ev_q_norm": [128], "ev_k_norm": [128], "ev_conv_w": [5, 3072], "ev_conv_b": [3072],
    "ev_a_log": [64], "ev_dt_bias": [64], "ev_d_skip": [32], "ev_ssd_norm": [2048],
    "ev_w_out": [4096, D], "od_norm_w": [D], "od_ada_w": [D, 3 * D], "od_ada_b": [3 * D],
    "od_w_in": [D, 8192], "od_w_out": [4096, D],
}


class WB:
    def __init__(self, npieces=8):
        self.b = [Buf() for _ in range(npieces)]

    def p(self, kc):
        return self.b[kc // 4]


class KB:
    def __init__(self, debug=(), stop_after=None):
        self.nc = bass.Bass("TRN2", target_bir_lowering=False)
        self.S = Sched(self.nc, n_dma_sems=48)
        self.debug = set(debug)
        self.stop_after = stop_after
        self.es = contextlib.ExitStack()
        self.I = {}
        for k, shp in IN_SHAPES.items():
            self.I[k] = self.nc.dram_tensor(k, shp, F32, kind="ExternalInput").ap()
        for k, (shp, dt) in CONST_SHAPES.items():
            self.I[k] = self.nc.dram_tensor(k, shp, F32 if dt == "f" else BF16, kind="ExternalInput").ap()
        self.out = self.nc.dram_tensor("y", [SEQ, D], F32, kind="ExternalOutput").ap()
        self.scr = {}
        self.dq = 0

    def dram(self, name, shape, dt):
        kind = "ExternalOutput" if name in self.debug else "Internal"
        t = self.nc.dram_tensor(name, shape, dt, kind=kind).ap()
        self.scr[name] = (t, Buf(name))
        return t

    def sb(self, st, name, shape, dt):
        self.uid = getattr(self, "uid", 0) + 1
        return st.enter_context(self.nc.sbuf_tensor("s%d_%s" % (self.uid, name), shape, dt))

    def ps(self, st, name, shape=(128, 512), dt=F32):
        self.uid = getattr(self, "uid", 0) + 1
        return st.enter_context(self.nc.psum_tensor("p%d_%s" % (self.uid, name), list(shape), dt))

    def q(self):
        self.dq ^= 1
        return "sp" if self.dq else "act"


def build(debug=(), stop_after=None):
    kb = KB(debug, stop_after)
    nc, S, I = kb.nc, kb.S, kb.I
    MUL, ADD, SUB = ALU.mult, ALU.add, ALU.subtract
    phases = ["ada", "norm0", "qk", "gv", "xbc", "ssd", "att", "out0", "norm1", "proj1", "dft1", "dft2", "out1"]

    def want(ph):
        return stop_after is None or phases.index(ph) <= phases.index(stop_after)

    with contextlib.ExitStack() as top:
        S.begin(top)
        ident_f = kb.sb(top, "ident_f", [128, 128], F32)
        ident_b = kb.sb(top, "ident_b", [128, 128], BF16)
        ones_f = kb.sb(top, "ones_f", [128, 128], F32)
        ones_b = kb.sb(top, "ones_b", [128, 128], BF16)
        gate_bc = kb.sb(top, "gate_bc", [128, 2, D], F32)
        modT = kb.sb(top, "modT", [128, 2, 4, 16], F32)
        b_const = Buf("const")
        b_gate = Buf("gate")
        b_modT = Buf("modT")
        b_hT = [Buf("hT%d" % i) for i in range(5)]
        for nm, t in (("ident_f", ident_f), ("ident_b", ident_b), ("ones_f", ones_f), ("ones_b", ones_b)):
            S.dma("sp", t[:], I[nm], w=[b_const])

        qT = kb.dram("qT", [16, 128, SEQ], BF16)
        kT = kb.dram("kT", [4, 128, NTOK], BF16)
        sgT = kb.dram("sgT", [16, 128, SEQ], BF16)
        vtm = kb.dram("vtm", [NTOK, 512], BF16)
        sz = kb.dram("sz", [SEQ, 2048], BF16)
        xs_tm = kb.dram("xs_tm", [NTOK, 2048], BF16)
        b_tm = kb.dram("b_tm", [NTOK, 512], BF16)
        BT = kb.dram("BT", [4, 128, NTOK], BF16)
        CT = kb.dram("CT", [4, 128, NTOK], BF16)
        dtp_d = kb.dram("dtp_d", [NTOK, 64], F32)
        dA_d = kb.dram("dA_d", [NTOK, 64], F32)
        yf_d = kb.dram("yf_d", [SEQ, 2048], F32)
        mixT = kb.dram("mixT", [16, 128, 32, 128], BF16)
        x1 = kb.dram("x1", [SEQ, D], F32)
        uT = kb.dram("uT", [32, 128, SEQ], BF16)
        szT = kb.dram("szT", [32, 128, SEQ], BF16)
        UCd = kb.dram("UCd", [2, 8, 128, 16, 512], BF16)
        mixT1 = kb.dram("mixT1", [16, 128, 32, 128], BF16)
        bd = {k: v[1] for k, v in kb.scr.items()}

        def load_w(dst, dbuf, wap, c0, n, nk=16):
            src = wap[:, c0:c0 + n].rearrange("(kc p) n -> p kc n", p=128)
            for k0 in range(0, nk, 4):
                S.dma("pool", dst[:, k0:k0 + 4, 0:n], src[:, k0:k0 + 4, :], w=[dbuf.p(k0)])

        ADA_W = (("ev_ada_w", "ev_ada_b"), ("od_ada_w", "od_ada_b"))

        def ada_setup(st, psum=None):
            A = {}
            A["cT"] = kb.sb(st, "cT", [128, 16, 64], F32)
            A["cTb"] = kb.sb(st, "cTb", [128, 16, 64], BF16)
            A["wblk"] = [kb.sb(st, "adaw%d" % i, [128, 16, 512], BF16) for i in range(2)]
            A["nwT"] = kb.sb(st, "nwT", [128, 2, 16], F32)
            A["biasF"] = kb.sb(st, "biasF", [128, 2, 48], F32)
            A["modF"] = kb.sb(st, "modF", [128, 2, 48, 2], F32)
            A["dg"] = [kb.sb(st, "dg%d" % i, [128, 128], F32) for i in range(2)]
            if psum is None:
                A["ps"] = [kb.ps(st, "psA%d" % i)[:] for i in range(2)]
                A["b_ps"] = [Buf(), Buf()]
            else:
                A["ps"] = [psum[0]]
                A["b_ps"] = [psum[1]]
            A["b_cT"], A["b_nw"], A["b_bias"], A["b_modF"] = Buf(), Buf(), Buf(), Buf()
            A["b_w"] = [WB(), WB()]
            A["b_dg"] = [Buf(), Buf()]
            A["wi"] = 0
            A["pi"] = 0
            cT, cTb = A["cT"], A["cTb"]
            S.dve(lambda e: e.memset(cT[:], 0.0), w=[A["b_cT"]])
            S.dma("sp", cT[:, :, 0:1], I["c"].rearrange("(c p one) -> p c one", p=128, one=1), w=[A["b_cT"]], slow=True)
            S.dma("sp", cT[:, :, 32:33], I["c_ctx"].rearrange("(c p one) -> p c one", p=128, one=1), w=[A["b_cT"]], slow=True)
            S.act(lambda e: e.activation(out=cTb[:], in_=cT[:], func=AF.Silu), r=[A["b_cT"]], w=[A["b_cT"]])
            S.dma("sp", A["nwT"][:, 0, :].unsqueeze(2), I["ev_norm_w"].rearrange("(c p one) -> p c one", p=128, one=1), w=[A["b_nw"]], slow=True)
            S.dma("sp", A["nwT"][:, 1, :].unsqueeze(2), I["od_norm_w"].rearrange("(c p one) -> p c one", p=128, one=1), w=[A["b_nw"]], slow=True)
            for layer in range(2):
                S.dma("act", A["biasF"][:, layer, :].unsqueeze(2), I[ADA_W[layer][1]].rearrange("(c p one) -> p c one", p=128, one=1),
                      w=[A["b_bias"]], slow=True)
            return A

        def ada_chunks(A, layer, c8s):
            wn = ADA_W[layer][0]
            cTb, modF, biasF = A["cTb"], A["modF"], A["biasF"]
            for c8 in c8s:
                pi = A["pi"]
                A["pi"] += 1
                pA, bA = A["ps"][pi % len(A["ps"])], A["b_ps"][pi % len(A["ps"])]
                for half in range(2):
                    wi = A["wi"]
                    A["wi"] += 1
                    w_t, w_b = A["wblk"][wi % 2], A["b_w"][wi % 2]
                    load_w(w_t, w_b, I[wn], (c8 * 2 + half) * 512, 512)
                    for q in range(4):
                        col = (half * 4 + q) * 64
                        for kc in range(16):
                            S.pe(lambda e, kc=kc, w_t=w_t, pA=pA, q=q, col=col: e.matmul(
                                pA[:, col:col + 64], lhsT=w_t[:, kc, q * 128:(q + 1) * 128], rhs=cTb[:, kc, :],
                                start=(kc == 0), stop=(kc == 15)), r=[A["b_cT"], w_b.p(kc)], w=[bA], c=0.1)
                pv = pA.rearrange("p (a c) -> p a c", c=64)
                S.dve(lambda e, pv=pv, c8=c8: e.tensor_tensor(
                    out=modF[:, layer, c8 * 8:(c8 + 1) * 8, :], in0=pv[:, :, 0:33:32],
                    in1=biasF[:, layer, c8 * 8:(c8 + 1) * 8].unsqueeze(2).broadcast_to([128, 8, 2]), op=ADD),
                      r=[bA, A["b_bias"]], w=[A["b_modF"]])

        def ada_mod(A, layer):
            modF, nwT = A["modF"], A["nwT"]
            for ri in range(2):
                S.dve(lambda e, ri=ri: e.tensor_copy(out=modT[:, layer, 2 * ri, :], in_=modF[:, layer, 0:16, ri]),
                      r=[A["b_modF"]], w=[b_modT])
                S.dve(lambda e, ri=ri: e.scalar_tensor_tensor(
                    out=modT[:, layer, 2 * ri + 1, :], in0=modF[:, layer, 16:32, ri], scalar=1.0, in1=nwT[:, layer, :],
                    op0=ADD, op1=MUL), r=[A["b_modF"], A["b_nw"]], w=[b_modT])

        def ada_gate(A, layer):
            modF = A["modF"]
            for gq in range(4):
                pi = A["pi"]
                A["pi"] += 1
                pG, bG = A["ps"][pi % len(A["ps"])], A["b_ps"][pi % len(A["ps"])]
                for q in range(4):
                    ch = gq * 4 + q
                    d2 = ch % 2
                    S.dve(lambda e, ch=ch, d2=d2: e.tensor_scalar(out=A["dg"][d2][:], in0=ident_f[:], scalar1=modF[:, layer, 32 + ch, 0:1],
                                                                  scalar2=None, op0=MUL),
                          r=[A["b_modF"], b_const], w=[A["b_dg"][d2]], c=0.25)
                    S.pe(lambda e, q=q, d2=d2, pG=pG: e.matmul(pG[:, q * 128:(q + 1) * 128], lhsT=ones_f[:], rhs=A["dg"][d2][:],
                                                              start=True, stop=True), r=[A["b_dg"][d2], b_const], w=[bG], c=0.3)
                S.dve(lambda e, gq=gq, pG=pG: e.tensor_copy(out=gate_bc[:, layer, gq * 512:(gq + 1) * 512], in_=pG),
                      r=[bG], w=[b_gate])

        if want("ada"):
            with contextlib.ExitStack() as st:
                A = ada_setup(st)
                ada_chunks(A, 0, range(0, 4))
                ada_mod(A, 0)
                S.flush()

        def norm_phase(layer, sources):
            with contextlib.ExitStack() as st:
                xt = [kb.sb(st, "nx%d" % i, [128, 4, D], F32) for i in range(2)]
                junk = kb.sb(st, "njunk", [128, D], BF16)
                ssq = kb.sb(st, "nssq", [128, 16], F32)
                pst = [kb.ps(st, "npst%d" % i) for i in range(4)]
                b_xt = [Buf(), Buf()]
                b_junk, b_ssq = Buf(), Buf()
                b_pst = [Buf() for _ in range(4)]
                gi = 0
                pi = 0
                for (src, rb, tok0, ntile, hb, mb) in sources:
                    x_t, x_b = xt[gi % 2], b_xt[gi % 2]
                    gi += 1
                    for j in range(ntile):
                        S.dma(kb.q(), x_t[:, j, :], src[j * 128:(j + 1) * 128, :], r=rb, w=[x_b])
                    for j in range(ntile):
                        S.act(lambda e, j=j, x_t=x_t: e.activation(out=junk[:], in_=x_t[:, j, :], func=AF.Square,
                                                                    accum_out=ssq[:, j:j + 1]),
                              r=[x_b], w=[b_junk, b_ssq], c=2.0)
                    S.act(lambda e, ntile=ntile: e.activation(out=ssq[:, 8:8 + ntile], in_=ssq[:, 0:ntile], func=AF.Sqrt,
                                                              scale=1.0 / D, bias=eps_t[:, 0:1]),
                          r=[b_ssq, b_const], w=[b_ssq])
                    S.dve(lambda e, ntile=ntile: e.reciprocal(out=ssq[:, 4:4 + ntile], in_=ssq[:, 8:8 + ntile]),
                          r=[b_ssq], w=[b_ssq])
                    for j in range(ntile):
                        S.dve(lambda e, j=j, x_t=x_t: e.tensor_scalar(out=x_t[:, j, :], in0=x_t[:, j, :],
                                                                       scalar1=ssq[:, 4 + j:5 + j], scalar2=None, op0=MUL),
                              r=[x_b, b_ssq], w=[x_b], c=2.3)
                    for kc in range(16):
                        p_t, p_b = pst[pi % 4], b_pst[pi % 4]
                        pi += 1
                        for j in range(ntile):
                            S.pe(lambda e, j=j, kc=kc, x_t=x_t, p_t=p_t: e.transpose(
                                p_t[:, j * 128:(j + 1) * 128], x_t[:, j, kc * 128:(kc + 1) * 128], ident_f[:]),
                                 r=[x_b, b_const], w=[p_b], c=0.2)
                        eng = S.dve if kc % 2 == 0 else S.act
                        if kc % 2 == 0:
                            S.dve(lambda e, kc=kc, p_t=p_t, ntile=ntile, tok0=tok0, mb=mb: e.tensor_scalar(
                                out=hT[:, kc, tok0:tok0 + ntile * 128], in0=p_t[:, 0:ntile * 128],
                                scalar1=modT[:, layer, mb + 1, kc:kc + 1], scalar2=modT[:, layer, mb, kc:kc + 1],
                                op0=MUL, op1=ADD), r=[p_b, b_modT], w=[hb])
                        else:
                            S.act(lambda e, kc=kc, p_t=p_t, ntile=ntile, tok0=tok0, mb=mb: e.activation(
                                out=hT[:, kc, tok0:tok0 + ntile * 128], in_=p_t[:, 0:ntile * 128], func=AF.Identity,
                                scale=modT[:, layer, mb + 1, kc:kc + 1], bias=modT[:, layer, mb, kc:kc + 1]),
                                  r=[p_b, b_modT], w=[hb])
                S.flush()

        eps_t = kb.sb(top, "eps_t", [128, 2], F32)
        S.dve(lambda e: e.memset(eps_t[:, 0:1], EPS), w=[b_const])
        S.dve(lambda e: e.memset(eps_t[:, 1:2], 1.0), w=[b_const])
        hctx = contextlib.ExitStack()
        hT = kb.sb(hctx, "hT", [128, 16, NTOK], BF16)

        if want("norm0"):
            srcs = [(I["ctx"], [], 0, 2, b_hT[0], 2)]
            for tb in range(4):
                srcs.append((I["x"][tb * 512:(tb + 1) * 512, :], [], CTX + tb * 512, 4, b_hT[1 + tb], 0))
            norm_phase(0, srcs)
            if "hT_d" in kb.debug:
                hT_d = kb.dram("hT_d", [128, 16, NTOK], BF16)
                S.dma("sp", hT_d, hT[:], r=b_hT, w=[kb.scr["hT_d"][1]])
                S.flush()
            S.flush()

        tblocks = [(0, CTX, b_hT[0], None)] + [(CTX + i * 512, 512, b_hT[1 + i], i * 512) for i in range(4)]

        if want("qk"):
            with contextlib.ExitStack() as st:
                wq = [kb.sb(st, "wq%d" % i, [128, 16, 512], BF16) for i in range(2)]
                b_wq = [WB(), WB()]
                ropecs = kb.sb(st, "ropecs", [128, 2, SEQ], F32)
                RTt = kb.sb(st, "RTt", [128, 128], BF16)
                qkn = kb.sb(st, "qkn", [128, 2], F32)
                b_rc = Buf()
                S.dma("sp", ropecs[:], I["ropecs"], w=[b_rc])
                S.dma("sp", RTt[:], I["RT"], w=[b_rc])
                S.dma("sp", qkn[:, 0:1], I["ev_q_norm"].rearrange("(p one) -> p one", one=1), w=[b_rc])
                S.dma("sp", qkn[:, 1:2], I["ev_k_norm"].rearrange("(p one) -> p one", one=1), w=[b_rc])
                NB = 2
                sq = [kb.sb(st, "sq%d" % i, [128, 512], BF16) for i in range(NB)]
                rs = [kb.sb(st, "rs%d" % i, [128, 512], F32) for i in range(NB)]
                qn = [kb.sb(st, "qn%d" % i, [128, 512], BF16) for i in range(NB)]
                t1 = [kb.sb(st, "t1%d" % i, [128, 512], F32) for i in range(NB)]
                t2 = [kb.sb(st, "t2%d" % i, [128, 512], F32) for i in range(NB)]
                qo = [kb.sb(st, "qo%d" % i, [128, 512], BF16) for i in range(NB)]
                b_sq, b_rs, b_qn, b_t1, b_t2, b_qo = [[Buf() for _ in range(NB)] for _ in range(6)]
                pacc = [kb.ps(st, "qacc%d" % i) for i in range(2)]
                pss = [kb.ps(st, "qss%d" % i) for i in range(2)]
                prot = [kb.ps(st, "qrot%d" % i) for i in range(2)]
                b_pacc, b_pss, b_prot = [[Buf(), Buf()] for _ in range(3)]
                it = 0
                for hb in range(20):
                    isq = hb < 16
                    c0 = C_Q + hb * 128 if isq else C_K + (hb - 16) * 128
                    w_t, w_b = wq[(hb // 4) % 2], b_wq[(hb // 4) % 2]
                    wo = (hb % 4) * 128
                    if hb % 4 == 0:
                        load_w(w_t, w_b, I["ev_w_in"], c0, 512)
                    for (tok0, ntok, hbuf, lat) in tblocks:
                        if lat is None and isq:
                            continue
                        i2 = it % 2
                        it += 1
                        for kc in range(16):
                            S.pe(lambda e, kc=kc, w_t=w_t, i2=i2, tok0=tok0, ntok=ntok, wo=wo: e.matmul(
                                pacc[i2][:, 0:ntok], lhsT=w_t[:, kc, wo:wo + 128], rhs=hT[:, kc, tok0:tok0 + ntok],
                                start=(kc == 0), stop=(kc == 15)), r=[w_b.p(kc), hbuf], w=[b_pacc[i2]])
                        S.act(lambda e, i2=i2, ntok=ntok: e.activation(out=sq[i2][:, 0:ntok], in_=pacc[i2][:, 0:ntok], func=AF.Square),
                              r=[b_pacc[i2]], w=[b_sq[i2]])
                        S.pe(lambda e, i2=i2, ntok=ntok: e.matmul(pss[i2][:, 0:ntok], lhsT=ones_b[:], rhs=sq[i2][:, 0:ntok],
                                                                   start=True, stop=True), r=[b_sq[i2], b_const], w=[b_pss[i2]], c=0.25)
                        S.act(lambda e, i2=i2, ntok=ntok: e.activation(out=rs[i2][:, 0:ntok], in_=pss[i2][:, 0:ntok], func=AF.Ln,
                                                                        scale=1.0 / 128, bias=eps_t[:, 0:1]),
                              r=[b_pss[i2], b_const], w=[b_rs[i2]], c=0.6)
                        S.act(lambda e, i2=i2, ntok=ntok: e.activation(out=rs[i2][:, 0:ntok], in_=rs[i2][:, 0:ntok], func=AF.Exp, scale=-0.5),
                              r=[b_rs[i2]], w=[b_rs[i2]], c=0.6)
                        nsel = 0 if isq else 1
                        S.dve(lambda e, i2=i2, ntok=ntok, nsel=nsel: e.scalar_tensor_tensor(
                            out=qn[i2][:, 0:ntok], in0=pacc[i2][:, 0:ntok], scalar=qkn[:, nsel:nsel + 1], in1=rs[i2][:, 0:ntok],
                            op0=MUL, op1=MUL), r=[b_pacc[i2], b_rs[i2], b_rc], w=[b_qn[i2]])
                        if lat is None:
                            g = hb - 16
                            S.dma(kb.q(), kT[g, :, 0:CTX], qn[i2][:, 0:CTX], r=[b_qn[i2]], w=[bd["kT"]])
                            continue
                        S.pe(lambda e, i2=i2: e.matmul(prot[i2][:], lhsT=RTt[:], rhs=qn[i2][:], start=True, stop=True),
                             r=[b_qn[i2], b_rc], w=[b_prot[i2]])
                        S.pool(lambda e, i2=i2, lat=lat: e.tensor_tensor(out=t1[i2][:], in0=qn[i2][:], in1=ropecs[:, 0, lat:lat + 512], op=MUL),
                               r=[b_qn[i2], b_rc], w=[b_t1[i2]], c=2.0)
                        S.dve(lambda e, i2=i2, lat=lat: e.tensor_tensor(out=t2[i2][:], in0=prot[i2][:], in1=ropecs[:, 1, lat:lat + 512], op=MUL),
                              r=[b_prot[i2], b_rc], w=[b_t2[i2]])
                        S.pool(lambda e, i2=i2: e.tensor_tensor(out=qo[i2][:], in0=t1[i2][:], in1=t2[i2][:], op=ADD),
                               r=[b_t1[i2], b_t2[i2]], w=[b_qo[i2]], c=2.0)
                        if isq:
                            S.dma(kb.q(), qT[hb, :, lat:lat + 512], qo[i2][:], r=[b_qo[i2]], w=[bd["qT"]])
                        else:
                            S.dma(kb.q(), kT[hb - 16, :, CTX + lat:CTX + lat + 512], qo[i2][:], r=[b_qo[i2]], w=[bd["kT"]])
                S.flush()

        if want("gv"):
            with contextlib.ExitStack() as st:
                wg = [kb.sb(st, "wg%d" % i, [128, 16, 512], BF16) for i in range(2)]
                b_wg = [WB(), WB()]
                og = [kb.sb(st, "og%d" % i, [128, 512], BF16) for i in range(3)]
                b_og = [Buf() for _ in range(3)]
                pacc = [kb.ps(st, "gacc%d" % i) for i in range(3)]
                b_pacc = [Buf() for _ in range(3)]
                dtb = kb.sb(st, "dtb", [128, 3, 64], F32)
                dto = kb.sb(st, "dto", [128, 2, 2, 64], F32)
                b_dtb = Buf()
                b_dto = [Buf(), Buf()]
                S.dma("sp", dtb[:, 0, :], I["ev_dt_bias"].partition_broadcast(128), w=[b_dtb])
                S.dma("sp", dtb[:, 1, :], I["ev_a_log"].partition_broadcast(128), w=[b_dtb])
                S.act(lambda e: e.activation(out=dtb[:, 1, :], in_=dtb[:, 1, :], func=AF.Exp), r=[b_dtb], w=[b_dtb])
                S.dve(lambda e: e.tensor_scalar(out=dtb[:, 1, :], in0=dtb[:, 1, :], scalar1=-1.0, scalar2=None, op0=MUL),
                      r=[b_dtb], w=[b_dtb])
                wi = 0
                it = 0
                for gb in range(4):
                    w_t, w_b = wg[wi % 2], b_wg[wi % 2]
                    wi += 1
                    load_w(w_t, w_b, I["ev_w_in"], C_G + gb * 512, 512)
                    for hh in range(4):
                        h = gb * 4 + hh
                        for (tok0, ntok, hbuf, lat) in tblocks[1:]:
                            i3 = it % 3
                            it += 1
                            for kc in range(16):
                                S.pe(lambda e, kc=kc, w_t=w_t, i3=i3, tok0=tok0, hh=hh: e.matmul(
                                    pacc[i3][:], lhsT=w_t[:, kc, hh * 128:(hh + 1) * 128], rhs=hT[:, kc, tok0:tok0 + 512],
                                    start=(kc == 0), stop=(kc == 15)), r=[w_b.p(kc), hbuf], w=[b_pacc[i3]])
                            S.act(lambda e, i3=i3: e.activation(out=og[i3][:], in_=pacc[i3][:], func=AF.Silu),
                                  r=[b_pacc[i3]], w=[b_og[i3]])
                            S.dma("sp", sgT[h, :, lat:lat + 512], og[i3][:], r=[b_og[i3]], w=[bd["sgT"]])
                for zb in range(4):
                    w_t, w_b = wg[wi % 2], b_wg[wi % 2]
                    wi += 1
                    load_w(w_t, w_b, I["ev_w_in"], C_Z + zb * 512, 512)
                    for tt in range(2, NT):
                        i3 = it % 3
                        it += 1
                        hbuf = b_hT[1 + (tt - 2) // 4]
                        for kc in range(16):
                            S.pe(lambda e, kc=kc, w_t=w_t, i3=i3, tt=tt: e.matmul(
                                pacc[i3][:], lhsT=hT[:, kc, tt * 128:(tt + 1) * 128], rhs=w_t[:, kc, :],
                                start=(kc == 0), stop=(kc == 15)), r=[w_b.p(kc), hbuf], w=[b_pacc[i3]])
                        S.act(lambda e, i3=i3: e.activation(out=og[i3][:], in_=pacc[i3][:], func=AF.Silu),
                              r=[b_pacc[i3]], w=[b_og[i3]])
                        S.dma("sp", sz[(tt - 2) * 128:(tt - 1) * 128, zb * 512:(zb + 1) * 512], og[i3][:], r=[b_og[i3]], w=[bd["sz"]])
                w_t, w_b = wg[wi % 2], b_wg[wi % 2]
                wi += 1
                load_w(w_t, w_b, I["ev_w_in"], C_V, 512)
                for tt in range(NT):
                    i3 = it % 3
                    it += 1
                    hbuf = b_hT[0] if tt < 2 else b_hT[1 + (tt - 2) // 4]
                    for kc in range(16):
                        S.pe(lambda e, kc=kc, w_t=w_t, i3=i3, tt=tt: e.matmul(
                            pacc[i3][:], lhsT=hT[:, kc, tt * 128:(tt + 1) * 128], rhs=w_t[:, kc, :],
                            start=(kc == 0), stop=(kc == 15)), r=[w_b.p(kc), hbuf], w=[b_pacc[i3]])
                    S.dve(lambda e, i3=i3: e.tensor_copy(out=og[i3][:], in_=pacc[i3][:]), r=[b_pacc[i3]], w=[b_og[i3]])
                    S.dma("sp", vtm[tt * 128:(tt + 1) * 128, :], og[i3][:], r=[b_og[i3]], w=[bd["vtm"]])
                w_t, w_b = wg[wi % 2], b_wg[wi % 2]
                wi += 1
                load_w(w_t, w_b, I["ev_w_in"], C_DT, 64)
                for tt in range(NT):
                    i3 = it % 3
                    it += 1
                    i2 = tt % 2
                    hbuf = b_hT[0] if tt < 2 else b_hT[1 + (tt - 2) // 4]
                    for kc in range(16):
                        S.pe(lambda e, kc=kc, w_t=w_t, i3=i3, tt=tt: e.matmul(
                            pacc[i3][:, 0:64], lhsT=hT[:, kc, tt * 128:(tt + 1) * 128], rhs=w_t[:, kc, 0:64],
                            start=(kc == 0), stop=(kc == 15)), r=[w_b.p(kc), hbuf], w=[b_pacc[i3]])
                    S.dve(lambda e, i3=i3, i2=i2: e.tensor_tensor(out=dto[:, i2, 0, :], in0=pacc[i3][:, 0:64], in1=dtb[:, 0, :], op=ADD),
                          r=[b_pacc[i3], b_dtb], w=[b_dto[i2]])
                    S.act(lambda e, i2=i2: e.activation(out=dto[:, i2, 0, :], in_=dto[:, i2, 0, :], func=AF.Exp), r=[b_dto[i2]], w=[b_dto[i2]])
                    S.act(lambda e, i2=i2: e.activation(out=dto[:, i2, 0, :], in_=dto[:, i2, 0, :], func=AF.Ln, bias=eps_t[:, 1:2]),
                          r=[b_dto[i2], b_const], w=[b_dto[i2]])
                    S.dve(lambda e, i2=i2: e.tensor_tensor(out=dto[:, i2, 1, :], in0=dto[:, i2, 0, :], in1=dtb[:, 1, :], op=MUL),
                          r=[b_dto[i2], b_dtb], w=[b_dto[i2]])
                    S.dma("sp", dtp_d[tt * 128:(tt + 1) * 128, :], dto[:, i2, 0, :], r=[b_dto[i2]], w=[bd["dtp_d"]])
                    S.dma("sp", dA_d[tt * 128:(tt + 1) * 128, :], dto[:, i2, 1, :], r=[b_dto[i2]], w=[bd["dA_d"]])
                S.flush()

        if want("xbc"):
            with contextlib.ExitStack() as st:
                wx = [kb.sb(st, "wx%d" % i, [128, 16, 512], BF16) for i in range(2)]
                b_wx = [WB(), WB()]
                cw = kb.sb(st, "cw", [128, 24, 8], F32)
                b_cw = Buf()
                for j in range(5):
                    S.dma("sp", cw[:, :, j:j + 1], I["ev_conv_w"][j, :].rearrange("(b p one) -> p b one", p=128, one=1),
                          w=[b_cw], slow=True)
                S.dma("sp", cw[:, :, 5:6], I["ev_conv_b"].rearrange("(b p one) -> p b one", p=128, one=1), w=[b_cw], slow=True)
                xpad = [kb.sb(st, "xpad%d" % i, [128, SEQ + 4], F32) for i in range(2)]
                cpad = [kb.sb(st, "cpad%d" % i, [128, CTX + 4], F32) for i in range(2)]
                acc = [kb.sb(st, "cacc%d" % i, [128, SEQ], F32) for i in range(2)]
                cacc = [kb.sb(st, "ccacc%d" % i, [128, CTX], F32) for i in range(2)]
                cvo = [kb.sb(st, "cvo%d" % i, [128, NTOK], BF16) for i in range(2)]
                tro = [kb.sb(st, "tro%d" % i, [128, 8, 128], BF16) for i in range(2)]
                b_xpad, b_cpad, b_acc, b_cacc, b_cvo, b_tro = [[Buf(), Buf()] for _ in range(6)]
                pacc = [kb.ps(st, "xacc%d" % i) for i in range(3)]
                ptr = [kb.ps(st, "xtr%d" % i, (128, 1024), BF16) for i in range(2)]
                b_pacc = [Buf() for _ in range(3)]
                b_ptr = [Buf(), Buf()]
                for i in range(2):
                    S.pool(lambda e, i=i: e.memset(xpad[i][:], 0.0), w=[b_xpad[i]])
                    S.pool(lambda e, i=i: e.memset(cpad[i][:], 0.0), w=[b_cpad[i]])
                it = 0
                ti = 0
                for cb in range(24):
                    w_t, w_b = wx[(cb // 4) % 2], b_wx[(cb // 4) % 2]
                    wo = (cb % 4) * 128
                    if cb % 4 == 0:
                        load_w(w_t, w_b, I["ev_w_in"], C_XBC + cb * 128, 512)
                    i2 = cb % 2
                    for (tok0, ntok, hbuf, lat) in tblocks:
                        i3 = it % 3
                        it += 1
                        for kc in range(16):
                            S.pe(lambda e, kc=kc, w_t=w_t, i3=i3, tok0=tok0, ntok=ntok, wo=wo: e.matmul(
                                pacc[i3][:, 0:ntok], lhsT=w_t[:, kc, wo:wo + 128], rhs=hT[:, kc, tok0:tok0 + ntok],
                                start=(kc == 0), stop=(kc == 15)), r=[w_b.p(kc), hbuf], w=[b_pacc[i3]])
                        if lat is None:
                            S.act(lambda e, i3=i3, i2=i2: e.activation(out=cpad[i2][:, 2:2 + CTX], in_=pacc[i3][:, 0:CTX], func=AF.Copy),
                                  r=[b_pacc[i3]], w=[b_cpad[i2]])
                        else:
                            S.act(lambda e, i3=i3, i2=i2, lat=lat: e.activation(out=xpad[i2][:, 2 + lat:2 + lat + 512], in_=pacc[i3][:],
                                                                                   func=AF.Copy), r=[b_pacc[i3]], w=[b_xpad[i2]])
                    for (pad, bp, ac, ba, n, o0) in ((xpad[i2], b_xpad[i2], acc[i2], b_acc[i2], SEQ, CTX),
                                                     (cpad[i2], b_cpad[i2], cacc[i2], b_cacc[i2], CTX, 0)):
                        S.act(lambda e, pad=pad, ac=ac, n=n, cb=cb: e.activation(out=ac[:, 0:n], in_=pad[:, 0:n], func=AF.Copy,
                                                                                 scale=cw[:, cb, 0:1]), r=[bp, b_cw], w=[ba], c=0.3 + n * 0.0009)
                        for j in range(1, 5):
                            S.dve(lambda e, pad=pad, ac=ac, n=n, cb=cb, j=j: e.scalar_tensor_tensor(
                                out=ac[:, 0:n], in0=pad[:, j:j + n], scalar=cw[:, cb, j:j + 1], in1=ac[:, 0:n], op0=MUL, op1=ADD),
                                  r=[bp, b_cw, ba], w=[ba], c=0.2 + n * 0.00105)
                        S.act(lambda e, ac=ac, n=n, cb=cb, o0=o0, i2=i2: e.activation(out=cvo[i2][:, o0:o0 + n], in_=ac[:, 0:n], func=AF.Silu,
                                                                                      bias=cw[:, cb, 5:6]), r=[ba, b_cw], w=[b_cvo[i2]], c=0.3 + n * 0.0009)
                    if cb >= 16:
                        g = (cb - 16) % 4
                        dst = BT if cb < 20 else CT
                        S.dma("sp", dst[g], cvo[i2][:], r=[b_cvo[i2]], w=[bd["BT"] if cb < 20 else bd["CT"]])
                    if cb < 20:
                        for (t0, nt) in ((0, 8), (8, 8), (16, 2)):
                            p2 = ti % 2
                            ti += 1
                            for i in range(nt):
                                S.pe(lambda e, i=i, t0=t0, p2=p2, i2=i2: e.transpose(
                                    ptr[p2][:, i * 128:(i + 1) * 128], cvo[i2][:, (t0 + i) * 128:(t0 + i + 1) * 128], ident_b[:]),
                                     r=[b_cvo[i2], b_const], w=[b_ptr[p2]], c=0.1)
                            S.dve(lambda e, p2=p2, nt=nt: e.tensor_copy(out=tro[p2][:, 0:nt, :],
                                                                        in_=ptr[p2][:, 0:nt * 128].rearrange("p (i c) -> p i c", c=128)),
                                  r=[b_ptr[p2]], w=[b_tro[p2]])
                            if cb < 16:
                                d_ap = xs_tm[t0 * 128:(t0 + nt) * 128, cb * 128:(cb + 1) * 128].rearrange("(i p) c -> p i c", p=128)
                                S.dma("sp", d_ap, tro[p2][:, 0:nt, :], r=[b_tro[p2]], w=[bd["xs_tm"]])
                            else:
                                g = cb - 16
                                d_ap = b_tm[t0 * 128:(t0 + nt) * 128, g * 128:(g + 1) * 128].rearrange("(i p) c -> p i c", p=128)
                                S.dma("sp", d_ap, tro[p2][:, 0:nt, :], r=[b_tro[p2]], w=[bd["b_tm"]])
                S.flush()
        hctx.close()

        if want("ssd"):
            with contextlib.ExitStack() as st:
                tri = kb.sb(st, "tri", [128, 4, 128], F32)
                dtp_all = kb.sb(st, "dtp_all", [128, NT, 64], F32)
                dA_all = kb.sb(st, "dA_all", [128, NT, 64], F32)
                dskip = kb.sb(st, "dskip", [128, 32], F32)
                ssdn = kb.sb(st, "ssdn", [128, 2048], F32)
                b_sc = Buf()
                S.dma("sp", tri[:], I["tri"], w=[b_sc])
                S.dma("sp", dtp_all[:], dtp_d.rearrange("(t p) c -> p t c", p=128), r=[bd["dtp_d"]], w=[b_sc])
                S.dma("sp", dA_all[:], dA_d.rearrange("(t p) c -> p t c", p=128), r=[bd["dA_d"]], w=[b_sc])
                S.dma("sp", dskip[:], I["ev_d_skip"].partition_broadcast(128), w=[b_sc])
                S.dma("sp", ssdn[:], I["ev_ssd_norm"].partition_broadcast(128), w=[b_sc])
                Dsk = kb.sb(st, "Dsk", [128, 32, 128], BF16)
                b_dsk = Buf()
                for h in range(32):
                    S.dve(lambda e, h=h: e.tensor_scalar(out=Dsk[:, h, :], in0=ident_b[:], scalar1=dskip[:, h:h + 1], scalar2=None, op0=MUL),
                          r=[b_sc, b_const], w=[b_dsk], c=0.2)
                H = kb.sb(st, "H", [128, 4, 512], F32)
                Hbf = kb.sb(st, "Hbf", [128, 4, 512], BF16)
                b_H = [Buf() for _ in range(4)]
                b_Hbf = [Buf() for _ in range(4)]
                xs_t = [kb.sb(st, "xs_t%d" % i, [128, 2048], BF16) for i in range(3)]
                btm_t = [kb.sb(st, "btm_t%d" % i, [128, 512], BF16) for i in range(3)]
                BTc = [kb.sb(st, "BTc%d" % i, [128, 4, 128], BF16) for i in range(3)]
                CTc = [kb.sb(st, "CTc%d" % i, [128, 4, 128], BF16) for i in range(3)]
                b_ld = [Buf(), Buf(), Buf()]
                X = [kb.sb(st, "X%d" % i, [128, 2048], BF16) for i in range(2)]
                Xd = [kb.sb(st, "Xd%d" % i, [128, 2048], BF16) for i in range(2)]
                b_X, b_Xd = [Buf(), Buf()], [Buf(), Buf()]
                sm = [kb.sb(st, "sm%d" % i, [128, 4, 32], F32) for i in range(2)]
                b_sm = [Buf(), Buf()]
                rhsA = [kb.sb(st, "rhsA%d" % i, [128, 8, 128], F32) for i in range(2)]
                expd = [kb.sb(st, "expd%d" % i, [128, 8, 128], BF16) for i in range(2)]
                MT = [kb.sb(st, "MT%d" % i, [128, 8, 128], BF16) for i in range(2)]
                CBm = [kb.sb(st, "CBm%d" % i, [128, 128], BF16) for i in range(2)]
                dteb = [kb.sb(st, "dteb%d" % i, [128, 32], BF16) for i in range(2)]
                tmpy = [kb.sb(st, "tmpy%d" % i, [128, 512], F32) for i in range(2)]
                b_rhsA, b_expd, b_MT, b_CBm, b_tmpy = [[Buf(), Buf()] for _ in range(5)]
                yt = [kb.sb(st, "yt%d" % i, [128, 2048], F32) for i in range(2)]
                b_yt = [Buf(), Buf()]
                yf_t = kb.sb(st, "yf_t", [128, 2048], F32)
                sz_t = kb.sb(st, "sz_t", [128, 2048], BF16)
                yn = kb.sb(st, "yn", [128, 2048], BF16)
                junk = kb.sb(st, "sjunk", [128, 512], BF16)
                gss = kb.sb(st, "gss", [128, 12], F32)
                trs = kb.sb(st, "trs", [128, 16, 128], BF16)
                b_yf, b_szt, b_tmpf, b_yn, b_junk, b_gss, b_trs = [Buf() for _ in range(7)]
                pcb = kb.ps(st, "pcb")
                pdiff = [kb.ps(st, "pdiff%d" % i) for i in range(2)]
                pyd = kb.ps(st, "pyd")
                pyo = kb.ps(st, "pyo")
                pst = kb.ps(st, "pst")
                psm = kb.ps(st, "psm")
                ptr = kb.ps(st, "sptr", (128, 1024), BF16)
                b_pcb, b_pyd, b_pyo, b_pst, b_psm, b_ptr = [Buf() for _ in range(6)]
                b_pdiff = [Buf(), Buf()]
                A = ada_setup(st, psum=(ptr[:].bitcast(F32), b_ptr))
                ada_chunks(A, 0, range(4, 6))
                ada_gate(A, 0)
                ada_chunks(A, 1, range(0, 6))
                ada_mod(A, 1)
                ada_gate(A, 1)
                li = 0
                gi = 0
                for d in range(2):
                    for g in range(4):
                        S.pool(lambda e, g=g: e.memset(H[:, g, :], 0.0), w=[b_H[g]])
                        S.pool(lambda e, g=g: e.memset(Hbf[:, g, :], 0.0), w=[b_Hbf[g]])
                    order = ([0, 1] + list(range(2, NT))) if d == 0 else ([1, 0] + list(range(NT - 1, 1, -1)))
                    Td = tri[:, d, :]
                    Wd = tri[:, 2 + d, :]
                    for tt in order:
                        is_lat = tt >= 2
                        l2 = li % 2
                        l3 = li % 3
                        li += 1
                        tok0 = tt * 128
                        S.dma("sp", xs_t[l3][:], xs_tm[tok0:tok0 + 128, :], r=[bd["xs_tm"]], w=[b_ld[l3]])
                        S.dma("sp", btm_t[l3][:], b_tm[tok0:tok0 + 128, :], r=[bd["b_tm"]], w=[b_ld[l3]])
                        S.dma("act", BTc[l3][:], BT[:, :, tok0:tok0 + 128].rearrange("g n t -> n g t"), r=[bd["BT"]], w=[b_ld[l3]])
                        S.dma("act", CTc[l3][:], CT[:, :, tok0:tok0 + 128].rearrange("g n t -> n g t"), r=[bd["CT"]], w=[b_ld[l3]])
                        if d == 1 and is_lat:
                            S.dma("sp", yf_t[:], yf_d[(tt - 2) * 128:(tt - 1) * 128, :], r=[bd["yf_d"]], w=[b_yf])
                        a_ap = dA_all[:, tt, d * 32:(d + 1) * 32]
                        dtp_ap = dtp_all[:, tt, d * 32:(d + 1) * 32]
                        S.pe(lambda e, a_ap=a_ap, Td=Td: e.matmul(psm[:, 0:32], lhsT=Td, rhs=a_ap, start=True, stop=True),
                             r=[b_sc], w=[b_psm], c=0.15)
                        S.pe(lambda e, a_ap=a_ap: e.matmul(psm[:, 32:64], lhsT=ones_f[:], rhs=a_ap, start=True, stop=True),
                             r=[b_sc, b_const], w=[b_psm], c=0.15)
                        smt, bsm = sm[l2], b_sm[l2]
                        S.dve(lambda e, smt=smt: e.tensor_copy(out=smt[:, 0, :], in_=psm[:, 0:32]), r=[b_psm], w=[bsm])
                        S.act(lambda e, smt=smt: e.activation(out=smt[:, 1, :], in_=psm[:, 0:32], func=AF.Exp), r=[b_psm], w=[bsm])
                        S.act(lambda e, smt=smt: e.activation(out=smt[:, 3, :], in_=psm[:, 32:64], func=AF.Exp), r=[b_psm], w=[bsm])
                        S.dve(lambda e, smt=smt: e.tensor_tensor(out=smt[:, 2, :], in0=psm[:, 32:64], in1=smt[:, 0, :], op=SUB),
                              r=[b_psm, bsm], w=[bsm])
                        S.act(lambda e, smt=smt: e.activation(out=smt[:, 2, :], in_=smt[:, 2, :], func=AF.Exp), r=[bsm], w=[bsm])
                        Xv = X[l2][:].rearrange("p (h c) -> p h c", c=64)
                        Xdv = Xd[l2][:].rearrange("p (h c) -> p h c", c=64)
                        xsv = xs_t[l3][:].rearrange("p (h c) -> p h c", c=64)
                        S.pool(lambda e, Xv=Xv, xsv=xsv, dtp_ap=dtp_ap: e.tensor_tensor(
                            out=Xv, in0=xsv, in1=dtp_ap.unsqueeze(2).broadcast_to([128, 32, 64]), op=MUL),
                               r=[b_ld[l3], b_sc], w=[b_X[l2]], c=6.0)
                        S.pool(lambda e, Xv=Xv, Xdv=Xdv, smt=smt: e.tensor_tensor(
                            out=Xdv, in0=Xv, in1=smt[:, 2, :].unsqueeze(2).broadcast_to([128, 32, 64]), op=MUL),
                               r=[b_X[l2], bsm], w=[b_Xd[l2]], c=6.0)
                        y_t, y_b = yt[l2], b_yt[l2]
                        for g in range(4):
                            g2 = gi % 2
                            gi += 1
                            if is_lat:
                                S.pe(lambda e, g=g, l2=l2, l3=l3: e.matmul(pcb[:, 0:128], lhsT=BTc[l3][:, g, :], rhs=CTc[l3][:, g, :],
                                                                    start=True, stop=True), r=[b_ld[l3]], w=[b_pcb], c=0.1)
                                S.dve(lambda e, g2=g2, Td=Td: e.tensor_tensor(out=CBm[g2][:], in0=pcb[:, 0:128], in1=Td, op=MUL),
                                      r=[b_pcb, b_sc], w=[b_CBm[g2]], c=0.25)
                                for h in range(8):
                                    S.act(lambda e, g=g, g2=g2, h=h, a_ap=a_ap, Td=Td: e.activation(
                                        out=rhsA[g2][:, h, :], in_=Td, func=AF.Copy, scale=a_ap[:, g * 8 + h:g * 8 + h + 1]),
                                          r=[b_sc], w=[b_rhsA[g2]], c=0.25)
                                for hf in range(2):
                                    S.pe(lambda e, hf=hf, g2=g2, Wd=Wd: e.matmul(
                                        pdiff[hf][:], lhsT=Wd, rhs=rhsA[g2][:, hf * 4:(hf + 1) * 4, :].rearrange("p h l -> p (h l)"),
                                        start=True, stop=True), r=[b_rhsA[g2], b_sc], w=[b_pdiff[hf]], c=0.9)
                                    S.act(lambda e, hf=hf, g2=g2: e.activation(
                                        out=expd[g2][:, hf * 4:(hf + 1) * 4, :].rearrange("p h l -> p (h l)"), in_=pdiff[hf][:], func=AF.Exp),
                                          r=[b_pdiff[hf]], w=[b_expd[g2]])
                                S.dve(lambda e, g2=g2: e.tensor_tensor(out=MT[g2][:], in0=expd[g2][:],
                                                                       in1=CBm[g2][:].unsqueeze(1).broadcast_to([128, 8, 128]), op=MUL),
                                      r=[b_expd[g2], b_CBm[g2]], w=[b_MT[g2]], c=0.7)
                                if d == 1:
                                    S.pe(lambda e, g=g: e.matmul(pyd[:], lhsT=ident_f[:], rhs=yf_t[:, g * 512:(g + 1) * 512],
                                                                 start=True, stop=False), r=[b_yf, b_const], w=[b_pyd], c=0.9)
                                for h in range(8):
                                    c0 = (g * 8 + h) * 64
                                    S.pe(lambda e, h=h, g2=g2, l2=l2, l3=l3, c0=c0, d=d: e.matmul(
                                        pyd[:, h * 64:(h + 1) * 64], lhsT=MT[g2][:, h, :], rhs=X[l2][:, c0:c0 + 64],
                                        start=(d == 0), stop=(d == 1 and h == 7)),
                                         r=[b_MT[g2], b_X[l2]], w=[b_pyd], c=0.1)
                                    if d == 0:
                                        S.pe(lambda e, h=h, g=g, l3=l3, c0=c0: e.matmul(
                                            pyd[:, h * 64:(h + 1) * 64], lhsT=Dsk[:, g * 8 + h, :], rhs=xs_t[l3][:, c0:c0 + 64],
                                            start=False, stop=True), r=[b_dsk, b_ld[l3]], w=[b_pyd], c=0.1)
                                S.pe(lambda e, g=g, l2=l2, l3=l3: e.matmul(pyo[:], lhsT=CTc[l3][:, g, :], rhs=Hbf[:, g, :], start=True, stop=True),
                                     r=[b_ld[l3], b_Hbf[g]], w=[b_pyo])
                                S.dve(lambda e, g=g, g2=g2, smt=smt: e.tensor_tensor(
                                    out=tmpy[g2][:].rearrange("p (h c) -> p h c", c=64), in0=pyo[:].rearrange("p (h c) -> p h c", c=64),
                                    in1=smt[:, 1, g * 8:(g + 1) * 8].unsqueeze(2).broadcast_to([128, 8, 64]), op=MUL),
                                      r=[b_pyo, bsm], w=[b_tmpy[g2]])
                                S.dve(lambda e, g=g, g2=g2, y_t=y_t: e.tensor_tensor(out=y_t[:, g * 512:(g + 1) * 512], in0=pyd[:],
                                                                                    in1=tmpy[g2][:], op=ADD),
                                      r=[b_pyd, b_tmpy[g2]], w=[y_b])
                            S.pe(lambda e, g=g, l2=l2, l3=l3: e.matmul(pst[:], lhsT=btm_t[l3][:, g * 128:(g + 1) * 128],
                                                                rhs=Xd[l2][:, g * 512:(g + 1) * 512], start=True, stop=True),
                                 r=[b_ld[l3], b_Xd[l2]], w=[b_pst])
                            S.pool(lambda e, g=g, smt=smt: e.tensor_tensor(
                                out=H[:, g, :].rearrange("p (h c) -> p h c", c=64), in0=H[:, g, :].rearrange("p (h c) -> p h c", c=64),
                                in1=smt[:, 3, g * 8:(g + 1) * 8].unsqueeze(2).broadcast_to([128, 8, 64]), op=MUL),
                                   r=[b_H[g], bsm], w=[b_H[g]], c=2.0)
                            S.dve(lambda e, g=g: e.tensor_tensor(out=H[:, g, :], in0=H[:, g, :], in1=pst[:], op=ADD),
                                  r=[b_H[g], b_pst], w=[b_H[g]])
                            S.act(lambda e, g=g: e.activation(out=Hbf[:, g, :], in_=H[:, g, :], func=AF.Copy), r=[b_H[g]], w=[b_Hbf[g]])
                        if not is_lat:
                            continue
                        lt = tt - 2
                        if d == 0:
                            S.dma("sp", yf_d[lt * 128:(lt + 1) * 128, :], y_t[:], r=[y_b], w=[bd["yf_d"]])
                            continue
                        S.dma("act", sz_t[:], sz[lt * 128:(lt + 1) * 128, :], r=[bd["sz"]], w=[b_szt])
                        S.dve(lambda e, y_t=y_t: e.tensor_tensor(out=y_t[:], in0=y_t[:], in1=sz_t[:], op=MUL), r=[y_b, b_szt], w=[y_b], c=2.3)
                        for g in range(4):
                            S.act(lambda e, g=g, y_t=y_t: e.activation(out=junk[:], in_=y_t[:, g * 512:(g + 1) * 512], func=AF.Square,
                                                                        accum_out=gss[:, g:g + 1]), r=[y_b], w=[b_junk, b_gss])
                        S.act(lambda e: e.activation(out=gss[:, 4:8], in_=gss[:, 0:4], func=AF.Sqrt, scale=1.0 / 512, bias=eps_t[:, 0:1]),
                              r=[b_gss, b_const], w=[b_gss])
                        S.dve(lambda e: e.reciprocal(out=gss[:, 8:12], in_=gss[:, 4:8]), r=[b_gss], w=[b_gss])
                        for g in range(4):
                            S.dve(lambda e, g=g, y_t=y_t: e.scalar_tensor_tensor(
                                out=yn[:, g * 512:(g + 1) * 512], in0=y_t[:, g * 512:(g + 1) * 512], scalar=gss[:, 8 + g:9 + g],
                                in1=ssdn[:, g * 512:(g + 1) * 512], op0=MUL, op1=MUL), r=[y_b, b_gss, b_sc], w=[b_yn])
                        for half in range(2):
                            for i in range(8):
                                cbi = half * 8 + i
                                S.pe(lambda e, i=i, cbi=cbi: e.transpose(ptr[:, i * 128:(i + 1) * 128], yn[:, cbi * 128:(cbi + 1) * 128], ident_b[:]),
                                     r=[b_yn, b_const], w=[b_ptr], c=0.1)
                            S.act(lambda e, half=half: e.activation(out=trs[:, half * 8:(half + 1) * 8, :].rearrange("p i c -> p (i c)"),
                                                                    in_=ptr[:], func=AF.Copy), r=[b_ptr], w=[b_trs])
                        S.dma("sp", mixT[lt, :, 16:32, :], trs[:], r=[b_trs], w=[bd["mixT"]])
                S.flush()

        if want("att"):
            with contextlib.ExitStack() as st:
                kTg = [kb.sb(st, "kTg%d" % i, [128, NTOK], BF16) for i in range(2)]
                vg = [kb.sb(st, "vg%d" % i, [128, NT, 128], BF16) for i in range(2)]
                qh = [kb.sb(st, "qh%d" % i, [128, SEQ], BF16) for i in range(2)]
                sgh = [kb.sb(st, "sgh%d" % i, [128, SEQ], BF16) for i in range(2)]
                b_kv, b_qh = [Buf(), Buf()], [Buf(), Buf()]
                PT = [kb.sb(st, "PT%d" % i, [128, 512], BF16) for i in range(6)]
                b_PT = [Buf() for _ in range(6)]
                negone = kb.sb(st, "negone", [128, 512], F32)
                b_negone = Buf()
                S.pool(lambda e: e.memset(negone[:], -1.0), w=[b_negone])
                PS = [kb.sb(st, "PS%d" % i, [128, 512], BF16) for i in range(3)]
                b_PS = [Buf() for _ in range(3)]
                gsi = 0
                rden = [kb.sb(st, "rden%d" % i, [128, 512], F32) for i in range(2)]
                ao = [kb.sb(st, "ao%d" % i, [128, 512], F32) for i in range(2)]
                aob = [kb.sb(st, "aob%d" % i, [128, 512], BF16) for i in range(2)]
                b_rden, b_ao, b_aob = [[Buf(), Buf()] for _ in range(3)]
                pS = [kb.ps(st, "pS%d" % i) for i in range(3)]
                pO = [kb.ps(st, "pO%d" % i) for i in range(2)]
                pD = [kb.ps(st, "pD%d" % i) for i in range(2)]
                b_pS = [Buf() for _ in range(3)]
                b_pO, b_pD = [Buf(), Buf()], [Buf(), Buf()]

                si = 0
                oi = 0
                sc = 1.0 / math.sqrt(128.0)
                for g in range(4):
                    kv2 = g % 2
                    S.dma("sp", kTg[kv2][:], kT[g], r=[bd["kT"]], w=[b_kv[kv2]])
                    S.dma("act", vg[kv2][:], vtm[:, g * 128:(g + 1) * 128].rearrange("(t p) c -> p t c", p=128), r=[bd["vtm"]], w=[b_kv[kv2]])
                    for hh in range(4):
                        h = g * 4 + hh
                        q2 = h % 2
                        S.dma("sp", qh[q2][:], qT[h], r=[bd["qT"]], w=[b_qh[q2]])
                        S.dma("act", sgh[q2][:], sgT[h], r=[bd["sgT"]], w=[b_qh[q2]])
                        for qb in range(4):
                            o2 = oi % 2
                            oi += 1
                            for kt in range(NT):
                                s3 = si % 3
                                p6 = si % 6
                                si += 1
                                S.pe(lambda e, kt=kt, s3=s3, kv2=kv2, q2=q2, qb=qb: e.matmul(
                                    pS[s3][:], lhsT=kTg[kv2][:, kt * 128:(kt + 1) * 128], rhs=qh[q2][:, qb * 512:(qb + 1) * 512],
                                    start=True, stop=True), r=[b_kv[kv2], b_qh[q2]], w=[b_pS[s3]])
                                S.act(lambda e, s3=s3, p6=p6: e.activation(out=PT[p6][:], in_=pS[s3][:], func=AF.Exp, scale=sc),
                                      r=[b_pS[s3]], w=[b_PT[p6]], c=0.6)
                                S.pe(lambda e, kt=kt, p6=p6, kv2=kv2, o2=o2: e.matmul(
                                    pO[o2][:], lhsT=vg[kv2][:, kt, :], rhs=PT[p6][:], start=(kt == 0), stop=(kt == NT - 1)),
                                     r=[b_kv[kv2], b_PT[p6]], w=[b_pO[o2]])
                                if kt % 3 == 2:
                                    pa, pb, pc = (p6 - 2) % 6, (p6 - 1) % 6, p6
                                    g3 = gsi % 3
                                    gsi += 1
                                    S.dve(lambda e, pa=pa, pb=pb, g3=g3: e.tensor_tensor(out=PS[g3][:], in0=PT[pa][:], in1=PT[pb][:], op=ADD),
                                          r=[b_PT[pa], b_PT[pb]], w=[b_PS[g3]], c=0.45)
                                    S.dve(lambda e, pc=pc, g3=g3: e.tensor_tensor(out=PS[g3][:], in0=PS[g3][:], in1=PT[pc][:], op=ADD),
                                          r=[b_PS[g3], b_PT[pc]], w=[b_PS[g3]], c=0.45)
                                    S.pe(lambda e, kt=kt, g3=g3, o2=o2: e.matmul(
                                        pD[o2][:], lhsT=ones_b[:], rhs=PS[g3][:], start=(kt == 2), stop=(kt == NT - 1)),
                                         r=[b_const, b_PS[g3]], w=[b_pD[o2]])
                            S.act(lambda e, o2=o2: e.activation(out=rden[o2][:], in_=pD[o2][:], func=AF.Ln), r=[b_pD[o2]], w=[b_rden[o2]], c=0.6)
                            S.act(lambda e, o2=o2: e.activation(out=rden[o2][:], in_=rden[o2][:], func=AF.Exp, scale=-1.0),
                                  r=[b_rden[o2]], w=[b_rden[o2]], c=0.6)
                            S.dve(lambda e, o2=o2: e.tensor_tensor(out=ao[o2][:], in0=pO[o2][:], in1=rden[o2][:], op=MUL),
                                  r=[b_pO[o2], b_rden[o2]], w=[b_ao[o2]])
                            S.pool(lambda e, o2=o2, q2=q2, qb=qb: e.tensor_tensor(out=aob[o2][:], in0=ao[o2][:],
                                                                                   in1=sgh[q2][:, qb * 512:(qb + 1) * 512], op=MUL),
                                   r=[b_ao[o2], b_qh[q2]], w=[b_aob[o2]], c=2.0)
                            S.dma("sp", mixT[qb * 4:(qb + 1) * 4, :, h, :].rearrange("t p c -> p t c"),
                                  aob[o2][:].rearrange("p (t c) -> p t c", c=128), r=[b_aob[o2]], w=[bd["mixT"]])
                S.flush()

        def out_phase(layer, mix_d, mix_b, w_ap, xsrc, xsrc_b, dst, dst_b):
            with contextlib.ExitStack() as st:
                wo_t = [kb.sb(st, "wo%d" % i, [128, 32, 512], BF16) for i in range(2)]
                b_wo = [WB(), WB()]
                mx = [kb.sb(st, "mx%d" % i, [128, 32, 128], BF16) for i in range(2)]
                xr = [kb.sb(st, "xr%d" % i, [128, 512], F32) for i in range(2)]
                oo = [kb.sb(st, "oo%d" % i, [128, 512], F32) for i in range(2)]
                b_mx, b_xr, b_oo = [[Buf(), Buf()] for _ in range(3)]
                pacc = [kb.ps(st, "oacc%d" % i) for i in range(2)]
                b_pacc = [Buf(), Buf()]
                it = 0
                for cbk in range(4):
                    w_t, w_b = wo_t[cbk % 2], b_wo[cbk % 2]
                    src = w_ap[:, cbk * 512:(cbk + 1) * 512].rearrange("(kc p) n -> p kc n", p=128)
                    for k0 in range(0, 32, 4):
                        S.dma("pool", w_t[:, k0:k0 + 4, :], src[:, k0:k0 + 4, :], w=[w_b.p(k0)])
                    for tt in range(16):
                        i2 = it % 2
                        it += 1
                        S.dma("sp", mx[i2][:], mix_d[tt], r=[mix_b], w=[b_mx[i2]])
                        S.dma("act", xr[i2][:], xsrc[tt * 128:(tt + 1) * 128, cbk * 512:(cbk + 1) * 512], r=xsrc_b, w=[b_xr[i2]])
                        for kc in range(32):
                            S.pe(lambda e, kc=kc, i2=i2, w_t=w_t: e.matmul(pacc[i2][:], lhsT=mx[i2][:, kc, :], rhs=w_t[:, kc, :],
                                                                            start=(kc == 0), stop=(kc == 31)),
                                 r=[b_mx[i2], w_b.p(kc)], w=[b_pacc[i2]])
                        S.dve(lambda e, i2=i2, cbk=cbk: e.tensor_tensor(out=oo[i2][:], in0=pacc[i2][:],
                                                                        in1=gate_bc[:, layer, cbk * 512:(cbk + 1) * 512], op=MUL),
                              r=[b_pacc[i2], b_gate], w=[b_oo[i2]])
                        S.pool(lambda e, i2=i2: e.tensor_tensor(out=oo[i2][:], in0=oo[i2][:], in1=xr[i2][:], op=ADD),
                               r=[b_oo[i2], b_xr[i2]], w=[b_oo[i2]], c=2.0)
                        S.dma("sp", dst[tt * 128:(tt + 1) * 128, cbk * 512:(cbk + 1) * 512], oo[i2][:], r=[b_oo[i2]], w=[dst_b])
                S.flush()

        if want("out0"):
            out_phase(0, mixT, bd["mixT"], I["ev_w_out"], I["x"], [], x1, bd["x1"])

        if want("norm1"):
            hctx = contextlib.ExitStack()
            hT = kb.sb(hctx, "hT1", [128, 16, NTOK], BF16)
            srcs = []
            for tb in range(4):
                srcs.append((x1[tb * 512:(tb + 1) * 512, :], [bd["x1"]], CTX + tb * 512, 4, b_hT[1 + tb], 0))
            norm_phase(1, srcs)

        if want("proj1"):
            with contextlib.ExitStack() as st:
                wg = [kb.sb(st, "w1_%d" % i, [128, 16, 512], BF16) for i in range(2)]
                b_wg = [WB(), WB()]
                og = [kb.sb(st, "o1_%d" % i, [128, 512], BF16) for i in range(3)]
                b_og = [Buf() for _ in range(3)]
                pacc = [kb.ps(st, "p1acc%d" % i) for i in range(3)]
                b_pacc = [Buf() for _ in range(3)]
                it = 0
                for blk in range(16):
                    w_t, w_b = wg[blk % 2], b_wg[blk % 2]
                    load_w(w_t, w_b, I["od_w_in"], blk * 512, 512)
                    isu = blk < 8
                    for hh in range(4):
                        j = (blk % 8) * 4 + hh
                        for (tok0, ntok, hbuf, lat) in tblocks[1:]:
                            i3 = it % 3
                            it += 1
                            for kc in range(16):
                                S.pe(lambda e, kc=kc, w_t=w_t, i3=i3, tok0=tok0, hh=hh: e.matmul(
                                    pacc[i3][:], lhsT=w_t[:, kc, hh * 128:(hh + 1) * 128], rhs=hT[:, kc, tok0:tok0 + 512],
                                    start=(kc == 0), stop=(kc == 15)), r=[w_b.p(kc), hbuf], w=[b_pacc[i3]])
                            if isu:
                                S.dve(lambda e, i3=i3: e.tensor_copy(out=og[i3][:], in_=pacc[i3][:]), r=[b_pacc[i3]], w=[b_og[i3]])
                                S.dma("sp", uT[j, :, lat:lat + 512], og[i3][:], r=[b_og[i3]], w=[bd["uT"]])
                            else:
                                S.act(lambda e, i3=i3: e.activation(out=og[i3][:], in_=pacc[i3][:], func=AF.Silu),
                                      r=[b_pacc[i3]], w=[b_og[i3]])
                                S.dma("sp", szT[j, :, lat:lat + 512], og[i3][:], r=[b_og[i3]], w=[bd["szT"]])
                S.flush()
            hctx.close()

        if want("dft1"):
            with contextlib.ExitStack() as st:
                dftc = kb.sb(st, "dftc", [128, 2, 4, 512], BF16)
                b_dc = Buf()
                S.dma("sp", dftc[:], I["dftc"], w=[b_dc])
                ug = [kb.sb(st, "ug%d" % i, [128, 4, SEQ], BF16) for i in range(2)]
                b_ug = [Buf(), Buf()]
                uo = [[kb.sb(st, "uo%d_%d" % (cs, i), [128, 16, 512], BF16) for i in range(2)] for cs in range(2)]
                b_uo = [[Buf(), Buf()], [Buf(), Buf()]]
                pacc = [kb.ps(st, "dacc%d" % i) for i in range(3)]
                b_pacc = [Buf() for _ in range(3)]
                it = 0
                for g in range(8):
                    g2 = g % 2
                    S.dma("sp", ug[g2][:, 0:2, :], uT[g * 4:g * 4 + 2].rearrange("j p t -> p j t"), r=[bd["uT"]], w=[b_ug[g2]])
                    S.dma("act", ug[g2][:, 2:4, :], uT[g * 4 + 2:g * 4 + 4].rearrange("j p t -> p j t"), r=[bd["uT"]], w=[b_ug[g2]])
                    for tt in range(16):
                        t0 = (tt // 8) + 256 * (tt % 8)
                        for cs in range(2):
                            i3 = it % 3
                            it += 1
                            for cc in range(4):
                                S.pe(lambda e, cc=cc, cs=cs, t0=t0, g2=g2, i3=i3: e.matmul(
                                    pacc[i3][:], lhsT=ug[g2][:, cc, t0:t0 + 255:2], rhs=dftc[:, cs, cc, :],
                                    start=(cc == 0), stop=(cc == 3)), r=[b_ug[g2], b_dc], w=[b_pacc[i3]])
                            if cs == 0:
                                S.dve(lambda e, i3=i3, tt=tt, g2=g2: e.tensor_copy(out=uo[0][g2][:, tt, :], in_=pacc[i3][:]),
                                      r=[b_pacc[i3]], w=[b_uo[0][g2]])
                            else:
                                S.act(lambda e, i3=i3, tt=tt, g2=g2: e.activation(out=uo[1][g2][:, tt, :], in_=pacc[i3][:], func=AF.Copy),
                                      r=[b_pacc[i3]], w=[b_uo[1][g2]])
                    for cs in range(2):
                        S.dma("sp", UCd[cs, g], uo[cs][g2][:], r=[b_uo[cs][g2]], w=[bd["UCd"]], lat=12.0)
                S.flush()

        if want("dft2"):
            with contextlib.ExitStack() as st:
                dftl = kb.sb(st, "dftl", [128, 2, 2, 8, 1024], BF16)
                b_dl = [[Buf(), Buf()], [Buf(), Buf()]]
                for par in range(2):
                    for cs in range(2):
                        S.dma("sp" if cs == 0 else "act", dftl[:, par, cs, :, :], I["dftl"][:, par, cs, :, :], w=[b_dl[par][cs]])
                ucs = [kb.sb(st, "ucs%d" % i, [128, 2, 16, 512], BF16) for i in range(2)]
                szj = [kb.sb(st, "szj%d" % i, [128, SEQ], BF16) for i in range(2)]
                b_ucs, b_szj = [Buf(), Buf()], [Buf(), Buf()]
                osb = [kb.sb(st, "osb%d" % i, [128, 512], F32) for i in range(2)]
                fa = [kb.sb(st, "fa%d" % i, [128, 2, 512], F32) for i in range(2)]
                mo = [kb.sb(st, "mo%d" % i, [128, 2, 512], BF16) for i in range(2)]
                b_osb, b_fa, b_mo = [[Buf(), Buf()] for _ in range(3)]
                pE = [kb.ps(st, "pE%d" % i) for i in range(2)]
                pO = [kb.ps(st, "pOd%d" % i) for i in range(2)]
                b_pE, b_pO = [Buf(), Buf()], [Buf(), Buf()]
                it = 0
                for j in range(32):
                    j2 = j % 2
                    u2 = (j // 4) % 2
                    jl = j % 4
                    if jl == 0:
                        S.dma("sp", ucs[u2][:, 0, :, :], UCd[0, j // 4], r=[bd["UCd"]], w=[b_ucs[u2]])
                        S.dma("act", ucs[u2][:, 1, :, :], UCd[1, j // 4], r=[bd["UCd"]], w=[b_ucs[u2]])
                    S.dma("sp", szj[j2][:], szT[j], r=[bd["szT"]], w=[b_szj[j2]])
                    for kbk in range(2):
                        i2 = it % 2
                        it += 1
                        for par, (pp, bpp) in enumerate(((pE[i2], b_pE[i2]), (pO[i2], b_pO[i2]))):
                            n = 0
                            for tt8 in range(8):
                                for cs in range(2):
                                    S.pe(lambda e, tt8=tt8, cs=cs, u2=u2, jl=jl, pp=pp, kbk=kbk, n=n, par=par: e.matmul(
                                        pp[:], lhsT=ucs[u2][:, cs, par * 8 + tt8, jl * 128:(jl + 1) * 128],
                                        rhs=dftl[:, par, cs, tt8, kbk * 512:(kbk + 1) * 512],
                                        start=(n == 0), stop=(n == 15)), r=[b_ucs[u2], b_dl[par][cs]], w=[bpp])
                                    n += 1
                        S.act(lambda e, i2=i2: e.activation(out=osb[i2][:], in_=pO[i2][:], func=AF.Copy), r=[b_pO[i2]], w=[b_osb[i2]], c=0.6)
                        S.dve(lambda e, i2=i2: e.tensor_tensor(out=fa[i2][:, 0, :], in0=pE[i2][:], in1=osb[i2][:], op=ADD),
                              r=[b_pE[i2], b_osb[i2]], w=[b_fa[i2]])
                        S.dve(lambda e, i2=i2: e.tensor_tensor(out=fa[i2][:, 1, :], in0=pE[i2][:], in1=osb[i2][:], op=SUB),
                              r=[b_pE[i2], b_osb[i2]], w=[b_fa[i2]])
                        k0 = kbk * 512
                        S.pool(lambda e, i2=i2, j2=j2, k0=k0: e.tensor_tensor(out=mo[i2][:, 0, :], in0=fa[i2][:, 0, :],
                                                                              in1=szj[j2][:, k0:k0 + 512], op=MUL),
                               r=[b_fa[i2], b_szj[j2]], w=[b_mo[i2]], c=2.0)
                        S.dve(lambda e, i2=i2, j2=j2, k0=k0: e.tensor_tensor(out=mo[i2][:, 1, :], in0=fa[i2][:, 1, :],
                                                                             in1=szj[j2][:, 1024 + k0:1024 + k0 + 512], op=MUL),
                              r=[b_fa[i2], b_szj[j2]], w=[b_mo[i2]])
                        for hf in range(2):
                            tt0 = hf * 8 + kbk * 4
                            S.dma("sp", mixT1[tt0:tt0 + 4, :, j, :].rearrange("t p c -> p t c"),
                                  mo[i2][:, hf, :].rearrange("p (t c) -> p t c", c=128), r=[b_mo[i2]], w=[bd["mixT1"]])
                S.flush()

        if want("out1"):
            b_out = Buf("y")
            out_phase(1, mixT1, bd["mixT1"], I["od_w_out"], x1, [bd["x1"]], kb.out, b_out)

        S.flush()
    return kb


_CONSTS = None


def _prep_inputs(inputs, b):
    global _CONSTS
    if _CONSTS is None:
        _CONSTS = host_consts()
    m = {}
    for k, shp in IN_SHAPES.items():
        a = np.asarray(inputs[k])
        if k in ("x", "c", "ctx"):
            a = a[b]
        elif k == "c_ctx":
            pass
        else:
            a = a[0]
        m[k] = np.ascontiguousarray(a, dtype=np.float32).reshape(shp)
    m.update(_CONSTS)
    return m


def kernel(**inputs):
    kb = build()
    in_maps = [_prep_inputs(inputs, b) for b in range(8)]
    res = run_bass_kernel_spmd(kb.nc, in_maps, core_ids=list(range(8)))
    return np.stack([np.asarray(r["y"], dtype=np.float32) for r in res.results], axis=0)
```

```python
import heapq
import numpy as np
import concourse.bass as bass
import concourse.mybir as mybir
from concourse.bass_utils import run_bass_kernel_spmd

F32 = mybir.dt.float32
BF16 = mybir.dt.bfloat16
AF = mybir.ActivationFunctionType
ALU = mybir.AluOpType
AX = mybir.AxisListType

ENGS = ("pe", "act", "dve", "pool", "sp")
DEFAULT_COST = {"pe": 0.25, "act": 0.7, "dve": 0.7, "pool": 3.0, "sp": 0.05}


class Buf:
    __slots__ = ("name", "w", "r")

    def __init__(self, name=""):
        self.name = name
        self.w = None
        self.r = []


class Sched:
    def __init__(self, nc, n_dma_sems=48):
        self.nc = nc
        self.eng = {"pe": nc.tensor, "act": nc.scalar, "dve": nc.vector,
                    "pool": nc.gpsimd, "sp": nc.sync}
        self.ops = []
        self.bufs = set()
        self.n_dma_sems = n_dma_sems
        self.ndma = 0
        self.pos = {e: 0 for e in ENGS}
        self.seen = {e: {f: -1 for f in ENGS} for e in ENGS}
        self.seen_dma = {e: set() for e in ENGS}
        self.total = {e: 0 for e in ENGS}
        self.reorder = True
        self.nrec = 0

    def op(self, eng, fn, reads=(), writes=(), dma=False, cost=None, lat=None):
        if cost is None:
            cost = DEFAULT_COST[eng]
            if dma:
                cost = 1.0 if eng == "pool" else 0.1
        rec = dict(eng=eng, fn=fn, dma=dma, deps={}, cost=cost, lat=lat, i=self.nrec, sig=False)
        self.nrec += 1
        deps = rec["deps"]
        for b in reads:
            if b.w is not None:
                deps[b.w["i"]] = (b.w, True)
        for b in writes:
            if b.w is not None and b.w["i"] not in deps:
                deps[b.w["i"]] = (b.w, False)
            for r in b.r:
                if r["i"] not in deps:
                    deps[r["i"]] = (r, False)
        for b in writes:
            b.w = rec
            b.r = []
            self.bufs.add(b)
        for b in reads:
            b.r.append(rec)
            self.bufs.add(b)
        self.ops.append(rec)
        return rec

    def pe(self, fn, r=(), w=(), c=None):
        return self.op("pe", fn, r, w, cost=c)

    def act(self, fn, r=(), w=(), c=None):
        return self.op("act", fn, r, w, cost=c)

    def dve(self, fn, r=(), w=(), c=None):
        return self.op("dve", fn, r, w, cost=c)

    def pool(self, fn, r=(), w=(), c=None):
        return self.op("pool", fn, r, w, cost=c)

    def dma(self, q, out, in_, r=(), w=(), slow=False, lat=4.5):
        if slow:
            return self.op(q, lambda e: e.dma_start(out=out, in_=in_, allow_slow_non_contiguous=True), r, w, dma=True, lat=lat)
        return self.op(q, lambda e: e.dma_start(out=out, in_=in_), r, w, dma=True, lat=lat)

    def barrier(self):
        pass

    def begin(self, stack):
        nc = self.nc
        self.csem = {e: stack.enter_context(nc.semaphore("c_" + e)) for e in ENGS}
        self.dsem = [stack.enter_context(nc.semaphore("d_%d" % i)) for i in range(self.n_dma_sems)]
        self.cnt = {e: 0 for e in ENGS}
        self.snap = {}

    def _schedule(self, batch):
        if not self.reorder:
            return list(batch)
        n = len(batch)
        ids = {o["i"]: k for k, o in enumerate(batch)}
        nwait = [0] * n
        users = [[] for _ in range(n)]
        for k, o in enumerate(batch):
            for pi in o["deps"]:
                if pi in ids:
                    nwait[k] += 1
                    users[ids[pi]].append(k)
        ready_t = [0.0] * n
        fut = {e: [] for e in ENGS}
        av = {e: [] for e in ENGS}
        free = {e: 0.0 for e in ENGS}
        for k, o in enumerate(batch):
            if nwait[k] == 0:
                heapq.heappush(fut[o["eng"]], (0.0, k))
        order = []
        done = 0
        while done < n:
            best = None
            for e in ENGS:
                f = fut[e]
                a = av[e]
                while f and f[0][0] <= free[e]:
                    heapq.heappush(a, heapq.heappop(f)[1])
                if a:
                    cand = (free[e], a[0], e, True)
                elif f:
                    cand = (f[0][0], f[0][1], e, False)
                else:
                    continue
                if best is None or cand[:2] < best[:2]:
                    best = cand
            t, k, e, from_av = best
            if from_av:
                heapq.heappop(av[e])
            else:
                heapq.heappop(fut[e])
            o = batch[k]
            start = max(free[e], ready_t[k])
            free[e] = start + o["cost"]
            fin = free[e] if not o["dma"] else start + o["cost"] + (o["lat"] or 3.0)
            order.append(o)
            done += 1
            for u in users[k]:
                nwait[u] -= 1
                if fin > ready_t[u]:
                    ready_t[u] = fin
                if nwait[u] == 0:
                    heapq.heappush(fut[batch[u]["eng"]], (ready_t[u], u))
        self.sim_time = max(free.values())
        return order

    def flush(self):
        batch = self.ops
        self.ops = []
        N = self.n_dma_sems
        order = self._schedule(batch)
        for o in order:
            if o["dma"]:
                o["did"] = self.ndma
                self.ndma += 1
            else:
                o["pos"] = self.pos[o["eng"]]
                self.pos[o["eng"]] += 1
        last_real = {}
        for o in order:
            eng = o["eng"]
            seen = self.seen[eng]
            sd = self.seen_dma[eng]
            need_c = {}
            need_d = set()
            for (p, is_raw) in o["deps"].values():
                if p["dma"]:
                    if p["did"] not in sd:
                        need_d.add(p["did"])
                else:
                    f = p["eng"]
                    if f == eng and not o["dma"]:
                        if eng == "pe":
                            continue
                    if p["pos"] > seen[f] and p["pos"] > need_c.get(f, (-1, None))[0]:
                        need_c[f] = (p["pos"], p)
            if o["dma"] and o["did"] >= N and (o["did"] - N) not in sd:
                need_d.add(o["did"] - N)
            for f, (ps, p) in need_c.items():
                p["sig"] = True
                if ps > seen[f]:
                    seen[f] = ps
                for g, j in self.snap[(f, ps)].items():
                    if j > seen[g]:
                        seen[g] = j
            for d in need_d:
                sd.add(d)
            if len(sd) > 6 * N:
                lo = self.ndma - 3 * N
                self.seen_dma[eng] = set(x for x in sd if x >= lo)
            o["wc"] = [p for (_, p) in need_c.values()]
            o["wd"] = sorted(need_d)
            if not o["dma"]:
                self.snap[(eng, o["pos"])] = dict(seen)
                last_real[eng] = o
        for o in last_real.values():
            o["sig"] = True
        for o in order:
            if not o["dma"] and o["sig"]:
                self.cnt[o["eng"]] += 1
                o["sigval"] = self.cnt[o["eng"]]
        for o in order:
            e = self.eng[o["eng"]]
            for p in o["wc"]:
                e.wait_ge(self.csem[p["eng"]], p["sigval"])
            for d in o["wd"]:
                e.wait_ge(self.dsem[d % N], 16 * (d // N + 1))
            ins = o["fn"](e)
            if o["dma"]:
                ins.then_inc(self.dsem[o["did"] % N], 16)
            elif o["sig"]:
                ins.then_inc(self.csem[o["eng"]], 1)
            o["fn"] = None
            self.total[o["eng"]] += 1
        dlo = max(0, self.ndma - N)
        for en in ENGS:
            e = self.eng[en]
            for f, o in last_real.items():
                e.wait_ge(self.csem[f], o["sigval"])
                self.seen[en][f] = max(self.seen[en][f], o["pos"])
            for d in range(dlo, self.ndma):
                e.wait_ge(self.dsem[d % N], 16 * (d // N + 1))
        self.eng["pe"].nop()
        for en in ENGS:
            self.seen_dma[en] = set(range(dlo, self.ndma))
        for b in self.bufs:
            b.w = None
            b.r = []
        self.bufs = set()
        self.snap = {}
        self.stats = dict(cnt=dict(self.cnt), total=dict(self.total), ndma=self.ndma)

import contextlib
import math
import ml_dtypes

D = 2048
SEQ = 2048
CTX = 256
NTOK = SEQ + CTX
NT = NTOK // 128
EV_COLS = 10304
EPS = 1e-6

C_Q, C_G, C_Z, C_K, C_V, C_XBC, C_DT = 0, 2048, 4096, 6144, 6656, 7168, 10240


def host_consts():
    c = {}
    c["ident_f"] = np.eye(128, dtype=np.float32)
    c["ident_b"] = np.eye(128, dtype=np.float32).astype(ml_dtypes.bfloat16)
    c["ones_f"] = np.ones((128, 128), np.float32)
    c["ones_b"] = np.ones((128, 128), np.float32).astype(ml_dtypes.bfloat16)
    s = np.arange(128)
    tf = (s[:, None] <= s[None, :]).astype(np.float32)
    tb = (s[:, None] >= s[None, :]).astype(np.float32)
    wf = (s[:, None] > s[None, :]).astype(np.float32)
    wb = (s[:, None] < s[None, :]).astype(np.float32)
    c["tri"] = np.stack([tf, tb, wf, wb], axis=1).copy()
    pos = np.arange(SEQ)
    row = (pos // 64).astype(np.float32)
    col = (pos % 64).astype(np.float32)
    inv = (np.float32(10000.0) ** (-np.arange(0, 64, 2, dtype=np.float32) / np.float32(64))).astype(np.float32)
    ang_row = (row[:, None] * inv).astype(np.float32)
    ang_col = (col[:, None] * inv).astype(np.float32)
    cosT = np.zeros((128, SEQ), np.float32)
    sinT = np.zeros((128, SEQ), np.float32)
    RT = np.zeros((128, 128), np.float32)
    for p in range(128):
        ax, i = p // 64, p % 64
        f = i % 32
        ang = ang_row[:, f] if ax == 0 else ang_col[:, f]
        cosT[p] = np.cos(ang.astype(np.float32))
        sinT[p] = np.sin(ang.astype(np.float32))
        if i < 32:
            partner, sgn = p + 32, -1.0
        else:
            partner, sgn = p - 32, 1.0
        RT[partner, p] = sgn
    c["ropecs"] = np.stack([cosT, sinT], axis=1).copy()
    c["RT"] = RT.astype(ml_dtypes.bfloat16)
    n = np.arange(512)
    angc = 2.0 * np.pi * ((n[:, None] * n[None, :]) % 512) / 512.0
    cc = np.cos(angc) / 1024.0
    sc = np.sin(angc) / 1024.0
    dc = np.stack([cc.reshape(4, 128, 512), sc.reshape(4, 128, 512)], axis=0)
    c["dftc"] = np.ascontiguousarray(dc.transpose(2, 0, 1, 3)).astype(ml_dtypes.bfloat16)
    kk = np.arange(1024)
    tabs = []
    for par in range(2):
        tt = 2 * np.arange(1024) + par
        ang = 2.0 * np.pi * ((tt[:, None] * kk[None, :]) % 2048) / 2048.0
        tabs.append(np.stack([np.cos(ang).reshape(8, 128, 1024), (-np.sin(ang)).reshape(8, 128, 1024)], axis=0))
    dl = np.stack(tabs, axis=0)
    c["dftl"] = np.ascontiguousarray(dl.transpose(3, 0, 1, 2, 4)).astype(ml_dtypes.bfloat16)
    return c


CONST_SHAPES = {
    "ident_f": ([128, 128], "f"), "ident_b": ([128, 128], "b"), "ones_f": ([128, 128], "f"),
    "ones_b": ([128, 128], "b"), "tri": ([128, 4, 128], "f"), "ropecs": ([128, 2, 2048], "f"),
    "RT": ([128, 128], "b"), "dftc": ([128, 2, 4, 512], "b"), "dftl": ([128, 2, 2, 8, 1024], "b"),
}

IN_SHAPES = {
    "x": [SEQ, D], "c": [D], "ctx": [CTX, D], "c_ctx": [D],
    "ev_norm_w": [D], "ev_ada_w": [D, 3 * D], "ev_ada_b": [3 * D], "ev_w_in": [D, EV_COLS],
    "ev_q_norm": [128], "ev_k_norm": [128], "ev_conv_w": [5, 3072], "ev_conv_b": [3072],
    "ev_a_log": [64], "ev_dt_bias": [64], "ev_d_skip": [32], "ev_ssd_norm": [2048],
    "ev_w_out": [4096, D], "od_norm_w": [D], "od_ada_w": [D, 3 * D], "od_ada_b": [3 * D],
    "od_w_in": [D, 8192], "od_w_out": [4096, D],
}


class WB:
    def __init__(self, npieces=8):
        self.b = [Buf() for _ in range(npieces)]

    def p(self, kc):
        return self.b[kc // 4]


class KB:
    def __init__(self, debug=(), stop_after=None):
        self.nc = bass.Bass("TRN2", target_bir_lowering=False)
        self.S = Sched(self.nc, n_dma_sems=48)
        self.debug = set(debug)
        self.stop_after = stop_after
        self.es = contextlib.ExitStack()
        self.I = {}
        for k, shp in IN_SHAPES.items():
            self.I[k] = self.nc.dram_tensor(k, shp, F32, kind="ExternalInput").ap()
        for k, (shp, dt) in CONST_SHAPES.items():
            self.I[k] = self.nc.dram_tensor(k, shp, F32 if dt == "f" else BF16, kind="ExternalInput").ap()
        self.out = self.nc.dram_tensor("y", [SEQ, D], F32, kind="ExternalOutput").ap()
        self.scr = {}
        self.dq = 0

    def dram(self, name, shape, dt):
        kind = "ExternalOutput" if name in self.debug else "Internal"
        t = self.nc.dram_tensor(name, shape, dt, kind=kind).ap()
        self.scr[name] = (t, Buf(name))
        return t

    def sb(self, st, name, shape, dt):
        self.uid = getattr(self, "uid", 0) + 1
        return st.enter_context(self.nc.sbuf_tensor("s%d_%s" % (self.uid, name), shape, dt))

    def ps(self, st, name, shape=(128, 512), dt=F32):
        self.uid = getattr(self, "uid", 0) + 1
        return st.enter_context(self.nc.psum_tensor("p%d_%s" % (self.uid, name), list(shape), dt))

    def q(self):
        self.dq ^= 1
        return "sp" if self.dq else "act"


def build(debug=(), stop_after=None):
    kb = KB(debug, stop_after)
    nc, S, I = kb.nc, kb.S, kb.I
    MUL, ADD, SUB = ALU.mult, ALU.add, ALU.subtract
    phases = ["ada", "norm0", "qk", "gv", "xbc", "ssd", "att", "out0", "norm1", "proj1", "dft1", "dft2", "out1"]

    def want(ph):
        return stop_after is None or phases.index(ph) <= phases.index(stop_after)

    with contextlib.ExitStack() as top:
        S.begin(top)
        ident_f = kb.sb(top, "ident_f", [128, 128], F32)
        ident_b = kb.sb(top, "ident_b", [128, 128], BF16)
        ones_f = kb.sb(top, "ones_f", [128, 128], F32)
        ones_b = kb.sb(top, "ones_b", [128, 128], BF16)
        gate_bc = kb.sb(top, "gate_bc", [128, 2, D], F32)
        modT = kb.sb(top, "modT", [128, 2, 4, 16], F32)
        b_const = Buf("const")
        b_gate = Buf("gate")
        b_modT = Buf("modT")
        b_hT = [Buf("hT%d" % i) for i in range(5)]
        for nm, t in (("ident_f", ident_f), ("ident_b", ident_b), ("ones_f", ones_f), ("ones_b", ones_b)):
            S.dma("sp", t[:], I[nm], w=[b_const])

        qT = kb.dram("qT", [16, 128, SEQ], BF16)
        kT = kb.dram("kT", [4, 128, NTOK], BF16)
        sgT = kb.dram("sgT", [16, 128, SEQ], BF16)
        vtm = kb.dram("vtm", [NTOK, 512], BF16)
        sz = kb.dram("sz", [SEQ, 2048], BF16)
        xs_tm = kb.dram("xs_tm", [NTOK, 2048], BF16)
        b_tm = kb.dram("b_tm", [NTOK, 512], BF16)
        BT = kb.dram("BT", [4, 128, NTOK], BF16)
        CT = kb.dram("CT", [4, 128, NTOK], BF16)
        dtp_d = kb.dram("dtp_d", [NTOK, 64], F32)
        dA_d = kb.dram("dA_d", [NTOK, 64], F32)
        yf_d = kb.dram("yf_d", [SEQ, 2048], F32)
        mixT = kb.dram("mixT", [16, 128, 32, 128], BF16)
        x1 = kb.dram("x1", [SEQ, D], F32)
        uT = kb.dram("uT", [32, 128, SEQ], BF16)
        szT = kb.dram("szT", [32, 128, SEQ], BF16)
        UCd = kb.dram("UCd", [2, 8, 128, 16, 512], BF16)
        mixT1 = kb.dram("mixT1", [16, 128, 32, 128], BF16)
        bd = {k: v[1] for k, v in kb.scr.items()}

        def load_w(dst, dbuf, wap, c0, n, nk=16):
            src = wap[:, c0:c0 + n].rearrange("(kc p) n -> p kc n", p=128)
            for k0 in range(0, nk, 4):
                S.dma("pool", dst[:, k0:k0 + 4, 0:n], src[:, k0:k0 + 4, :], w=[dbuf.p(k0)])

        ADA_W = (("ev_ada_w", "ev_ada_b"), ("od_ada_w", "od_ada_b"))

        def ada_setup(st, psum=None):
            A = {}
            A["cT"] = kb.sb(st, "cT", [128, 16, 64], F32)
            A["cTb"] = kb.sb(st, "cTb", [128, 16, 64], BF16)
            A["wblk"] = [kb.sb(st, "adaw%d" % i, [128, 16, 512], BF16) for i in range(2)]
            A["nwT"] = kb.sb(st, "nwT", [128, 2, 16], F32)
            A["biasF"] = kb.sb(st, "biasF", [128, 2, 48], F32)
            A["modF"] = kb.sb(st, "modF", [128, 2, 48, 2], F32)
            A["dg"] = [kb.sb(st, "dg%d" % i, [128, 128], F32) for i in range(2)]
            if psum is None:
                A["ps"] = [kb.ps(st, "psA%d" % i)[:] for i in range(2)]
                A["b_ps"] = [Buf(), Buf()]
            else:
                A["ps"] = [psum[0]]
                A["b_ps"] = [psum[1]]
            A["b_cT"], A["b_nw"], A["b_bias"], A["b_modF"] = Buf(), Buf(), Buf(), Buf()
            A["b_w"] = [WB(), WB()]
            A["b_dg"] = [Buf(), Buf()]
            A["wi"] = 0
            A["pi"] = 0
            cT, cTb = A["cT"], A["cTb"]
            S.dve(lambda e: e.memset(cT[:], 0.0), w=[A["b_cT"]])
            S.dma("sp", cT[:, :, 0:1], I["c"].rearrange("(c p one) -> p c one", p=128, one=1), w=[A["b_cT"]], slow=True)
            S.dma("sp", cT[:, :, 32:33], I["c_ctx"].rearrange("(c p one) -> p c one", p=128, one=1), w=[A["b_cT"]], slow=True)
            S.act(lambda e: e.activation(out=cTb[:], in_=cT[:], func=AF.Silu), r=[A["b_cT"]], w=[A["b_cT"]])
            S.dma("sp", A["nwT"][:, 0, :].unsqueeze(2), I["ev_norm_w"].rearrange("(c p one) -> p c one", p=128, one=1), w=[A["b_nw"]], slow=True)
            S.dma("sp", A["nwT"][:, 1, :].unsqueeze(2), I["od_norm_w"].rearrange("(c p one) -> p c one", p=128, one=1), w=[A["b_nw"]], slow=True)
            for layer in range(2):
                S.dma("act", A["biasF"][:, layer, :].unsqueeze(2), I[ADA_W[layer][1]].rearrange("(c p one) -> p c one", p=128, one=1),
                      w=[A["b_bias"]], slow=True)
            return A

        def ada_chunks(A, layer, c8s):
            wn = ADA_W[layer][0]
            cTb, modF, biasF = A["cTb"], A["modF"], A["biasF"]
            for c8 in c8s:
                pi = A["pi"]
                A["pi"] += 1
                pA, bA = A["ps"][pi % len(A["ps"])], A["b_ps"][pi % len(A["ps"])]
                for half in range(2):
                    wi = A["wi"]
                    A["wi"] += 1
                    w_t, w_b = A["wblk"][wi % 2], A["b_w"][wi % 2]
                    load_w(w_t, w_b, I[wn], (c8 * 2 + half) * 512, 512)
                    for q in range(4):
                        col = (half * 4 + q) * 64
                        for kc in range(16):
                            S.pe(lambda e, kc=kc, w_t=w_t, pA=pA, q=q, col=col: e.matmul(
                                pA[:, col:col + 64], lhsT=w_t[:, kc, q * 128:(q + 1) * 128], rhs=cTb[:, kc, :],
                                start=(kc == 0), stop=(kc == 15)), r=[A["b_cT"], w_b.p(kc)], w=[bA], c=0.1)
                pv = pA.rearrange("p (a c) -> p a c", c=64)
                S.dve(lambda e, pv=pv, c8=c8: e.tensor_tensor(
                    out=modF[:, layer, c8 * 8:(c8 + 1) * 8, :], in0=pv[:, :, 0:33:32],
                    in1=biasF[:, layer, c8 * 8:(c8 + 1) * 8].unsqueeze(2).broadcast_to([128, 8, 2]), op=ADD),
                      r=[bA, A["b_bias"]], w=[A["b_modF"]])

        def ada_mod(A, layer):
            modF, nwT = A["modF"], A["nwT"]
            for ri in range(2):
                S.dve(lambda e, ri=ri: e.tensor_copy(out=modT[:, layer, 2 * ri, :], in_=modF[:, layer, 0:16, ri]),
                      r=[A["b_modF"]], w=[b_modT])
                S.dve(lambda e, ri=ri: e.scalar_tensor_tensor(
                    out=modT[:, layer, 2 * ri + 1, :], in0=modF[:, layer, 16:32, ri], scalar=1.0, in1=nwT[:, layer, :],
                    op0=ADD, op1=MUL), r=[A["b_modF"], A["b_nw"]], w=[b_modT])

        def ada_gate(A, layer):
            modF = A["modF"]
            for gq in range(4):
                pi = A["pi"]
                A["pi"] += 1
                pG, bG = A["ps"][pi % len(A["ps"])], A["b_ps"][pi % len(A["ps"])]
                for q in range(4):
                    ch = gq * 4 + q
                    d2 = ch % 2
                    S.dve(lambda e, ch=ch, d2=d2: e.tensor_scalar(out=A["dg"][d2][:], in0=ident_f[:], scalar1=modF[:, layer, 32 + ch, 0:1],
                                                                  scalar2=None, op0=MUL),
                          r=[A["b_modF"], b_const], w=[A["b_dg"][d2]], c=0.25)
                    S.pe(lambda e, q=q, d2=d2, pG=pG: e.matmul(pG[:, q * 128:(q + 1) * 128], lhsT=ones_f[:], rhs=A["dg"][d2][:],
                                                              start=True, stop=True), r=[A["b_dg"][d2], b_const], w=[bG], c=0.3)
                S.dve(lambda e, gq=gq, pG=pG: e.tensor_copy(out=gate_bc[:, layer, gq * 512:(gq + 1) * 512], in_=pG),
                      r=[bG], w=[b_gate])

        cw = kb.sb(top, "cw", [128, 24, 8], F32)
        b_cw0 = Buf()
        for j in range(5):
            S.dma("act", cw[:, :, j:j + 1], I["ev_conv_w"][j, :].rearrange("(b p one) -> p b one", p=128, one=1),
                  w=[b_cw0], slow=True)
        S.dma("act", cw[:, :, 5:6], I["ev_conv_b"].rearrange("(b p one) -> p b one", p=128, one=1), w=[b_cw0], slow=True)

        if want("ada"):
            with contextlib.ExitStack() as st:
                A = ada_setup(st)
                ada_chunks(A, 0, range(0, 4))
                ada_mod(A, 0)
                S.flush()

        def norm_phase(layer, sources, st=None):
            own = st is None
            if own:
                st = contextlib.ExitStack()
            if True:
                xt = [kb.sb(st, "nx%d" % i, [128, 4, D], F32) for i in range(2)]
                junk = kb.sb(st, "njunk", [128, D], BF16)
                ssq = kb.sb(st, "nssq", [128, 16], F32)
                pst = [kb.ps(st, "npst%d" % i) for i in range(4)]
                b_xt = [Buf(), Buf()]
                b_junk, b_ssq = Buf(), Buf()
                b_pst = [Buf() for _ in range(4)]
                gi = 0
                pi = 0
                for (src, rb, tok0, ntile, hb, mb) in sources:
                    x_t, x_b = xt[gi % 2], b_xt[gi % 2]
                    gi += 1
                    for j in range(ntile):
                        S.dma(kb.q(), x_t[:, j, :], src[j * 128:(j + 1) * 128, :], r=rb, w=[x_b])
                    for j in range(ntile):
                        S.act(lambda e, j=j, x_t=x_t: e.activation(out=junk[:], in_=x_t[:, j, :], func=AF.Square,
                                                                    accum_out=ssq[:, j:j + 1]),
                              r=[x_b], w=[b_junk, b_ssq], c=2.0)
                    S.act(lambda e, ntile=ntile: e.activation(out=ssq[:, 8:8 + ntile], in_=ssq[:, 0:ntile], func=AF.Sqrt,
                                                              scale=1.0 / D, bias=eps_t[:, 0:1]),
                          r=[b_ssq, b_const], w=[b_ssq])
                    S.dve(lambda e, ntile=ntile: e.reciprocal(out=ssq[:, 4:4 + ntile], in_=ssq[:, 8:8 + ntile]),
                          r=[b_ssq], w=[b_ssq])
                    for j in range(ntile):
                        S.dve(lambda e, j=j, x_t=x_t: e.tensor_scalar(out=x_t[:, j, :], in0=x_t[:, j, :],
                                                                       scalar1=ssq[:, 4 + j:5 + j], scalar2=None, op0=MUL),
                              r=[x_b, b_ssq], w=[x_b], c=2.3)
                    for kc in range(16):
                        p_t, p_b = pst[pi % 4], b_pst[pi % 4]
                        pi += 1
                        for j in range(ntile):
                            S.pe(lambda e, j=j, kc=kc, x_t=x_t, p_t=p_t: e.transpose(
                                p_t[:, j * 128:(j + 1) * 128], x_t[:, j, kc * 128:(kc + 1) * 128], ident_f[:]),
                                 r=[x_b, b_const], w=[p_b], c=0.2)
                        eng = S.dve if kc % 2 == 0 else S.act
                        if kc % 2 == 0:
                            S.dve(lambda e, kc=kc, p_t=p_t, ntile=ntile, tok0=tok0, mb=mb: e.tensor_scalar(
                                out=hT[:, kc, tok0:tok0 + ntile * 128], in0=p_t[:, 0:ntile * 128],
                                scalar1=modT[:, layer, mb + 1, kc:kc + 1], scalar2=modT[:, layer, mb, kc:kc + 1],
                                op0=MUL, op1=ADD), r=[p_b, b_modT], w=[hb])
                        else:
                            S.act(lambda e, kc=kc, p_t=p_t, ntile=ntile, tok0=tok0, mb=mb: e.activation(
                                out=hT[:, kc, tok0:tok0 + ntile * 128], in_=p_t[:, 0:ntile * 128], func=AF.Identity,
                                scale=modT[:, layer, mb + 1, kc:kc + 1], bias=modT[:, layer, mb, kc:kc + 1]),
                                  r=[p_b, b_modT], w=[hb])
                if own:
                    S.flush()
                    st.close()

        eps_t = kb.sb(top, "eps_t", [128, 2], F32)
        S.dve(lambda e: e.memset(eps_t[:, 0:1], EPS), w=[b_const])
        S.dve(lambda e: e.memset(eps_t[:, 1:2], 1.0), w=[b_const])
        hctx = contextlib.ExitStack()
        hT = kb.sb(hctx, "hT", [128, 16, NTOK], BF16)

        if want("norm0"):
            srcs = [(I["ctx"], [], 0, 2, b_hT[0], 2)]
            for tb in range(4):
                srcs.append((I["x"][tb * 512:(tb + 1) * 512, :], [], CTX + tb * 512, 4, b_hT[1 + tb], 0))
            stn0 = contextlib.ExitStack()
            norm_phase(0, srcs, st=stn0)

        tblocks = [(0, CTX, b_hT[0], None)] + [(CTX + i * 512, 512, b_hT[1 + i], i * 512) for i in range(4)]

        if want("gv"):
            with contextlib.ExitStack() as st:
                wg = [kb.sb(st, "wg%d" % i, [128, 16, 512], BF16) for i in range(2)]
                b_wg = [WB(), WB()]
                og = [kb.sb(st, "og%d" % i, [128, 512], BF16) for i in range(3)]
                b_og = [Buf() for _ in range(3)]
                pacc = [kb.ps(st, "gacc%d" % i) for i in range(3)]
                b_pacc = [Buf() for _ in range(3)]
                dtb = kb.sb(st, "dtb", [128, 3, 64], F32)
                dto = kb.sb(st, "dto", [128, 2, 2, 64], F32)
                b_dtb = Buf()
                b_dto = [Buf(), Buf()]
                S.dma("sp", dtb[:, 0, :], I["ev_dt_bias"].partition_broadcast(128), w=[b_dtb])
                S.dma("sp", dtb[:, 1, :], I["ev_a_log"].partition_broadcast(128), w=[b_dtb])
                S.act(lambda e: e.activation(out=dtb[:, 1, :], in_=dtb[:, 1, :], func=AF.Exp), r=[b_dtb], w=[b_dtb])
                S.dve(lambda e: e.tensor_scalar(out=dtb[:, 1, :], in0=dtb[:, 1, :], scalar1=-1.0, scalar2=None, op0=MUL),
                      r=[b_dtb], w=[b_dtb])
                wi = 0
                it = 0
                for gb in range(4):
                    w_t, w_b = wg[wi % 2], b_wg[wi % 2]
                    wi += 1
                    load_w(w_t, w_b, I["ev_w_in"], C_G + gb * 512, 512)
                    for hh in range(4):
                        h = gb * 4 + hh
                        for (tok0, ntok, hbuf, lat) in tblocks[1:]:
                            i3 = it % 3
                            it += 1
                            for kc in range(16):
                                S.pe(lambda e, kc=kc, w_t=w_t, i3=i3, tok0=tok0, hh=hh: e.matmul(
                                    pacc[i3][:], lhsT=w_t[:, kc, hh * 128:(hh + 1) * 128], rhs=hT[:, kc, tok0:tok0 + 512],
                                    start=(kc == 0), stop=(kc == 15)), r=[w_b.p(kc), hbuf], w=[b_pacc[i3]])
                            S.act(lambda e, i3=i3: e.activation(out=og[i3][:], in_=pacc[i3][:], func=AF.Silu),
                                  r=[b_pacc[i3]], w=[b_og[i3]])
                            S.dma("sp", sgT[h, :, lat:lat + 512], og[i3][:], r=[b_og[i3]], w=[bd["sgT"]])
                for zb in range(4):
                    w_t, w_b = wg[wi % 2], b_wg[wi % 2]
                    wi += 1
                    load_w(w_t, w_b, I["ev_w_in"], C_Z + zb * 512, 512)
                    for tt in range(2, NT):
                        i3 = it % 3
                        it += 1
                        hbuf = b_hT[1 + (tt - 2) // 4]
                        for kc in range(16):
                            S.pe(lambda e, kc=kc, w_t=w_t, i3=i3, tt=tt: e.matmul(
                                pacc[i3][:], lhsT=hT[:, kc, tt * 128:(tt + 1) * 128], rhs=w_t[:, kc, :],
                                start=(kc == 0), stop=(kc == 15)), r=[w_b.p(kc), hbuf], w=[b_pacc[i3]])
                        S.act(lambda e, i3=i3: e.activation(out=og[i3][:], in_=pacc[i3][:], func=AF.Silu),
                              r=[b_pacc[i3]], w=[b_og[i3]])
                        S.dma("sp", sz[(tt - 2) * 128:(tt - 1) * 128, zb * 512:(zb + 1) * 512], og[i3][:], r=[b_og[i3]], w=[bd["sz"]])
                w_t, w_b = wg[wi % 2], b_wg[wi % 2]
                wi += 1
                load_w(w_t, w_b, I["ev_w_in"], C_V, 512)
                for tt in range(NT):
                    i3 = it % 3
                    it += 1
                    hbuf = b_hT[0] if tt < 2 else b_hT[1 + (tt - 2) // 4]
                    for kc in range(16):
                        S.pe(lambda e, kc=kc, w_t=w_t, i3=i3, tt=tt: e.matmul(
                            pacc[i3][:], lhsT=hT[:, kc, tt * 128:(tt + 1) * 128], rhs=w_t[:, kc, :],
                            start=(kc == 0), stop=(kc == 15)), r=[w_b.p(kc), hbuf], w=[b_pacc[i3]])
                    S.dve(lambda e, i3=i3: e.tensor_copy(out=og[i3][:], in_=pacc[i3][:]), r=[b_pacc[i3]], w=[b_og[i3]])
                    S.dma("sp", vtm[tt * 128:(tt + 1) * 128, :], og[i3][:], r=[b_og[i3]], w=[bd["vtm"]])
                w_t, w_b = wg[wi % 2], b_wg[wi % 2]
                wi += 1
                load_w(w_t, w_b, I["ev_w_in"], C_DT, 64)
                for tt in range(NT):
                    i3 = it % 3
                    it += 1
                    i2 = tt % 2
                    hbuf = b_hT[0] if tt < 2 else b_hT[1 + (tt - 2) // 4]
                    for kc in range(16):
                        S.pe(lambda e, kc=kc, w_t=w_t, i3=i3, tt=tt: e.matmul(
                            pacc[i3][:, 0:64], lhsT=hT[:, kc, tt * 128:(tt + 1) * 128], rhs=w_t[:, kc, 0:64],
                            start=(kc == 0), stop=(kc == 15)), r=[w_b.p(kc), hbuf], w=[b_pacc[i3]])
                    S.dve(lambda e, i3=i3, i2=i2: e.tensor_tensor(out=dto[:, i2, 0, :], in0=pacc[i3][:, 0:64], in1=dtb[:, 0, :], op=ADD),
                          r=[b_pacc[i3], b_dtb], w=[b_dto[i2]])
                    S.act(lambda e, i2=i2: e.activation(out=dto[:, i2, 0, :], in_=dto[:, i2, 0, :], func=AF.Exp), r=[b_dto[i2]], w=[b_dto[i2]])
                    S.act(lambda e, i2=i2: e.activation(out=dto[:, i2, 0, :], in_=dto[:, i2, 0, :], func=AF.Ln, bias=eps_t[:, 1:2]),
                          r=[b_dto[i2], b_const], w=[b_dto[i2]])
                    S.dve(lambda e, i2=i2: e.tensor_tensor(out=dto[:, i2, 1, :], in0=dto[:, i2, 0, :], in1=dtb[:, 1, :], op=MUL),
                          r=[b_dto[i2], b_dtb], w=[b_dto[i2]])
                    S.dma("sp", dtp_d[tt * 128:(tt + 1) * 128, :], dto[:, i2, 0, :], r=[b_dto[i2]], w=[bd["dtp_d"]])
                    S.dma("sp", dA_d[tt * 128:(tt + 1) * 128, :], dto[:, i2, 1, :], r=[b_dto[i2]], w=[bd["dA_d"]])
                S.flush()

        stn0.close()

        if want("qk"):
            with contextlib.ExitStack() as st:
                wq = [kb.sb(st, "wq%d" % i, [128, 16, 512], BF16) for i in range(2)]
                b_wq = [WB(), WB()]
                ropecs = kb.sb(st, "ropecs", [128, 2, SEQ], F32)
                RTt = kb.sb(st, "RTt", [128, 128], BF16)
                qkn = kb.sb(st, "qkn", [128, 2], F32)
                b_rc = Buf()
                S.dma("sp", ropecs[:], I["ropecs"], w=[b_rc])
                S.dma("sp", RTt[:], I["RT"], w=[b_rc])
                S.dma("sp", qkn[:, 0:1], I["ev_q_norm"].rearrange("(p one) -> p one", one=1), w=[b_rc])
                S.dma("sp", qkn[:, 1:2], I["ev_k_norm"].rearrange("(p one) -> p one", one=1), w=[b_rc])
                NB = 2
                sq = [kb.sb(st, "sq%d" % i, [128, 512], BF16) for i in range(NB)]
                rs = [kb.sb(st, "rs%d" % i, [128, 512], F32) for i in range(NB)]
                qn = [kb.sb(st, "qn%d" % i, [128, 512], BF16) for i in range(NB)]
                t1 = [kb.sb(st, "t1%d" % i, [128, 512], F32) for i in range(NB)]
                t2 = [kb.sb(st, "t2%d" % i, [128, 512], F32) for i in range(NB)]
                qo = [kb.sb(st, "qo%d" % i, [128, 512], BF16) for i in range(NB)]
                b_sq, b_rs, b_qn, b_t1, b_t2, b_qo = [[Buf() for _ in range(NB)] for _ in range(6)]
                pacc = [kb.ps(st, "qacc%d" % i) for i in range(2)]
                pss = [kb.ps(st, "qss%d" % i) for i in range(2)]
                prot = [kb.ps(st, "qrot%d" % i) for i in range(2)]
                b_pacc, b_pss, b_prot = [[Buf(), Buf()] for _ in range(3)]
                it = 0
                for hb in range(20):
                    isq = hb < 16
                    c0 = C_Q + hb * 128 if isq else C_K + (hb - 16) * 128
                    w_t, w_b = wq[(hb // 4) % 2], b_wq[(hb // 4) % 2]
                    wo = (hb % 4) * 128
                    if hb % 4 == 0:
                        load_w(w_t, w_b, I["ev_w_in"], c0, 512)
                    for (tok0, ntok, hbuf, lat) in tblocks:
                        if lat is None and isq:
                            continue
                        i2 = it % 2
                        it += 1
                        for kc in range(16):
                            S.pe(lambda e, kc=kc, w_t=w_t, i2=i2, tok0=tok0, ntok=ntok, wo=wo: e.matmul(
                                pacc[i2][:, 0:ntok], lhsT=w_t[:, kc, wo:wo + 128], rhs=hT[:, kc, tok0:tok0 + ntok],
                                start=(kc == 0), stop=(kc == 15)), r=[w_b.p(kc), hbuf], w=[b_pacc[i2]])
                        S.act(lambda e, i2=i2, ntok=ntok: e.activation(out=sq[i2][:, 0:ntok], in_=pacc[i2][:, 0:ntok], func=AF.Square),
                              r=[b_pacc[i2]], w=[b_sq[i2]])
                        S.pe(lambda e, i2=i2, ntok=ntok: e.matmul(pss[i2][:, 0:ntok], lhsT=ones_b[:], rhs=sq[i2][:, 0:ntok],
                                                                   start=True, stop=True), r=[b_sq[i2], b_const], w=[b_pss[i2]], c=0.25)
                        S.act(lambda e, i2=i2, ntok=ntok: e.activation(out=rs[i2][:, 0:ntok], in_=pss[i2][:, 0:ntok], func=AF.Ln,
                                                                        scale=1.0 / 128, bias=eps_t[:, 0:1]),
                              r=[b_pss[i2], b_const], w=[b_rs[i2]], c=0.6)
                        S.act(lambda e, i2=i2, ntok=ntok: e.activation(out=rs[i2][:, 0:ntok], in_=rs[i2][:, 0:ntok], func=AF.Exp, scale=-0.5),
                              r=[b_rs[i2]], w=[b_rs[i2]], c=0.6)
                        nsel = 0 if isq else 1
                        S.dve(lambda e, i2=i2, ntok=ntok, nsel=nsel: e.scalar_tensor_tensor(
                            out=qn[i2][:, 0:ntok], in0=pacc[i2][:, 0:ntok], scalar=qkn[:, nsel:nsel + 1], in1=rs[i2][:, 0:ntok],
                            op0=MUL, op1=MUL), r=[b_pacc[i2], b_rs[i2], b_rc], w=[b_qn[i2]])
                        if lat is None:
                            g = hb - 16
                            S.dma(kb.q(), kT[g, :, 0:CTX], qn[i2][:, 0:CTX], r=[b_qn[i2]], w=[bd["kT"]])
                            continue
                        S.pe(lambda e, i2=i2: e.matmul(prot[i2][:], lhsT=RTt[:], rhs=qn[i2][:], start=True, stop=True),
                             r=[b_qn[i2], b_rc], w=[b_prot[i2]])
                        S.pool(lambda e, i2=i2, lat=lat: e.tensor_tensor(out=t1[i2][:], in0=qn[i2][:], in1=ropecs[:, 0, lat:lat + 512], op=MUL),
                               r=[b_qn[i2], b_rc], w=[b_t1[i2]], c=2.0)
                        S.dve(lambda e, i2=i2, lat=lat: e.tensor_tensor(out=t2[i2][:], in0=prot[i2][:], in1=ropecs[:, 1, lat:lat + 512], op=MUL),
                              r=[b_prot[i2], b_rc], w=[b_t2[i2]])
                        S.pool(lambda e, i2=i2: e.tensor_tensor(out=qo[i2][:], in0=t1[i2][:], in1=t2[i2][:], op=ADD),
                               r=[b_t1[i2], b_t2[i2]], w=[b_qo[i2]], c=2.0)
                        if isq:
                            S.dma(kb.q(), qT[hb, :, lat:lat + 512], qo[i2][:], r=[b_qo[i2]], w=[bd["qT"]])
                        else:
                            S.dma(kb.q(), kT[hb - 16, :, CTX + lat:CTX + lat + 512], qo[i2][:], r=[b_qo[i2]], w=[bd["kT"]])
                S.flush()

        if want("xbc"):
            with contextlib.ExitStack() as st:
                wx = [kb.sb(st, "wx%d" % i, [128, 16, 512], BF16) for i in range(2)]
                b_wx = [WB(), WB()]
                b_cw = Buf()
                xpad = [kb.sb(st, "xpad%d" % i, [128, SEQ + 4], F32) for i in range(2)]
                cpad = [kb.sb(st, "cpad%d" % i, [128, CTX + 4], F32) for i in range(2)]
                acc = [kb.sb(st, "cacc%d" % i, [128, SEQ], F32) for i in range(2)]
                cacc = [kb.sb(st, "ccacc%d" % i, [128, CTX], F32) for i in range(2)]
                cvo = [kb.sb(st, "cvo%d" % i, [128, NTOK], BF16) for i in range(2)]
                tro = [kb.sb(st, "tro%d" % i, [128, 8, 128], BF16) for i in range(2)]
                b_xpad, b_cpad, b_acc, b_cacc, b_cvo, b_tro = [[Buf(), Buf()] for _ in range(6)]
                pacc = [kb.ps(st, "xacc%d" % i) for i in range(3)]
                ptr = [kb.ps(st, "xtr%d" % i, (128, 1024), BF16) for i in range(2)]
                b_pacc = [Buf() for _ in range(3)]
                b_ptr = [Buf(), Buf()]
                for i in range(2):
                    S.pool(lambda e, i=i: e.memset(xpad[i][:], 0.0), w=[b_xpad[i]])
                    S.pool(lambda e, i=i: e.memset(cpad[i][:], 0.0), w=[b_cpad[i]])
                it = 0
                ti = 0
                for cb in range(24):
                    w_t, w_b = wx[(cb // 4) % 2], b_wx[(cb // 4) % 2]
                    wo = (cb % 4) * 128
                    if cb % 4 == 0:
                        load_w(w_t, w_b, I["ev_w_in"], C_XBC + cb * 128, 512)
                    i2 = cb % 2
                    for (tok0, ntok, hbuf, lat) in tblocks:
                        i3 = it % 3
                        it += 1
                        for kc in range(16):
                            S.pe(lambda e, kc=kc, w_t=w_t, i3=i3, tok0=tok0, ntok=ntok, wo=wo: e.matmul(
                                pacc[i3][:, 0:ntok], lhsT=w_t[:, kc, wo:wo + 128], rhs=hT[:, kc, tok0:tok0 + ntok],
                                start=(kc == 0), stop=(kc == 15)), r=[w_b.p(kc), hbuf], w=[b_pacc[i3]])
                        if lat is None:
                            S.act(lambda e, i3=i3, i2=i2: e.activation(out=cpad[i2][:, 2:2 + CTX], in_=pacc[i3][:, 0:CTX], func=AF.Copy),
                                  r=[b_pacc[i3]], w=[b_cpad[i2]])
                        else:
                            S.act(lambda e, i3=i3, i2=i2, lat=lat: e.activation(out=xpad[i2][:, 2 + lat:2 + lat + 512], in_=pacc[i3][:],
                                                                                   func=AF.Copy), r=[b_pacc[i3]], w=[b_xpad[i2]])
                    for (pad, bp, ac, ba, n, o0) in ((xpad[i2], b_xpad[i2], acc[i2], b_acc[i2], SEQ, CTX),
                                                     (cpad[i2], b_cpad[i2], cacc[i2], b_cacc[i2], CTX, 0)):
                        S.act(lambda e, pad=pad, ac=ac, n=n, cb=cb: e.activation(out=ac[:, 0:n], in_=pad[:, 0:n], func=AF.Copy,
                                                                                 scale=cw[:, cb, 0:1]), r=[bp, b_cw], w=[ba], c=0.3 + n * 0.0009)
                        for j in range(1, 5):
                            S.dve(lambda e, pad=pad, ac=ac, n=n, cb=cb, j=j: e.scalar_tensor_tensor(
                                out=ac[:, 0:n], in0=pad[:, j:j + n], scalar=cw[:, cb, j:j + 1], in1=ac[:, 0:n], op0=MUL, op1=ADD),
                                  r=[bp, b_cw, ba], w=[ba], c=0.2 + n * 0.00105)
                        S.act(lambda e, ac=ac, n=n, cb=cb, o0=o0, i2=i2: e.activation(out=cvo[i2][:, o0:o0 + n], in_=ac[:, 0:n], func=AF.Silu,
                                                                                      bias=cw[:, cb, 5:6]), r=[ba, b_cw], w=[b_cvo[i2]], c=0.3 + n * 0.0009)
                    if cb >= 16:
                        g = (cb - 16) % 4
                        dst = BT if cb < 20 else CT
                        S.dma("sp", dst[g], cvo[i2][:], r=[b_cvo[i2]], w=[bd["BT"] if cb < 20 else bd["CT"]])
                    if cb < 20:
                        for (t0, nt) in ((0, 8), (8, 8), (16, 2)):
                            p2 = ti % 2
                            ti += 1
                            for i in range(nt):
                                S.pe(lambda e, i=i, t0=t0, p2=p2, i2=i2: e.transpose(
                                    ptr[p2][:, i * 128:(i + 1) * 128], cvo[i2][:, (t0 + i) * 128:(t0 + i + 1) * 128], ident_b[:]),
                                     r=[b_cvo[i2], b_const], w=[b_ptr[p2]], c=0.1)
                            S.dve(lambda e, p2=p2, nt=nt: e.tensor_copy(out=tro[p2][:, 0:nt, :],
                                                                        in_=ptr[p2][:, 0:nt * 128].rearrange("p (i c) -> p i c", c=128)),
                                  r=[b_ptr[p2]], w=[b_tro[p2]])
                            if cb < 16:
                                d_ap = xs_tm[t0 * 128:(t0 + nt) * 128, cb * 128:(cb + 1) * 128].rearrange("(i p) c -> p i c", p=128)
                                S.dma("sp", d_ap, tro[p2][:, 0:nt, :], r=[b_tro[p2]], w=[bd["xs_tm"]])
                            else:
                                g = cb - 16
                                d_ap = b_tm[t0 * 128:(t0 + nt) * 128, g * 128:(g + 1) * 128].rearrange("(i p) c -> p i c", p=128)
                                S.dma("sp", d_ap, tro[p2][:, 0:nt, :], r=[b_tro[p2]], w=[bd["b_tm"]])
                S.flush()
        hctx.close()

        if want("ssd"):
            with contextlib.ExitStack() as st:
                tri = kb.sb(st, "tri", [128, 4, 128], F32)
                dtp_all = kb.sb(st, "dtp_all", [128, NT, 64], F32)
                dA_all = kb.sb(st, "dA_all", [128, NT, 64], F32)
                dskip = kb.sb(st, "dskip", [128, 32], F32)
                ssdn = kb.sb(st, "ssdn", [128, 2048], F32)
                b_sc = Buf()
                S.dma("sp", tri[:], I["tri"], w=[b_sc])
                S.dma("sp", dtp_all[:], dtp_d.rearrange("(t p) c -> p t c", p=128), r=[bd["dtp_d"]], w=[b_sc])
                S.dma("sp", dA_all[:], dA_d.rearrange("(t p) c -> p t c", p=128), r=[bd["dA_d"]], w=[b_sc])
                S.dma("sp", dskip[:], I["ev_d_skip"].partition_broadcast(128), w=[b_sc])
                S.dma("sp", ssdn[:], I["ev_ssd_norm"].partition_broadcast(128), w=[b_sc])
                Dsk = kb.sb(st, "Dsk", [128, 32, 128], BF16)
                b_dsk = Buf()
                for h in range(32):
                    S.dve(lambda e, h=h: e.tensor_scalar(out=Dsk[:, h, :], in0=ident_b[:], scalar1=dskip[:, h:h + 1], scalar2=None, op0=MUL),
                          r=[b_sc, b_const], w=[b_dsk], c=0.2)
                H = kb.sb(st, "H", [128, 4, 512], F32)
                Hbf = kb.sb(st, "Hbf", [128, 4, 512], BF16)
                b_H = [Buf() for _ in range(4)]
                b_Hbf = [Buf() for _ in range(4)]
                xs_t = [kb.sb(st, "xs_t%d" % i, [128, 2048], BF16) for i in range(3)]
                btm_t = [kb.sb(st, "btm_t%d" % i, [128, 512], BF16) for i in range(3)]
                BTc = [kb.sb(st, "BTc%d" % i, [128, 4, 128], BF16) for i in range(3)]
                CTc = [kb.sb(st, "CTc%d" % i, [128, 4, 128], BF16) for i in range(3)]
                b_ld = [Buf(), Buf(), Buf()]
                X = [kb.sb(st, "X%d" % i, [128, 2048], BF16) for i in range(2)]
                Xd = [kb.sb(st, "Xd%d" % i, [128, 2048], BF16) for i in range(2)]
                b_X, b_Xd = [Buf(), Buf()], [Buf(), Buf()]
                sm = [kb.sb(st, "sm%d" % i, [128, 4, 32], F32) for i in range(2)]
                b_sm = [Buf(), Buf()]
                rhsA = [kb.sb(st, "rhsA%d" % i, [128, 8, 128], F32) for i in range(2)]
                expd = [kb.sb(st, "expd%d" % i, [128, 8, 128], BF16) for i in range(2)]
                MT = [kb.sb(st, "MT%d" % i, [128, 8, 128], BF16) for i in range(2)]
                CBm = [kb.sb(st, "CBm%d" % i, [128, 128], BF16) for i in range(2)]
                dteb = [kb.sb(st, "dteb%d" % i, [128, 32], BF16) for i in range(2)]
                tmpy = [kb.sb(st, "tmpy%d" % i, [128, 512], F32) for i in range(2)]
                b_rhsA, b_expd, b_MT, b_CBm, b_tmpy = [[Buf(), Buf()] for _ in range(5)]
                yt = [kb.sb(st, "yt%d" % i, [128, 2048], F32) for i in range(2)]
                b_yt = [Buf(), Buf()]
                yf_t = kb.sb(st, "yf_t", [128, 2048], F32)
                sz_t = kb.sb(st, "sz_t", [128, 2048], BF16)
                yn = kb.sb(st, "yn", [128, 2048], BF16)
                junk = kb.sb(st, "sjunk", [128, 512], BF16)
                gss = kb.sb(st, "gss", [128, 12], F32)
                trs = kb.sb(st, "trs", [128, 16, 128], BF16)
                b_yf, b_szt, b_tmpf, b_yn, b_junk, b_gss, b_trs = [Buf() for _ in range(7)]
                pcb = kb.ps(st, "pcb")
                pdiff = [kb.ps(st, "pdiff%d" % i) for i in range(2)]
                pyd = kb.ps(st, "pyd")
                pyo = kb.ps(st, "pyo")
                pst = kb.ps(st, "pst")
                psm = kb.ps(st, "psm")
                ptr = kb.ps(st, "sptr", (128, 1024), BF16)
                b_pcb, b_pyd, b_pyo, b_pst, b_psm, b_ptr = [Buf() for _ in range(6)]
                b_pdiff = [Buf(), Buf()]
                A = ada_setup(st, psum=(ptr[:].bitcast(F32), b_ptr))
                ada_chunks(A, 0, range(4, 6))
                ada_gate(A, 0)
                ada_chunks(A, 1, range(0, 6))
                ada_mod(A, 1)
                ada_gate(A, 1)
                li = 0
                gi = 0
                for d in range(2):
                    for g in range(4):
                        S.pool(lambda e, g=g: e.memset(H[:, g, :], 0.0), w=[b_H[g]])
                        S.pool(lambda e, g=g: e.memset(Hbf[:, g, :], 0.0), w=[b_Hbf[g]])
                    order = ([0, 1] + list(range(2, NT))) if d == 0 else ([1, 0] + list(range(NT - 1, 1, -1)))
                    Td = tri[:, d, :]
                    Wd = tri[:, 2 + d, :]
                    for tt in order:
                        is_lat = tt >= 2
                        l2 = li % 2
                        l3 = li % 3
                        li += 1
                        tok0 = tt * 128
                        S.dma("sp", xs_t[l3][:], xs_tm[tok0:tok0 + 128, :], r=[bd["xs_tm"]], w=[b_ld[l3]])
                        S.dma("sp", btm_t[l3][:], b_tm[tok0:tok0 + 128, :], r=[bd["b_tm"]], w=[b_ld[l3]])
                        S.dma("act", BTc[l3][:], BT[:, :, tok0:tok0 + 128].rearrange("g n t -> n g t"), r=[bd["BT"]], w=[b_ld[l3]])
                        S.dma("act", CTc[l3][:], CT[:, :, tok0:tok0 + 128].rearrange("g n t -> n g t"), r=[bd["CT"]], w=[b_ld[l3]])
                        if d == 1 and is_lat:
                            S.dma("sp", yf_t[:], yf_d[(tt - 2) * 128:(tt - 1) * 128, :], r=[bd["yf_d"]], w=[b_yf])
                        a_ap = dA_all[:, tt, d * 32:(d + 1) * 32]
                        dtp_ap = dtp_all[:, tt, d * 32:(d + 1) * 32]
                        S.pe(lambda e, a_ap=a_ap, Td=Td: e.matmul(psm[:, 0:32], lhsT=Td, rhs=a_ap, start=True, stop=True),
                             r=[b_sc], w=[b_psm], c=0.15)
                        S.pe(lambda e, a_ap=a_ap: e.matmul(psm[:, 32:64], lhsT=ones_f[:], rhs=a_ap, start=True, stop=True),
                             r=[b_sc, b_const], w=[b_psm], c=0.15)
                        smt, bsm = sm[l2], b_sm[l2]
                        S.dve(lambda e, smt=smt: e.tensor_copy(out=smt[:, 0, :], in_=psm[:, 0:32]), r=[b_psm], w=[bsm])
                        S.act(lambda e, smt=smt: e.activation(out=smt[:, 1, :], in_=psm[:, 0:32], func=AF.Exp), r=[b_psm], w=[bsm])
                        S.act(lambda e, smt=smt: e.activation(out=smt[:, 3, :], in_=psm[:, 32:64], func=AF.Exp), r=[b_psm], w=[bsm])
                        S.dve(lambda e, smt=smt: e.tensor_tensor(out=smt[:, 2, :], in0=psm[:, 32:64], in1=smt[:, 0, :], op=SUB),
                              r=[b_psm, bsm], w=[bsm])
                        S.act(lambda e, smt=smt: e.activation(out=smt[:, 2, :], in_=smt[:, 2, :], func=AF.Exp), r=[bsm], w=[bsm])
                        Xv = X[l2][:].rearrange("p (h c) -> p h c", c=64)
                        Xdv = Xd[l2][:].rearrange("p (h c) -> p h c", c=64)
                        xsv = xs_t[l3][:].rearrange("p (h c) -> p h c", c=64)
                        S.pool(lambda e, Xv=Xv, xsv=xsv, dtp_ap=dtp_ap: e.tensor_tensor(
                            out=Xv, in0=xsv, in1=dtp_ap.unsqueeze(2).broadcast_to([128, 32, 64]), op=MUL),
                               r=[b_ld[l3], b_sc], w=[b_X[l2]], c=6.0)
                        S.pool(lambda e, Xv=Xv, Xdv=Xdv, smt=smt: e.tensor_tensor(
                            out=Xdv, in0=Xv, in1=smt[:, 2, :].unsqueeze(2).broadcast_to([128, 32, 64]), op=MUL),
                               r=[b_X[l2], bsm], w=[b_Xd[l2]], c=6.0)
                        y_t, y_b = yt[l2], b_yt[l2]
                        for g in range(4):
                            g2 = gi % 2
                            gi += 1
                            if is_lat:
                                S.pe(lambda e, g=g, l2=l2, l3=l3: e.matmul(pcb[:, 0:128], lhsT=BTc[l3][:, g, :], rhs=CTc[l3][:, g, :],
                                                                    start=True, stop=True), r=[b_ld[l3]], w=[b_pcb], c=0.1)
                                S.dve(lambda e, g2=g2, Td=Td: e.tensor_tensor(out=CBm[g2][:], in0=pcb[:, 0:128], in1=Td, op=MUL),
                                      r=[b_pcb, b_sc], w=[b_CBm[g2]], c=0.25)
                                for h in range(8):
                                    S.act(lambda e, g=g, g2=g2, h=h, a_ap=a_ap, Td=Td: e.activation(
                                        out=rhsA[g2][:, h, :], in_=Td, func=AF.Copy, scale=a_ap[:, g * 8 + h:g * 8 + h + 1]),
                                          r=[b_sc], w=[b_rhsA[g2]], c=0.25)
                                for hf in range(2):
                                    S.pe(lambda e, hf=hf, g2=g2, Wd=Wd: e.matmul(
                                        pdiff[hf][:], lhsT=Wd, rhs=rhsA[g2][:, hf * 4:(hf + 1) * 4, :].rearrange("p h l -> p (h l)"),
                                        start=True, stop=True), r=[b_rhsA[g2], b_sc], w=[b_pdiff[hf]], c=0.9)
                                    S.act(lambda e, hf=hf, g2=g2: e.activation(
                                        out=expd[g2][:, hf * 4:(hf + 1) * 4, :].rearrange("p h l -> p (h l)"), in_=pdiff[hf][:], func=AF.Exp),
                                          r=[b_pdiff[hf]], w=[b_expd[g2]])
                                S.dve(lambda e, g2=g2: e.tensor_tensor(out=MT[g2][:], in0=expd[g2][:],
                                                                       in1=CBm[g2][:].unsqueeze(1).broadcast_to([128, 8, 128]), op=MUL),
                                      r=[b_expd[g2], b_CBm[g2]], w=[b_MT[g2]], c=0.7)
                                if d == 1:
                                    S.pe(lambda e, g=g: e.matmul(pyd[:], lhsT=ident_f[:], rhs=yf_t[:, g * 512:(g + 1) * 512],
                                                                 start=True, stop=False), r=[b_yf, b_const], w=[b_pyd], c=0.9)
                                for h in range(8):
                                    c0 = (g * 8 + h) * 64
                                    S.pe(lambda e, h=h, g2=g2, l2=l2, l3=l3, c0=c0, d=d: e.matmul(
                                        pyd[:, h * 64:(h + 1) * 64], lhsT=MT[g2][:, h, :], rhs=X[l2][:, c0:c0 + 64],
                                        start=(d == 0), stop=(d == 1 and h == 7)),
                                         r=[b_MT[g2], b_X[l2]], w=[b_pyd], c=0.1)
                                    if d == 0:
                                        S.pe(lambda e, h=h, g=g, l3=l3, c0=c0: e.matmul(
                                            pyd[:, h * 64:(h + 1) * 64], lhsT=Dsk[:, g * 8 + h, :], rhs=xs_t[l3][:, c0:c0 + 64],
                                            start=False, stop=True), r=[b_dsk, b_ld[l3]], w=[b_pyd], c=0.1)
                                S.pe(lambda e, g=g, l2=l2, l3=l3: e.matmul(pyo[:], lhsT=CTc[l3][:, g, :], rhs=Hbf[:, g, :], start=True, stop=True),
                                     r=[b_ld[l3], b_Hbf[g]], w=[b_pyo])
                                S.dve(lambda e, g=g, g2=g2, smt=smt: e.tensor_tensor(
                                    out=tmpy[g2][:].rearrange("p (h c) -> p h c", c=64), in0=pyo[:].rearrange("p (h c) -> p h c", c=64),
                                    in1=smt[:, 1, g * 8:(g + 1) * 8].unsqueeze(2).broadcast_to([128, 8, 64]), op=MUL),
                                      r=[b_pyo, bsm], w=[b_tmpy[g2]])
                                S.dve(lambda e, g=g, g2=g2, y_t=y_t: e.tensor_tensor(out=y_t[:, g * 512:(g + 1) * 512], in0=pyd[:],
                                                                                    in1=tmpy[g2][:], op=ADD),
                                      r=[b_pyd, b_tmpy[g2]], w=[y_b])
                            S.pe(lambda e, g=g, l2=l2, l3=l3: e.matmul(pst[:], lhsT=btm_t[l3][:, g * 128:(g + 1) * 128],
                                                                rhs=Xd[l2][:, g * 512:(g + 1) * 512], start=True, stop=True),
                                 r=[b_ld[l3], b_Xd[l2]], w=[b_pst])
                            S.pool(lambda e, g=g, smt=smt: e.tensor_tensor(
                                out=H[:, g, :].rearrange("p (h c) -> p h c", c=64), in0=H[:, g, :].rearrange("p (h c) -> p h c", c=64),
                                in1=smt[:, 3, g * 8:(g + 1) * 8].unsqueeze(2).broadcast_to([128, 8, 64]), op=MUL),
                                   r=[b_H[g], bsm], w=[b_H[g]], c=2.0)
                            S.dve(lambda e, g=g: e.tensor_tensor(out=H[:, g, :], in0=H[:, g, :], in1=pst[:], op=ADD),
                                  r=[b_H[g], b_pst], w=[b_H[g]])
                            S.act(lambda e, g=g: e.activation(out=Hbf[:, g, :], in_=H[:, g, :], func=AF.Copy), r=[b_H[g]], w=[b_Hbf[g]])
                        if not is_lat:
                            continue
                        lt = tt - 2
                        if d == 0:
                            S.dma("sp", yf_d[lt * 128:(lt + 1) * 128, :], y_t[:], r=[y_b], w=[bd["yf_d"]])
                            continue
                        S.dma("act", sz_t[:], sz[lt * 128:(lt + 1) * 128, :], r=[bd["sz"]], w=[b_szt])
                        S.dve(lambda e, y_t=y_t: e.tensor_tensor(out=y_t[:], in0=y_t[:], in1=sz_t[:], op=MUL), r=[y_b, b_szt], w=[y_b], c=2.3)
                        for g in range(4):
                            S.act(lambda e, g=g, y_t=y_t: e.activation(out=junk[:], in_=y_t[:, g * 512:(g + 1) * 512], func=AF.Square,
                                                                        accum_out=gss[:, g:g + 1]), r=[y_b], w=[b_junk, b_gss])
                        S.act(lambda e: e.activation(out=gss[:, 4:8], in_=gss[:, 0:4], func=AF.Sqrt, scale=1.0 / 512, bias=eps_t[:, 0:1]),
                              r=[b_gss, b_const], w=[b_gss])
                        S.dve(lambda e: e.reciprocal(out=gss[:, 8:12], in_=gss[:, 4:8]), r=[b_gss], w=[b_gss])
                        for g in range(4):
                            S.dve(lambda e, g=g, y_t=y_t: e.scalar_tensor_tensor(
                                out=yn[:, g * 512:(g + 1) * 512], in0=y_t[:, g * 512:(g + 1) * 512], scalar=gss[:, 8 + g:9 + g],
                                in1=ssdn[:, g * 512:(g + 1) * 512], op0=MUL, op1=MUL), r=[y_b, b_gss, b_sc], w=[b_yn])
                        for half in range(2):
                            for i in range(8):
                                cbi = half * 8 + i
                                S.pe(lambda e, i=i, cbi=cbi: e.transpose(ptr[:, i * 128:(i + 1) * 128], yn[:, cbi * 128:(cbi + 1) * 128], ident_b[:]),
                                     r=[b_yn, b_const], w=[b_ptr], c=0.1)
                            S.act(lambda e, half=half: e.activation(out=trs[:, half * 8:(half + 1) * 8, :].rearrange("p i c -> p (i c)"),
                                                                    in_=ptr[:], func=AF.Copy), r=[b_ptr], w=[b_trs])
                        S.dma("sp", mixT[lt, :, 16:32, :], trs[:], r=[b_trs], w=[bd["mixT"]])
                S.flush()

        if want("att"):
            with contextlib.ExitStack() as st:
                kTg = [kb.sb(st, "kTg%d" % i, [128, NTOK], BF16) for i in range(2)]
                vg = [kb.sb(st, "vg%d" % i, [128, NT, 128], BF16) for i in range(2)]
                qh = [kb.sb(st, "qh%d" % i, [128, SEQ], BF16) for i in range(2)]
                sgh = [kb.sb(st, "sgh%d" % i, [128, SEQ], BF16) for i in range(2)]
                b_kv, b_qh = [Buf(), Buf()], [Buf(), Buf()]
                PT = [kb.sb(st, "PT%d" % i, [128, 512], BF16) for i in range(6)]
                b_PT = [Buf() for _ in range(6)]
                negone = kb.sb(st, "negone", [128, 512], F32)
                b_negone = Buf()
                S.pool(lambda e: e.memset(negone[:], -1.0), w=[b_negone])
                PS = [kb.sb(st, "PS%d" % i, [128, 512], BF16) for i in range(3)]
                b_PS = [Buf() for _ in range(3)]
                gsi = 0
                rden = [kb.sb(st, "rden%d" % i, [128, 512], F32) for i in range(2)]
                ao = [kb.sb(st, "ao%d" % i, [128, 512], F32) for i in range(2)]
                aob = [kb.sb(st, "aob%d" % i, [128, 512], BF16) for i in range(2)]
                b_rden, b_ao, b_aob = [[Buf(), Buf()] for _ in range(3)]
                pS = [kb.ps(st, "pS%d" % i) for i in range(3)]
                pO = [kb.ps(st, "pO%d" % i) for i in range(2)]
                pD = [kb.ps(st, "pD%d" % i) for i in range(2)]
                b_pS = [Buf() for _ in range(3)]
                b_pO, b_pD = [Buf(), Buf()], [Buf(), Buf()]

                si = 0
                oi = 0
                sc = 1.0 / math.sqrt(128.0)
                for g in range(4):
                    kv2 = g % 2
                    S.dma("sp", kTg[kv2][:], kT[g], r=[bd["kT"]], w=[b_kv[kv2]])
                    S.dma("act", vg[kv2][:], vtm[:, g * 128:(g + 1) * 128].rearrange("(t p) c -> p t c", p=128), r=[bd["vtm"]], w=[b_kv[kv2]])
                    for hh in range(4):
                        h = g * 4 + hh
                        q2 = h % 2
                        S.dma("sp", qh[q2][:], qT[h], r=[bd["qT"]], w=[b_qh[q2]])
                        S.dma("act", sgh[q2][:], sgT[h], r=[bd["sgT"]], w=[b_qh[q2]])
                        for qb in range(4):
                            o2 = oi % 2
                            oi += 1
                            for kt in range(NT):
                                s3 = si % 3
                                p6 = si % 6
                                si += 1
                                S.pe(lambda e, kt=kt, s3=s3, kv2=kv2, q2=q2, qb=qb: e.matmul(
                                    pS[s3][:], lhsT=kTg[kv2][:, kt * 128:(kt + 1) * 128], rhs=qh[q2][:, qb * 512:(qb + 1) * 512],
                                    start=True, stop=True), r=[b_kv[kv2], b_qh[q2]], w=[b_pS[s3]])
                                S.act(lambda e, s3=s3, p6=p6: e.activation(out=PT[p6][:], in_=pS[s3][:], func=AF.Exp, scale=sc),
                                      r=[b_pS[s3]], w=[b_PT[p6]], c=0.6)
                                S.pe(lambda e, kt=kt, p6=p6, kv2=kv2, o2=o2: e.matmul(
                                    pO[o2][:], lhsT=vg[kv2][:, kt, :], rhs=PT[p6][:], start=(kt == 0), stop=(kt == NT - 1)),
                                     r=[b_kv[kv2], b_PT[p6]], w=[b_pO[o2]])
                                if kt % 3 == 2:
                                    pa, pb, pc = (p6 - 2) % 6, (p6 - 1) % 6, p6
                                    g3 = gsi % 3
                                    gsi += 1
                                    S.dve(lambda e, pa=pa, pb=pb, g3=g3: e.tensor_tensor(out=PS[g3][:], in0=PT[pa][:], in1=PT[pb][:], op=ADD),
                                          r=[b_PT[pa], b_PT[pb]], w=[b_PS[g3]], c=0.45)
                                    S.dve(lambda e, pc=pc, g3=g3: e.tensor_tensor(out=PS[g3][:], in0=PS[g3][:], in1=PT[pc][:], op=ADD),
                                          r=[b_PS[g3], b_PT[pc]], w=[b_PS[g3]], c=0.45)
                                    S.pe(lambda e, kt=kt, g3=g3, o2=o2: e.matmul(
                                        pD[o2][:], lhsT=ones_b[:], rhs=PS[g3][:], start=(kt == 2), stop=(kt == NT - 1)),
                                         r=[b_const, b_PS[g3]], w=[b_pD[o2]])
                            S.act(lambda e, o2=o2: e.activation(out=rden[o2][:], in_=pD[o2][:], func=AF.Ln), r=[b_pD[o2]], w=[b_rden[o2]], c=0.6)
                            S.act(lambda e, o2=o2: e.activation(out=rden[o2][:], in_=rden[o2][:], func=AF.Exp, scale=-1.0),
                                  r=[b_rden[o2]], w=[b_rden[o2]], c=0.6)
                            S.dve(lambda e, o2=o2: e.tensor_tensor(out=ao[o2][:], in0=pO[o2][:], in1=rden[o2][:], op=MUL),
                                  r=[b_pO[o2], b_rden[o2]], w=[b_ao[o2]])
                            S.pool(lambda e, o2=o2, q2=q2, qb=qb: e.tensor_tensor(out=aob[o2][:], in0=ao[o2][:],
                                                                                   in1=sgh[q2][:, qb * 512:(qb + 1) * 512], op=MUL),
                                   r=[b_ao[o2], b_qh[q2]], w=[b_aob[o2]], c=2.0)
                            S.dma("sp", mixT[qb * 4:(qb + 1) * 4, :, h, :].rearrange("t p c -> p t c"),
                                  aob[o2][:].rearrange("p (t c) -> p t c", c=128), r=[b_aob[o2]], w=[bd["mixT"]])
                S.flush()

        def out_phase(layer, mix_d, mix_b, w_ap, xsrc, xsrc_b, dst, dst_b):
            with contextlib.ExitStack() as st:
                wo_t = [kb.sb(st, "wo%d" % i, [128, 32, 512], BF16) for i in range(2)]
                b_wo = [WB(), WB()]
                mx = [kb.sb(st, "mx%d" % i, [128, 32, 128], BF16) for i in range(2)]
                xr = [kb.sb(st, "xr%d" % i, [128, 512], F32) for i in range(2)]
                oo = [kb.sb(st, "oo%d" % i, [128, 512], F32) for i in range(2)]
                b_mx, b_xr, b_oo = [[Buf(), Buf()] for _ in range(3)]
                pacc = [kb.ps(st, "oacc%d" % i) for i in range(2)]
                b_pacc = [Buf(), Buf()]
                it = 0
                for cbk in range(4):
                    w_t, w_b = wo_t[cbk % 2], b_wo[cbk % 2]
                    src = w_ap[:, cbk * 512:(cbk + 1) * 512].rearrange("(kc p) n -> p kc n", p=128)
                    for k0 in range(0, 32, 4):
                        S.dma("pool", w_t[:, k0:k0 + 4, :], src[:, k0:k0 + 4, :], w=[w_b.p(k0)])
                    for tt in range(16):
                        i2 = it % 2
                        it += 1
                        S.dma("sp", mx[i2][:], mix_d[tt], r=[mix_b], w=[b_mx[i2]])
                        S.dma("act", xr[i2][:], xsrc[tt * 128:(tt + 1) * 128, cbk * 512:(cbk + 1) * 512], r=xsrc_b, w=[b_xr[i2]])
                        for kc in range(32):
                            S.pe(lambda e, kc=kc, i2=i2, w_t=w_t: e.matmul(pacc[i2][:], lhsT=mx[i2][:, kc, :], rhs=w_t[:, kc, :],
                                                                            start=(kc == 0), stop=(kc == 31)),
                                 r=[b_mx[i2], w_b.p(kc)], w=[b_pacc[i2]])
                        S.dve(lambda e, i2=i2, cbk=cbk: e.tensor_tensor(out=oo[i2][:], in0=pacc[i2][:],
                                                                        in1=gate_bc[:, layer, cbk * 512:(cbk + 1) * 512], op=MUL),
                              r=[b_pacc[i2], b_gate], w=[b_oo[i2]])
                        S.pool(lambda e, i2=i2: e.tensor_tensor(out=oo[i2][:], in0=oo[i2][:], in1=xr[i2][:], op=ADD),
                               r=[b_oo[i2], b_xr[i2]], w=[b_oo[i2]], c=2.0)
                        S.dma("sp", dst[tt * 128:(tt + 1) * 128, cbk * 512:(cbk + 1) * 512], oo[i2][:], r=[b_oo[i2]], w=[dst_b])
                S.flush()

        if want("out0"):
            out_phase(0, mixT, bd["mixT"], I["ev_w_out"], I["x"], [], x1, bd["x1"])

        if want("norm1"):
            hctx = contextlib.ExitStack()
            hT = kb.sb(hctx, "hT1", [128, 16, NTOK], BF16)
            srcs = []
            for tb in range(4):
                srcs.append((x1[tb * 512:(tb + 1) * 512, :], [bd["x1"]], CTX + tb * 512, 4, b_hT[1 + tb], 0))
            stn1 = contextlib.ExitStack()
            norm_phase(1, srcs, st=stn1)

        if want("proj1"):
            with contextlib.ExitStack() as st:
                wg = [kb.sb(st, "w1_%d" % i, [128, 16, 512], BF16) for i in range(2)]
                b_wg = [WB(), WB()]
                og = [kb.sb(st, "o1_%d" % i, [128, 512], BF16) for i in range(3)]
                b_og = [Buf() for _ in range(3)]
                pacc = [kb.ps(st, "p1acc%d" % i) for i in range(3)]
                b_pacc = [Buf() for _ in range(3)]
                it = 0
                for blk in range(16):
                    w_t, w_b = wg[blk % 2], b_wg[blk % 2]
                    load_w(w_t, w_b, I["od_w_in"], blk * 512, 512)
                    isu = blk < 8
                    for hh in range(4):
                        j = (blk % 8) * 4 + hh
                        for (tok0, ntok, hbuf, lat) in tblocks[1:]:
                            i3 = it % 3
                            it += 1
                            for kc in range(16):
                                S.pe(lambda e, kc=kc, w_t=w_t, i3=i3, tok0=tok0, hh=hh: e.matmul(
                                    pacc[i3][:], lhsT=w_t[:, kc, hh * 128:(hh + 1) * 128], rhs=hT[:, kc, tok0:tok0 + 512],
                                    start=(kc == 0), stop=(kc == 15)), r=[w_b.p(kc), hbuf], w=[b_pacc[i3]])
                            if isu:
                                S.dve(lambda e, i3=i3: e.tensor_copy(out=og[i3][:], in_=pacc[i3][:]), r=[b_pacc[i3]], w=[b_og[i3]])
                                S.dma("sp", uT[j, :, lat:lat + 512], og[i3][:], r=[b_og[i3]], w=[bd["uT"]])
                            else:
                                S.act(lambda e, i3=i3: e.activation(out=og[i3][:], in_=pacc[i3][:], func=AF.Silu),
                                      r=[b_pacc[i3]], w=[b_og[i3]])
                                S.dma("sp", szT[j, :, lat:lat + 512], og[i3][:], r=[b_og[i3]], w=[bd["szT"]])
                S.flush()
            stn1.close()
            hctx.close()

        dctx = contextlib.ExitStack()
        if want("dft1"):
            dftl = kb.sb(dctx, "dftl", [128, 2, 2, 8, 1024], BF16)
            b_dl0 = Buf()
            with contextlib.ExitStack() as st:
                dftc = kb.sb(st, "dftc", [128, 2, 4, 512], BF16)
                b_dc = Buf()
                S.dma("sp", dftc[:], I["dftc"], w=[b_dc])
                ug = [kb.sb(st, "ug%d" % i, [128, 4, SEQ], BF16) for i in range(2)]
                b_ug = [Buf(), Buf()]
                uo = [[kb.sb(st, "uo%d_%d" % (cs, i), [128, 16, 512], BF16) for i in range(2)] for cs in range(2)]
                b_uo = [[Buf(), Buf()], [Buf(), Buf()]]
                pacc = [kb.ps(st, "dacc%d" % i) for i in range(3)]
                b_pacc = [Buf() for _ in range(3)]
                it = 0
                for g in range(8):
                    g2 = g % 2
                    S.dma("sp", ug[g2][:, 0:2, :], uT[g * 4:g * 4 + 2].rearrange("j p t -> p j t"), r=[bd["uT"]], w=[b_ug[g2]])
                    S.dma("act", ug[g2][:, 2:4, :], uT[g * 4 + 2:g * 4 + 4].rearrange("j p t -> p j t"), r=[bd["uT"]], w=[b_ug[g2]])
                    if g < 4:
                        S.dma("sp" if g % 2 == 0 else "act", dftl[:, g // 2, g % 2, :, :], I["dftl"][:, g // 2, g % 2, :, :],
                              w=[b_dl0], lat=20.0)
                    for tt in range(16):
                        t0 = (tt // 8) + 256 * (tt % 8)
                        for cs in range(2):
                            i3 = it % 3
                            it += 1
                            for cc in range(4):
                                S.pe(lambda e, cc=cc, cs=cs, t0=t0, g2=g2, i3=i3: e.matmul(
                                    pacc[i3][:], lhsT=ug[g2][:, cc, t0:t0 + 255:2], rhs=dftc[:, cs, cc, :],
                                    start=(cc == 0), stop=(cc == 3)), r=[b_ug[g2], b_dc], w=[b_pacc[i3]])
                            if cs == 0:
                                S.dve(lambda e, i3=i3, tt=tt, g2=g2: e.tensor_copy(out=uo[0][g2][:, tt, :], in_=pacc[i3][:]),
                                      r=[b_pacc[i3]], w=[b_uo[0][g2]])
                            else:
                                S.act(lambda e, i3=i3, tt=tt, g2=g2: e.activation(out=uo[1][g2][:, tt, :], in_=pacc[i3][:], func=AF.Copy),
                                      r=[b_pacc[i3]], w=[b_uo[1][g2]])
                    for cs in range(2):
                        S.dma("sp", UCd[cs, g], uo[cs][g2][:], r=[b_uo[cs][g2]], w=[bd["UCd"]], lat=12.0)
                S.flush()

        if want("dft2"):
            with contextlib.ExitStack() as st:
                b_dl = [[Buf(), Buf()], [Buf(), Buf()]]
                ucs = [kb.sb(st, "ucs%d" % i, [128, 2, 16, 512], BF16) for i in range(2)]
                szj = [kb.sb(st, "szj%d" % i, [128, SEQ], BF16) for i in range(2)]
                b_ucs, b_szj = [Buf(), Buf()], [Buf(), Buf()]
                osb = [kb.sb(st, "osb%d" % i, [128, 512], F32) for i in range(2)]
                fa = [kb.sb(st, "fa%d" % i, [128, 2, 512], F32) for i in range(2)]
                mo = [kb.sb(st, "mo%d" % i, [128, 2, 512], BF16) for i in range(2)]
                b_osb, b_fa, b_mo = [[Buf(), Buf()] for _ in range(3)]
                pE = [kb.ps(st, "pE%d" % i) for i in range(2)]
                pO = [kb.ps(st, "pOd%d" % i) for i in range(2)]
                b_pE, b_pO = [Buf(), Buf()], [Buf(), Buf()]
                it = 0
                for j in range(32):
                    j2 = j % 2
                    u2 = (j // 4) % 2
                    jl = j % 4
                    if jl == 0:
                        S.dma("sp", ucs[u2][:, 0, :, :], UCd[0, j // 4], r=[bd["UCd"]], w=[b_ucs[u2]])
                        S.dma("act", ucs[u2][:, 1, :, :], UCd[1, j // 4], r=[bd["UCd"]], w=[b_ucs[u2]])
                    S.dma("sp", szj[j2][:], szT[j], r=[bd["szT"]], w=[b_szj[j2]])
                    for kbk in range(2):
                        i2 = it % 2
                        it += 1
                        for par, (pp, bpp) in enumerate(((pE[i2], b_pE[i2]), (pO[i2], b_pO[i2]))):
                            n = 0
                            for tt8 in range(8):
                                for cs in range(2):
                                    S.pe(lambda e, tt8=tt8, cs=cs, u2=u2, jl=jl, pp=pp, kbk=kbk, n=n, par=par: e.matmul(
                                        pp[:], lhsT=ucs[u2][:, cs, par * 8 + tt8, jl * 128:(jl + 1) * 128],
                                        rhs=dftl[:, par, cs, tt8, kbk * 512:(kbk + 1) * 512],
                                        start=(n == 0), stop=(n == 15)), r=[b_ucs[u2], b_dl[par][cs]], w=[bpp])
                                    n += 1
                        S.act(lambda e, i2=i2: e.activation(out=osb[i2][:], in_=pO[i2][:], func=AF.Copy), r=[b_pO[i2]], w=[b_osb[i2]], c=0.6)
                        S.dve(lambda e, i2=i2: e.tensor_tensor(out=fa[i2][:, 0, :], in0=pE[i2][:], in1=osb[i2][:], op=ADD),
                              r=[b_pE[i2], b_osb[i2]], w=[b_fa[i2]])
                        S.dve(lambda e, i2=i2: e.tensor_tensor(out=fa[i2][:, 1, :], in0=pE[i2][:], in1=osb[i2][:], op=SUB),
                              r=[b_pE[i2], b_osb[i2]], w=[b_fa[i2]])
                        k0 = kbk * 512
                        S.pool(lambda e, i2=i2, j2=j2, k0=k0: e.tensor_tensor(out=mo[i2][:, 0, :], in0=fa[i2][:, 0, :],
                                                                              in1=szj[j2][:, k0:k0 + 512], op=MUL),
                               r=[b_fa[i2], b_szj[j2]], w=[b_mo[i2]], c=2.0)
                        S.dve(lambda e, i2=i2, j2=j2, k0=k0: e.tensor_tensor(out=mo[i2][:, 1, :], in0=fa[i2][:, 1, :],
                                                                             in1=szj[j2][:, 1024 + k0:1024 + k0 + 512], op=MUL),
                              r=[b_fa[i2], b_szj[j2]], w=[b_mo[i2]])
                        for hf in range(2):
                            tt0 = hf * 8 + kbk * 4
                            S.dma("sp", mixT1[tt0:tt0 + 4, :, j, :].rearrange("t p c -> p t c"),
                                  mo[i2][:, hf, :].rearrange("p (t c) -> p t c", c=128), r=[b_mo[i2]], w=[bd["mixT1"]])
                S.flush()
        dctx.close()

        if want("out1"):
            b_out = Buf("y")
            out_phase(1, mixT1, bd["mixT1"], I["od_w_out"], x1, [bd["x1"]], kb.out, b_out)

        S.flush()
    return kb


_CONSTS = None


def _prep_inputs(inputs, b):
    global _CONSTS
    if _CONSTS is None:
        _CONSTS = host_consts()
    m = {}
    for k, shp in IN_SHAPES.items():
        a = np.asarray(inputs[k])
        if k in ("x", "c", "ctx"):
            a = a[b]
        elif k == "c_ctx":
            pass
        else:
            a = a[0]
        m[k] = np.ascontiguousarray(a, dtype=np.float32).reshape(shp)
    m.update(_CONSTS)
    return m


def kernel(**inputs):
    kb = build()
    in_maps = [_prep_inputs(inputs, b) for b in range(8)]
    res = run_bass_kernel_spmd(kb.nc, in_maps, core_ids=list(range(8)))
    return np.stack([np.asarray(r["y"], dtype=np.float32) for r in res.results], axis=0)
```

```python
import heapq
import numpy as np
import concourse.bass as bass
import concourse.mybir as mybir
from concourse.bass_utils import run_bass_kernel_spmd

F32 = mybir.dt.float32
BF16 = mybir.dt.bfloat16
AF = mybir.ActivationFunctionType
ALU = mybir.AluOpType
AX = mybir.AxisListType

ENGS = ("pe", "act", "dve", "pool", "sp")
DEFAULT_COST = {"pe": 0.25, "act": 0.7, "dve": 0.7, "pool": 3.0, "sp": 0.05}


class Buf:
    __slots__ = ("name", "w", "r")

    def __init__(self, name=""):
        self.name = name
        self.w = None
        self.r = []


class Sched:
    def __init__(self, nc, n_dma_sems=48):
        self.nc = nc
        self.eng = {"pe": nc.tensor, "act": nc.scalar, "dve": nc.vector,
                    "pool": nc.gpsimd, "sp": nc.sync}
        self.ops = []
        self.bufs = set()
        self.n_dma_sems = n_dma_sems
        self.ndma = 0
        self.pos = {e: 0 for e in ENGS}
        self.seen = {e: {f: -1 for f in ENGS} for e in ENGS}
        self.seen_dma = {e: set() for e in ENGS}
        self.total = {e: 0 for e in ENGS}
        self.reorder = True
        self.nrec = 0

    def op(self, eng, fn, reads=(), writes=(), dma=False, cost=None, lat=None):
        if cost is None:
            cost = DEFAULT_COST[eng]
            if dma:
                cost = 1.0 if eng == "pool" else 0.1
        rec = dict(eng=eng, fn=fn, dma=dma, deps={}, cost=cost, lat=lat, i=self.nrec, sig=False)
        self.nrec += 1
        deps = rec["deps"]
        for b in reads:
            if b.w is not None:
                deps[b.w["i"]] = (b.w, True)
        for b in writes:
            if b.w is not None and b.w["i"] not in deps:
                deps[b.w["i"]] = (b.w, False)
            for r in b.r:
                if r["i"] not in deps:
                    deps[r["i"]] = (r, False)
        for b in writes:
            b.w = rec
            b.r = []
            self.bufs.add(b)
        for b in reads:
            b.r.append(rec)
            self.bufs.add(b)
        self.ops.append(rec)
        return rec

    def pe(self, fn, r=(), w=(), c=None):
        return self.op("pe", fn, r, w, cost=c)

    def act(self, fn, r=(), w=(), c=None):
        return self.op("act", fn, r, w, cost=c)

    def dve(self, fn, r=(), w=(), c=None):
        return self.op("dve", fn, r, w, cost=c)

    def pool(self, fn, r=(), w=(), c=None):
        return self.op("pool", fn, r, w, cost=c)

    def dma(self, q, out, in_, r=(), w=(), slow=False, lat=4.5):
        if slow:
            return self.op(q, lambda e: e.dma_start(out=out, in_=in_, allow_slow_non_contiguous=True), r, w, dma=True, lat=lat)
        return self.op(q, lambda e: e.dma_start(out=out, in_=in_), r, w, dma=True, lat=lat)

    def barrier(self):
        pass

    def begin(self, stack):
        nc = self.nc
        self.csem = {e: stack.enter_context(nc.semaphore("c_" + e)) for e in ENGS}
        self.dsem = [stack.enter_context(nc.semaphore("d_%d" % i)) for i in range(self.n_dma_sems)]
        self.cnt = {e: 0 for e in ENGS}
        self.snap = {}

    def _schedule(self, batch):
        if not self.reorder:
            return list(batch)
        n = len(batch)
        ids = {o["i"]: k for k, o in enumerate(batch)}
        nwait = [0] * n
        users = [[] for _ in range(n)]
        for k, o in enumerate(batch):
            for pi in o["deps"]:
                if pi in ids:
                    nwait[k] += 1
                    users[ids[pi]].append(k)
        ready_t = [0.0] * n
        fut = {e: [] for e in ENGS}
        av = {e: [] for e in ENGS}
        free = {e: 0.0 for e in ENGS}
        for k, o in enumerate(batch):
            if nwait[k] == 0:
                heapq.heappush(fut[o["eng"]], (0.0, k))
        order = []
        done = 0
        while done < n:
            best = None
            for e in ENGS:
                f = fut[e]
                a = av[e]
                while f and f[0][0] <= free[e]:
                    heapq.heappush(a, heapq.heappop(f)[1])
                if a:
                    cand = (free[e], a[0], e, True)
                elif f:
                    cand = (f[0][0], f[0][1], e, False)
                else:
                    continue
                if best is None or cand[:2] < best[:2]:
                    best = cand
            t, k, e, from_av = best
            if from_av:
                heapq.heappop(av[e])
            else:
                heapq.heappop(fut[e])
            o = batch[k]
            start = max(free[e], ready_t[k])
            free[e] = start + o["cost"]
            fin = free[e] if not o["dma"] else start + o["cost"] + (o["lat"] or 3.0)
            order.append(o)
            done += 1
            for u in users[k]:
                nwait[u] -= 1
                if fin > ready_t[u]:
                    ready_t[u] = fin
                if nwait[u] == 0:
                    heapq.heappush(fut[batch[u]["eng"]], (ready_t[u], u))
        self.sim_time = max(free.values())
        return order

    def flush(self):
        batch = self.ops
        self.ops = []
        N = self.n_dma_sems
        order = self._schedule(batch)
        for o in order:
            if o["dma"]:
                o["did"] = self.ndma
                self.ndma += 1
            else:
                o["pos"] = self.pos[o["eng"]]
                self.pos[o["eng"]] += 1
        last_real = {}
        for o in order:
            eng = o["eng"]
            seen = self.seen[eng]
            sd = self.seen_dma[eng]
            need_c = {}
            need_d = set()
            for (p, is_raw) in o["deps"].values():
                if p["dma"]:
                    if p["did"] not in sd:
                        need_d.add(p["did"])
                else:
                    f = p["eng"]
                    if f == eng and not o["dma"]:
                        if eng == "pe":
                            continue
                    if p["pos"] > seen[f] and p["pos"] > need_c.get(f, (-1, None))[0]:
                        need_c[f] = (p["pos"], p)
            if o["dma"] and o["did"] >= N and (o["did"] - N) not in sd:
                need_d.add(o["did"] - N)
            for f, (ps, p) in need_c.items():
                p["sig"] = True
                if ps > seen[f]:
                    seen[f] = ps
                for g, j in self.snap[(f, ps)].items():
                    if j > seen[g]:
                        seen[g] = j
            for d in need_d:
                sd.add(d)
            if len(sd) > 6 * N:
                lo = self.ndma - 3 * N
                self.seen_dma[eng] = set(x for x in sd if x >= lo)
            o["wc"] = [p for (_, p) in need_c.values()]
            o["wd"] = sorted(need_d)
            if not o["dma"]:
                self.snap[(eng, o["pos"])] = dict(seen)
                last_real[eng] = o
        for o in last_real.values():
            o["sig"] = True
        for o in order:
            if not o["dma"] and o["sig"]:
                self.cnt[o["eng"]] += 1
                o["sigval"] = self.cnt[o["eng"]]
        for o in order:
            e = self.eng[o["eng"]]
            for p in o["wc"]:
                e.wait_ge(self.csem[p["eng"]], p["sigval"])
            for d in o["wd"]:
                e.wait_ge(self.dsem[d % N], 16 * (d // N + 1))
            ins = o["fn"](e)
            if o["dma"]:
                ins.then_inc(self.dsem[o["did"] % N], 16)
            elif o["sig"]:
                ins.then_inc(self.csem[o["eng"]], 1)
            o["fn"] = None
            self.total[o["eng"]] += 1
        dlo = max(0, self.ndma - N)
        for en in ENGS:
            e = self.eng[en]
            for f, o in last_real.items():
                e.wait_ge(self.csem[f], o["sigval"])
                self.seen[en][f] = max(self.seen[en][f], o["pos"])
            for d in range(dlo, self.ndma):
                e.wait_ge(self.dsem[d % N], 16 * (d // N + 1))
        self.eng["pe"].nop()
        for en in ENGS:
            self.seen_dma[en] = set(range(dlo, self.ndma))
        for b in self.bufs:
            b.w = None
            b.r = []
        self.bufs = set()
        self.snap = {}
        self.stats = dict(cnt=dict(self.cnt), total=dict(self.total), ndma=self.ndma)

import contextlib
import math
import ml_dtypes

D = 2048
SEQ = 2048
CTX = 256
NTOK = SEQ + CTX
NT = NTOK // 128
EV_COLS = 10304
EPS = 1e-6

C_Q, C_G, C_Z, C_K, C_V, C_XBC, C_DT = 0, 2048, 4096, 6144, 6656, 7168, 10240


def host_consts():
    c = {}
    c["ident_f"] = np.eye(128, dtype=np.float32)
    c["ident_b"] = np.eye(128, dtype=np.float32).astype(ml_dtypes.bfloat16)
    c["ones_f"] = np.ones((128, 128), np.float32)
    c["ones_b"] = np.ones((128, 128), np.float32).astype(ml_dtypes.bfloat16)
    s = np.arange(128)
    tf = (s[:, None] <= s[None, :]).astype(np.float32)
    tb = (s[:, None] >= s[None, :]).astype(np.float32)
    wf = (s[:, None] > s[None, :]).astype(np.float32)
    wb = (s[:, None] < s[None, :]).astype(np.float32)
    c["tri"] = np.stack([tf, tb, wf, wb], axis=1).copy()
    pos = np.arange(SEQ)
    row = (pos // 64).astype(np.float32)
    col = (pos % 64).astype(np.float32)
    inv = (np.float32(10000.0) ** (-np.arange(0, 64, 2, dtype=np.float32) / np.float32(64))).astype(np.float32)
    ang_row = (row[:, None] * inv).astype(np.float32)
    ang_col = (col[:, None] * inv).astype(np.float32)
    cosT = np.zeros((128, SEQ), np.float32)
    sinT = np.zeros((128, SEQ), np.float32)
    RT = np.zeros((128, 128), np.float32)
    for p in range(128):
        ax, i = p // 64, p % 64
        f = i % 32
        ang = ang_row[:, f] if ax == 0 else ang_col[:, f]
        cosT[p] = np.cos(ang.astype(np.float32))
        sinT[p] = np.sin(ang.astype(np.float32))
        if i < 32:
            partner, sgn = p + 32, -1.0
        else:
            partner, sgn = p - 32, 1.0
        RT[partner, p] = sgn
    c["ropecs"] = np.stack([cosT, sinT], axis=1).copy()
    c["RT"] = RT.astype(ml_dtypes.bfloat16)
    n = np.arange(512)
    angc = 2.0 * np.pi * ((n[:, None] * n[None, :]) % 512) / 512.0
    cc = np.cos(angc) / 1024.0
    sc = np.sin(angc) / 1024.0
    dc = np.stack([cc.reshape(4, 128, 512), sc.reshape(4, 128, 512)], axis=0)
    c["dftc"] = np.ascontiguousarray(dc.transpose(2, 0, 1, 3)).astype(ml_dtypes.bfloat16)
    kk = np.arange(1024)
    tabs = []
    for par in range(2):
        tt = 2 * np.arange(1024) + par
        ang = 2.0 * np.pi * ((tt[:, None] * kk[None, :]) % 2048) / 2048.0
        tabs.append(np.stack([np.cos(ang).reshape(8, 128, 1024), (-np.sin(ang)).reshape(8, 128, 1024)], axis=0))
    dl = np.stack(tabs, axis=0)
    c["dftl"] = np.ascontiguousarray(dl.transpose(3, 0, 1, 2, 4)).astype(ml_dtypes.bfloat16)
    return c


CONST_SHAPES = {
    "ident_f": ([128, 128], "f"), "ident_b": ([128, 128], "b"), "ones_f": ([128, 128], "f"),
    "ones_b": ([128, 128], "b"), "tri": ([128, 4, 128], "f"), "ropecs": ([128, 2, 2048], "f"),
    "RT": ([128, 128], "b"), "dftc": ([128, 2, 4, 512], "b"), "dftl": ([128, 2, 2, 8, 1024], "b"),
}

IN_SHAPES = {
    "x": [SEQ, D], "c": [D], "ctx": [CTX, D], "c_ctx": [D],
    "ev_norm_w": [D], "ev_ada_w": [D, 3 * D], "ev_ada_b": [3 * D], "ev_w_in": [D, EV_COLS],
    "ev_q_norm": [128], "ev_k_norm": [128], "ev_conv_w": [5, 3072], "ev_conv_b": [3072],
    "ev_a_log": [64], "ev_dt_bias": [64], "ev_d_skip": [32], "ev_ssd_norm": [2048],
    "ev_w_out": [4096, D], "od_norm_w": [D], "od_ada_w": [D, 3 * D], "od_ada_b": [3 * D],
    "od_w_in": [D, 8192], "od_w_out": [4096, D],
}


class WB:
    def __init__(self, npieces=8):
        self.b = [Buf() for _ in range(npieces)]

    def p(self, kc):
        return self.b[kc // 4]


class KB:
    def __init__(self, debug=(), stop_after=None):
        self.nc = bass.Bass("TRN2", target_bir_lowering=False)
        self.S = Sched(self.nc, n_dma_sems=48)
        self.debug = set(debug)
        self.stop_after = stop_after
        self.es = contextlib.ExitStack()
        self.I = {}
        for k, shp in IN_SHAPES.items():
            self.I[k] = self.nc.dram_tensor(k, shp, F32, kind="ExternalInput").ap()
        for k, (shp, dt) in CONST_SHAPES.items():
            self.I[k] = self.nc.dram_tensor(k, shp, F32 if dt == "f" else BF16, kind="ExternalInput").ap()
        self.out = self.nc.dram_tensor("y", [SEQ, D], F32, kind="ExternalOutput").ap()
        self.scr = {}
        self.dq = 0

    def dram(self, name, shape, dt):
        kind = "ExternalOutput" if name in self.debug else "Internal"
        t = self.nc.dram_tensor(name, shape, dt, kind=kind).ap()
        self.scr[name] = (t, Buf(name))
        return t

    def sb(self, st, name, shape, dt):
        self.uid = getattr(self, "uid", 0) + 1
        return st.enter_context(self.nc.sbuf_tensor("s%d_%s" % (self.uid, name), shape, dt))

    def ps(self, st, name, shape=(128, 512), dt=F32):
        self.uid = getattr(self, "uid", 0) + 1
        return st.enter_context(self.nc.psum_tensor("p%d_%s" % (self.uid, name), list(shape), dt))

    def q(self):
        self.dq ^= 1
        return "sp" if self.dq else "act"


def build(debug=(), stop_after=None):
    kb = KB(debug, stop_after)
    nc, S, I = kb.nc, kb.S, kb.I
    MUL, ADD, SUB = ALU.mult, ALU.add, ALU.subtract
    phases = ["ada", "norm0", "qk", "gv", "xbc", "ssd", "att", "out0", "norm1", "proj1", "dft1", "dft2", "out1"]

    def want(ph):
        return stop_after is None or phases.index(ph) <= phases.index(stop_after)

    with contextlib.ExitStack() as top:
        S.begin(top)
        ident_f = kb.sb(top, "ident_f", [128, 128], F32)
        ident_b = kb.sb(top, "ident_b", [128, 128], BF16)
        ones_f = kb.sb(top, "ones_f", [128, 128], F32)
        ones_b = kb.sb(top, "ones_b", [128, 128], BF16)
        gate_bc = kb.sb(top, "gate_bc", [128, 2, D], F32)
        modT = kb.sb(top, "modT", [128, 2, 4, 16], F32)
        b_const = Buf("const")
        b_gate = Buf("gate")
        b_modT = Buf("modT")
        b_hT = [Buf("hT%d" % i) for i in range(5)]
        for nm, t in (("ident_f", ident_f), ("ident_b", ident_b), ("ones_f", ones_f), ("ones_b", ones_b)):
            S.dma("sp", t[:], I[nm], w=[b_const])

        qT = kb.dram("qT", [16, 128, SEQ], BF16)
        kT = kb.dram("kT", [4, 128, NTOK], BF16)
        sgT = kb.dram("sgT", [16, 128, SEQ], BF16)
        vtm = kb.dram("vtm", [NTOK, 512], BF16)
        sz = kb.dram("sz", [SEQ, 2048], BF16)
        xs_tm = kb.dram("xs_tm", [NTOK, 2048], BF16)
        b_tm = kb.dram("b_tm", [NTOK, 512], BF16)
        BT = kb.dram("BT", [4, 128, NTOK], BF16)
        CT = kb.dram("CT", [4, 128, NTOK], BF16)
        dtp_d = kb.dram("dtp_d", [NTOK, 64], F32)
        dA_d = kb.dram("dA_d", [NTOK, 64], F32)
        yf_d = kb.dram("yf_d", [SEQ, 2048], F32)
        mixT = kb.dram("mixT", [16, 128, 32, 128], BF16)
        x1 = kb.dram("x1", [SEQ, D], F32)
        uT = kb.dram("uT", [32, 128, SEQ], BF16)
        szT = kb.dram("szT", [32, 128, SEQ], BF16)
        UCd = kb.dram("UCd", [2, 8, 128, 16, 512], BF16)
        mixT1 = kb.dram("mixT1", [16, 128, 32, 128], BF16)
        bd = {k: v[1] for k, v in kb.scr.items()}

        def load_w(dst, dbuf, wap, c0, n, nk=16):
            src = wap[:, c0:c0 + n].rearrange("(kc p) n -> p kc n", p=128)
            for k0 in range(0, nk, 4):
                S.dma("pool", dst[:, k0:k0 + 4, 0:n], src[:, k0:k0 + 4, :], w=[dbuf.p(k0)])

        ADA_W = (("ev_ada_w", "ev_ada_b"), ("od_ada_w", "od_ada_b"))

        def ada_setup(st, psum=None):
            A = {}
            A["cT"] = kb.sb(st, "cT", [128, 16, 64], F32)
            A["cTb"] = kb.sb(st, "cTb", [128, 16, 64], BF16)
            A["wblk"] = [kb.sb(st, "adaw%d" % i, [128, 16, 512], BF16) for i in range(2)]
            A["nwT"] = kb.sb(st, "nwT", [128, 2, 16], F32)
            A["biasF"] = kb.sb(st, "biasF", [128, 2, 48], F32)
            A["modF"] = kb.sb(st, "modF", [128, 2, 48, 2], F32)
            A["dg"] = [kb.sb(st, "dg%d" % i, [128, 128], F32) for i in range(2)]
            if psum is None:
                A["ps"] = [kb.ps(st, "psA%d" % i)[:] for i in range(2)]
                A["b_ps"] = [Buf(), Buf()]
            else:
                A["ps"] = [psum[0]]
                A["b_ps"] = [psum[1]]
            A["b_cT"], A["b_nw"], A["b_bias"], A["b_modF"] = Buf(), Buf(), Buf(), Buf()
            A["b_w"] = [WB(), WB()]
            A["b_dg"] = [Buf(), Buf()]
            A["wi"] = 0
            A["pi"] = 0
            cT, cTb = A["cT"], A["cTb"]
            S.dve(lambda e: e.memset(cT[:], 0.0), w=[A["b_cT"]])
            S.dma("sp", cT[:, :, 0:1], I["c"].rearrange("(c p one) -> p c one", p=128, one=1), w=[A["b_cT"]], slow=True)
            S.dma("sp", cT[:, :, 32:33], I["c_ctx"].rearrange("(c p one) -> p c one", p=128, one=1), w=[A["b_cT"]], slow=True)
            S.act(lambda e: e.activation(out=cTb[:], in_=cT[:], func=AF.Silu), r=[A["b_cT"]], w=[A["b_cT"]])
            S.dma("sp", A["nwT"][:, 0, :].unsqueeze(2), I["ev_norm_w"].rearrange("(c p one) -> p c one", p=128, one=1), w=[A["b_nw"]], slow=True)
            S.dma("sp", A["nwT"][:, 1, :].unsqueeze(2), I["od_norm_w"].rearrange("(c p one) -> p c one", p=128, one=1), w=[A["b_nw"]], slow=True)
            for layer in range(2):
                S.dma("act", A["biasF"][:, layer, :].unsqueeze(2), I[ADA_W[layer][1]].rearrange("(c p one) -> p c one", p=128, one=1),
                      w=[A["b_bias"]], slow=True)
            return A

        def ada_chunks(A, layer, c8s):
            wn = ADA_W[layer][0]
            cTb, modF, biasF = A["cTb"], A["modF"], A["biasF"]
            for c8 in c8s:
                pi = A["pi"]
                A["pi"] += 1
                pA, bA = A["ps"][pi % len(A["ps"])], A["b_ps"][pi % len(A["ps"])]
                for half in range(2):
                    wi = A["wi"]
                    A["wi"] += 1
                    w_t, w_b = A["wblk"][wi % 2], A["b_w"][wi % 2]
                    load_w(w_t, w_b, I[wn], (c8 * 2 + half) * 512, 512)
                    for q in range(4):
                        col = (half * 4 + q) * 64
                        for kc in range(16):
                            S.pe(lambda e, kc=kc, w_t=w_t, pA=pA, q=q, col=col: e.matmul(
                                pA[:, col:col + 64], lhsT=w_t[:, kc, q * 128:(q + 1) * 128], rhs=cTb[:, kc, :],
                                start=(kc == 0), stop=(kc == 15)), r=[A["b_cT"], w_b.p(kc)], w=[bA], c=0.1)
                pv = pA.rearrange("p (a c) -> p a c", c=64)
                S.dve(lambda e, pv=pv, c8=c8: e.tensor_tensor(
                    out=modF[:, layer, c8 * 8:(c8 + 1) * 8, :], in0=pv[:, :, 0:33:32],
                    in1=biasF[:, layer, c8 * 8:(c8 + 1) * 8].unsqueeze(2).broadcast_to([128, 8, 2]), op=ADD),
                      r=[bA, A["b_bias"]], w=[A["b_modF"]])

        def ada_mod(A, layer):
            modF, nwT = A["modF"], A["nwT"]
            for ri in range(2):
                S.dve(lambda e, ri=ri: e.tensor_copy(out=modT[:, layer, 2 * ri, :], in_=modF[:, layer, 0:16, ri]),
                      r=[A["b_modF"]], w=[b_modT])
                S.dve(lambda e, ri=ri: e.scalar_tensor_tensor(
                    out=modT[:, layer, 2 * ri + 1, :], in0=modF[:, layer, 16:32, ri], scalar=1.0, in1=nwT[:, layer, :],
                    op0=ADD, op1=MUL), r=[A["b_modF"], A["b_nw"]], w=[b_modT])

        def ada_gate(A, layer):
            modF = A["modF"]
            for gq in range(4):
                pi = A["pi"]
                A["pi"] += 1
                pG, bG = A["ps"][pi % len(A["ps"])], A["b_ps"][pi % len(A["ps"])]
                for q in range(4):
                    ch = gq * 4 + q
                    d2 = ch % 2
                    S.dve(lambda e, ch=ch, d2=d2: e.tensor_scalar(out=A["dg"][d2][:], in0=ident_f[:], scalar1=modF[:, layer, 32 + ch, 0:1],
                                                                  scalar2=None, op0=MUL),
                          r=[A["b_modF"], b_const], w=[A["b_dg"][d2]], c=0.25)
                    S.pe(lambda e, q=q, d2=d2, pG=pG: e.matmul(pG[:, q * 128:(q + 1) * 128], lhsT=ones_f[:], rhs=A["dg"][d2][:],
                                                              start=True, stop=True), r=[A["b_dg"][d2], b_const], w=[bG], c=0.3)
                S.dve(lambda e, gq=gq, pG=pG: e.tensor_copy(out=gate_bc[:, layer, gq * 512:(gq + 1) * 512], in_=pG),
                      r=[bG], w=[b_gate])

        if want("ada"):
            with contextlib.ExitStack() as st:
                A = ada_setup(st)
                ada_chunks(A, 0, range(0, 4))
                ada_mod(A, 0)
                S.flush()

        def norm_phase(layer, sources, st=None):
            own = st is None
            if own:
                st = contextlib.ExitStack()
            if True:
                xt = [kb.sb(st, "nx%d" % i, [128, 4, D], F32) for i in range(2)]
                junk = kb.sb(st, "njunk", [128, D], BF16)
                ssq = kb.sb(st, "nssq", [128, 16], F32)
                pst = [kb.ps(st, "npst%d" % i) for i in range(4)]
                b_xt = [Buf(), Buf()]
                b_junk, b_ssq = Buf(), Buf()
                b_pst = [Buf() for _ in range(4)]
                gi = 0
                pi = 0
                for (src, rb, tok0, ntile, hb, mb) in sources:
                    x_t, x_b = xt[gi % 2], b_xt[gi % 2]
                    gi += 1
                    for j in range(ntile):
                        S.dma(kb.q(), x_t[:, j, :], src[j * 128:(j + 1) * 128, :], r=rb, w=[x_b])
                    for j in range(ntile):
                        S.act(lambda e, j=j, x_t=x_t: e.activation(out=junk[:], in_=x_t[:, j, :], func=AF.Square,
                                                                    accum_out=ssq[:, j:j + 1]),
                              r=[x_b], w=[b_junk, b_ssq], c=2.0)
                    S.act(lambda e, ntile=ntile: e.activation(out=ssq[:, 8:8 + ntile], in_=ssq[:, 0:ntile], func=AF.Sqrt,
                                                              scale=1.0 / D, bias=eps_t[:, 0:1]),
                          r=[b_ssq, b_const], w=[b_ssq])
                    S.dve(lambda e, ntile=ntile: e.reciprocal(out=ssq[:, 4:4 + ntile], in_=ssq[:, 8:8 + ntile]),
                          r=[b_ssq], w=[b_ssq])
                    for j in range(ntile):
                        S.dve(lambda e, j=j, x_t=x_t: e.tensor_scalar(out=x_t[:, j, :], in0=x_t[:, j, :],
                                                                       scalar1=ssq[:, 4 + j:5 + j], scalar2=None, op0=MUL),
                              r=[x_b, b_ssq], w=[x_b], c=2.3)
                    for kc in range(16):
                        p_t, p_b = pst[pi % 4], b_pst[pi % 4]
                        pi += 1
                        for j in range(ntile):
                            S.pe(lambda e, j=j, kc=kc, x_t=x_t, p_t=p_t: e.transpose(
                                p_t[:, j * 128:(j + 1) * 128], x_t[:, j, kc * 128:(kc + 1) * 128], ident_f[:]),
                                 r=[x_b, b_const], w=[p_b], c=0.2)
                        eng = S.dve if kc % 2 == 0 else S.act
                        if kc % 2 == 0:
                            S.dve(lambda e, kc=kc, p_t=p_t, ntile=ntile, tok0=tok0, mb=mb: e.tensor_scalar(
                                out=hT[:, kc, tok0:tok0 + ntile * 128], in0=p_t[:, 0:ntile * 128],
                                scalar1=modT[:, layer, mb + 1, kc:kc + 1], scalar2=modT[:, layer, mb, kc:kc + 1],
                                op0=MUL, op1=ADD), r=[p_b, b_modT], w=[hb])
                        else:
                            S.act(lambda e, kc=kc, p_t=p_t, ntile=ntile, tok0=tok0, mb=mb: e.activation(
                                out=hT[:, kc, tok0:tok0 + ntile * 128], in_=p_t[:, 0:ntile * 128], func=AF.Identity,
                                scale=modT[:, layer, mb + 1, kc:kc + 1], bias=modT[:, layer, mb, kc:kc + 1]),
                                  r=[p_b, b_modT], w=[hb])
                if own:
                    S.flush()
                    st.close()

        eps_t = kb.sb(top, "eps_t", [128, 2], F32)
        S.dve(lambda e: e.memset(eps_t[:, 0:1], EPS), w=[b_const])
        S.dve(lambda e: e.memset(eps_t[:, 1:2], 1.0), w=[b_const])
        hctx = contextlib.ExitStack()
        hT = kb.sb(hctx, "hT", [128, 16, NTOK], BF16)

        if want("norm0"):
            srcs = [(I["ctx"], [], 0, 2, b_hT[0], 2)]
            for tb in range(4):
                srcs.append((I["x"][tb * 512:(tb + 1) * 512, :], [], CTX + tb * 512, 4, b_hT[1 + tb], 0))
            stn0 = contextlib.ExitStack()
            norm_phase(0, srcs, st=stn0)

        tblocks = [(0, CTX, b_hT[0], None)] + [(CTX + i * 512, 512, b_hT[1 + i], i * 512) for i in range(4)]

        if want("gv"):
            with contextlib.ExitStack() as st:
                wg = [kb.sb(st, "wg%d" % i, [128, 16, 512], BF16) for i in range(2)]
                b_wg = [WB(), WB()]
                og = [kb.sb(st, "og%d" % i, [128, 512], BF16) for i in range(3)]
                b_og = [Buf() for _ in range(3)]
                pacc = [kb.ps(st, "gacc%d" % i) for i in range(3)]
                b_pacc = [Buf() for _ in range(3)]
                dtb = kb.sb(st, "dtb", [128, 3, 64], F32)
                dto = kb.sb(st, "dto", [128, 2, 2, 64], F32)
                b_dtb = Buf()
                b_dto = [Buf(), Buf()]
                S.dma("sp", dtb[:, 0, :], I["ev_dt_bias"].partition_broadcast(128), w=[b_dtb])
                S.dma("sp", dtb[:, 1, :], I["ev_a_log"].partition_broadcast(128), w=[b_dtb])
                S.act(lambda e: e.activation(out=dtb[:, 1, :], in_=dtb[:, 1, :], func=AF.Exp), r=[b_dtb], w=[b_dtb])
                S.dve(lambda e: e.tensor_scalar(out=dtb[:, 1, :], in0=dtb[:, 1, :], scalar1=-1.0, scalar2=None, op0=MUL),
                      r=[b_dtb], w=[b_dtb])
                wi = 0
                it = 0
                for gb in range(4):
                    w_t, w_b = wg[wi % 2], b_wg[wi % 2]
                    wi += 1
                    load_w(w_t, w_b, I["ev_w_in"], C_G + gb * 512, 512)
                    for hh in range(4):
                        h = gb * 4 + hh
                        for (tok0, ntok, hbuf, lat) in tblocks[1:]:
                            i3 = it % 3
                            it += 1
                            for kc in range(16):
                                S.pe(lambda e, kc=kc, w_t=w_t, i3=i3, tok0=tok0, hh=hh: e.matmul(
                                    pacc[i3][:], lhsT=w_t[:, kc, hh * 128:(hh + 1) * 128], rhs=hT[:, kc, tok0:tok0 + 512],
                                    start=(kc == 0), stop=(kc == 15)), r=[w_b.p(kc), hbuf], w=[b_pacc[i3]])
                            S.act(lambda e, i3=i3: e.activation(out=og[i3][:], in_=pacc[i3][:], func=AF.Silu),
                                  r=[b_pacc[i3]], w=[b_og[i3]])
                            S.dma("sp", sgT[h, :, lat:lat + 512], og[i3][:], r=[b_og[i3]], w=[bd["sgT"]])
                for zb in range(4):
                    w_t, w_b = wg[wi % 2], b_wg[wi % 2]
                    wi += 1
                    load_w(w_t, w_b, I["ev_w_in"], C_Z + zb * 512, 512)
                    for tt in range(2, NT):
                        i3 = it % 3
                        it += 1
                        hbuf = b_hT[1 + (tt - 2) // 4]
                        for kc in range(16):
                            S.pe(lambda e, kc=kc, w_t=w_t, i3=i3, tt=tt: e.matmul(
                                pacc[i3][:], lhsT=hT[:, kc, tt * 128:(tt + 1) * 128], rhs=w_t[:, kc, :],
                                start=(kc == 0), stop=(kc == 15)), r=[w_b.p(kc), hbuf], w=[b_pacc[i3]])
                        S.act(lambda e, i3=i3: e.activation(out=og[i3][:], in_=pacc[i3][:], func=AF.Silu),
                              r=[b_pacc[i3]], w=[b_og[i3]])
                        S.dma("sp", sz[(tt - 2) * 128:(tt - 1) * 128, zb * 512:(zb + 1) * 512], og[i3][:], r=[b_og[i3]], w=[bd["sz"]])
                w_t, w_b = wg[wi % 2], b_wg[wi % 2]
                wi += 1
                load_w(w_t, w_b, I["ev_w_in"], C_V, 512)
                for tt in range(NT):
                    i3 = it % 3
                    it += 1
                    hbuf = b_hT[0] if tt < 2 else b_hT[1 + (tt - 2) // 4]
                    for kc in range(16):
                        S.pe(lambda e, kc=kc, w_t=w_t, i3=i3, tt=tt: e.matmul(
                            pacc[i3][:], lhsT=hT[:, kc, tt * 128:(tt + 1) * 128], rhs=w_t[:, kc, :],
                            start=(kc == 0), stop=(kc == 15)), r=[w_b.p(kc), hbuf], w=[b_pacc[i3]])
                    S.dve(lambda e, i3=i3: e.tensor_copy(out=og[i3][:], in_=pacc[i3][:]), r=[b_pacc[i3]], w=[b_og[i3]])
                    S.dma("sp", vtm[tt * 128:(tt + 1) * 128, :], og[i3][:], r=[b_og[i3]], w=[bd["vtm"]])
                w_t, w_b = wg[wi % 2], b_wg[wi % 2]
                wi += 1
                load_w(w_t, w_b, I["ev_w_in"], C_DT, 64)
                for tt in range(NT):
                    i3 = it % 3
                    it += 1
                    i2 = tt % 2
                    hbuf = b_hT[0] if tt < 2 else b_hT[1 + (tt - 2) // 4]
                    for kc in range(16):
                        S.pe(lambda e, kc=kc, w_t=w_t, i3=i3, tt=tt: e.matmul(
                            pacc[i3][:, 0:64], lhsT=hT[:, kc, tt * 128:(tt + 1) * 128], rhs=w_t[:, kc, 0:64],
                            start=(kc == 0), stop=(kc == 15)), r=[w_b.p(kc), hbuf], w=[b_pacc[i3]])
                    S.dve(lambda e, i3=i3, i2=i2: e.tensor_tensor(out=dto[:, i2, 0, :], in0=pacc[i3][:, 0:64], in1=dtb[:, 0, :], op=ADD),
                          r=[b_pacc[i3], b_dtb], w=[b_dto[i2]])
                    S.act(lambda e, i2=i2: e.activation(out=dto[:, i2, 0, :], in_=dto[:, i2, 0, :], func=AF.Exp), r=[b_dto[i2]], w=[b_dto[i2]])
                    S.act(lambda e, i2=i2: e.activation(out=dto[:, i2, 0, :], in_=dto[:, i2, 0, :], func=AF.Ln, bias=eps_t[:, 1:2]),
                          r=[b_dto[i2], b_const], w=[b_dto[i2]])
                    S.dve(lambda e, i2=i2: e.tensor_tensor(out=dto[:, i2, 1, :], in0=dto[:, i2, 0, :], in1=dtb[:, 1, :], op=MUL),
                          r=[b_dto[i2], b_dtb], w=[b_dto[i2]])
                    S.dma("sp", dtp_d[tt * 128:(tt + 1) * 128, :], dto[:, i2, 0, :], r=[b_dto[i2]], w=[bd["dtp_d"]])
                    S.dma("sp", dA_d[tt * 128:(tt + 1) * 128, :], dto[:, i2, 1, :], r=[b_dto[i2]], w=[bd["dA_d"]])
                S.flush()

        stn0.close()

        if want("qk"):
            with contextlib.ExitStack() as st:
                wq = [kb.sb(st, "wq%d" % i, [128, 16, 512], BF16) for i in range(2)]
                b_wq = [WB(), WB()]
                ropecs = kb.sb(st, "ropecs", [128, 2, SEQ], F32)
                RTt = kb.sb(st, "RTt", [128, 128], BF16)
                qkn = kb.sb(st, "qkn", [128, 2], F32)
                b_rc = Buf()
                S.dma("sp", ropecs[:], I["ropecs"], w=[b_rc])
                S.dma("sp", RTt[:], I["RT"], w=[b_rc])
                S.dma("sp", qkn[:, 0:1], I["ev_q_norm"].rearrange("(p one) -> p one", one=1), w=[b_rc])
                S.dma("sp", qkn[:, 1:2], I["ev_k_norm"].rearrange("(p one) -> p one", one=1), w=[b_rc])
                NB = 2
                sq = [kb.sb(st, "sq%d" % i, [128, 512], BF16) for i in range(NB)]
                rs = [kb.sb(st, "rs%d" % i, [128, 512], F32) for i in range(NB)]
                qn = [kb.sb(st, "qn%d" % i, [128, 512], BF16) for i in range(NB)]
                t1 = [kb.sb(st, "t1%d" % i, [128, 512], F32) for i in range(NB)]
                t2 = [kb.sb(st, "t2%d" % i, [128, 512], F32) for i in range(NB)]
                qo = [kb.sb(st, "qo%d" % i, [128, 512], BF16) for i in range(NB)]
                b_sq, b_rs, b_qn, b_t1, b_t2, b_qo = [[Buf() for _ in range(NB)] for _ in range(6)]
                pacc = [kb.ps(st, "qacc%d" % i) for i in range(3)]
                pss = [kb.ps(st, "qss%d" % i) for i in range(2)]
                prot = [kb.ps(st, "qrot%d" % i) for i in range(2)]
                b_pacc, b_pss, b_prot = [[Buf(), Buf(), Buf()] for _ in range(3)]
                it = 0
                for hb in range(20):
                    isq = hb < 16
                    c0 = C_Q + hb * 128 if isq else C_K + (hb - 16) * 128
                    w_t, w_b = wq[(hb // 4) % 2], b_wq[(hb // 4) % 2]
                    wo = (hb % 4) * 128
                    if hb % 4 == 0:
                        load_w(w_t, w_b, I["ev_w_in"], c0, 512)
                    for (tok0, ntok, hbuf, lat) in tblocks:
                        if lat is None and isq:
                            continue
                        i2 = it % 2
                        i3 = it % 3
                        it += 1
                        for kc in range(16):
                            S.pe(lambda e, kc=kc, w_t=w_t, i2=i2, i3=i3, tok0=tok0, ntok=ntok, wo=wo: e.matmul(
                                pacc[i3][:, 0:ntok], lhsT=w_t[:, kc, wo:wo + 128], rhs=hT[:, kc, tok0:tok0 + ntok],
                                start=(kc == 0), stop=(kc == 15)), r=[w_b.p(kc), hbuf], w=[b_pacc[i3]])
                        S.act(lambda e, i2=i2, i3=i3, ntok=ntok: e.activation(out=sq[i2][:, 0:ntok], in_=pacc[i3][:, 0:ntok], func=AF.Square),
                              r=[b_pacc[i3]], w=[b_sq[i2]])
                        S.pe(lambda e, i2=i2, i3=i3, ntok=ntok: e.matmul(pss[i2][:, 0:ntok], lhsT=ones_b[:], rhs=sq[i2][:, 0:ntok],
                                                                   start=True, stop=True), r=[b_sq[i2], b_const], w=[b_pss[i2]], c=0.25)
                        S.act(lambda e, i2=i2, i3=i3, ntok=ntok: e.activation(out=rs[i2][:, 0:ntok], in_=pss[i2][:, 0:ntok], func=AF.Ln,
                                                                        scale=1.0 / 128, bias=eps_t[:, 0:1]),
                              r=[b_pss[i2], b_const], w=[b_rs[i2]], c=0.6)
                        S.act(lambda e, i2=i2, i3=i3, ntok=ntok: e.activation(out=rs[i2][:, 0:ntok], in_=rs[i2][:, 0:ntok], func=AF.Exp, scale=-0.5),
                              r=[b_rs[i2]], w=[b_rs[i2]], c=0.6)
                        nsel = 0 if isq else 1
                        S.dve(lambda e, i2=i2, i3=i3, ntok=ntok, nsel=nsel: e.scalar_tensor_tensor(
                            out=qn[i2][:, 0:ntok], in0=pacc[i3][:, 0:ntok], scalar=qkn[:, nsel:nsel + 1], in1=rs[i2][:, 0:ntok],
                            op0=MUL, op1=MUL), r=[b_pacc[i3], b_rs[i2], b_rc], w=[b_qn[i2]])
                        if lat is None:
                            g = hb - 16
                            S.dma(kb.q(), kT[g, :, 0:CTX], qn[i2][:, 0:CTX], r=[b_qn[i2]], w=[bd["kT"]])
                            continue
                        S.pe(lambda e, i2=i2, i3=i3: e.matmul(prot[i2][:], lhsT=RTt[:], rhs=qn[i2][:], start=True, stop=True),
                             r=[b_qn[i2], b_rc], w=[b_prot[i2]])
                        S.pool(lambda e, i2=i2, i3=i3, lat=lat: e.tensor_tensor(out=t1[i2][:], in0=qn[i2][:], in1=ropecs[:, 0, lat:lat + 512], op=MUL),
                               r=[b_qn[i2], b_rc], w=[b_t1[i2]], c=2.0)
                        S.dve(lambda e, i2=i2, i3=i3, lat=lat: e.tensor_tensor(out=t2[i2][:], in0=prot[i2][:], in1=ropecs[:, 1, lat:lat + 512], op=MUL),
                              r=[b_prot[i2], b_rc], w=[b_t2[i2]])
                        S.pool(lambda e, i2=i2, i3=i3: e.tensor_tensor(out=qo[i2][:], in0=t1[i2][:], in1=t2[i2][:], op=ADD),
                               r=[b_t1[i2], b_t2[i2]], w=[b_qo[i2]], c=2.0)
                        if isq:
                            S.dma(kb.q(), qT[hb, :, lat:lat + 512], qo[i2][:], r=[b_qo[i2]], w=[bd["qT"]])
                        else:
                            S.dma(kb.q(), kT[hb - 16, :, CTX + lat:CTX + lat + 512], qo[i2][:], r=[b_qo[i2]], w=[bd["kT"]])
                S.flush()

        if want("xbc"):
            with contextlib.ExitStack() as st:
                wx = [kb.sb(st, "wx%d" % i, [128, 16, 512], BF16) for i in range(2)]
                b_wx = [WB(), WB()]
                cw = kb.sb(st, "cw", [128, 24, 8], F32)
                b_cw = Buf()
                for j in range(5):
                    S.dma("sp", cw[:, :, j:j + 1], I["ev_conv_w"][j, :].rearrange("(b p one) -> p b one", p=128, one=1),
                          w=[b_cw], slow=True)
                S.dma("sp", cw[:, :, 5:6], I["ev_conv_b"].rearrange("(b p one) -> p b one", p=128, one=1), w=[b_cw], slow=True)
                xpad = [kb.sb(st, "xpad%d" % i, [128, SEQ + 4], F32) for i in range(2)]
                cpad = [kb.sb(st, "cpad%d" % i, [128, CTX + 4], F32) for i in range(2)]
                acc = [kb.sb(st, "cacc%d" % i, [128, SEQ], F32) for i in range(2)]
                cacc = [kb.sb(st, "ccacc%d" % i, [128, CTX], F32) for i in range(2)]
                cvo = [kb.sb(st, "cvo%d" % i, [128, NTOK], BF16) for i in range(2)]
                tro = [kb.sb(st, "tro%d" % i, [128, 8, 128], BF16) for i in range(2)]
                b_xpad, b_cpad, b_acc, b_cacc, b_cvo, b_tro = [[Buf(), Buf()] for _ in range(6)]
                pacc = [kb.ps(st, "xacc%d" % i) for i in range(3)]
                ptr = [kb.ps(st, "xtr%d" % i, (128, 1024), BF16) for i in range(2)]
                b_pacc = [Buf() for _ in range(3)]
                b_ptr = [Buf(), Buf()]
                for i in range(2):
                    S.pool(lambda e, i=i: e.memset(xpad[i][:], 0.0), w=[b_xpad[i]])
                    S.pool(lambda e, i=i: e.memset(cpad[i][:], 0.0), w=[b_cpad[i]])
                it = 0
                ti = 0
                for cb in range(24):
                    w_t, w_b = wx[(cb // 4) % 2], b_wx[(cb // 4) % 2]
                    wo = (cb % 4) * 128
                    if cb % 4 == 0:
                        load_w(w_t, w_b, I["ev_w_in"], C_XBC + cb * 128, 512)
                    i2 = cb % 2
                    for (tok0, ntok, hbuf, lat) in tblocks:
                        i3 = it % 3
                        it += 1
                        for kc in range(16):
                            S.pe(lambda e, kc=kc, w_t=w_t, i3=i3, tok0=tok0, ntok=ntok, wo=wo: e.matmul(
                                pacc[i3][:, 0:ntok], lhsT=w_t[:, kc, wo:wo + 128], rhs=hT[:, kc, tok0:tok0 + ntok],
                                start=(kc == 0), stop=(kc == 15)), r=[w_b.p(kc), hbuf], w=[b_pacc[i3]])
                        if lat is None:
                            S.act(lambda e, i3=i3, i2=i2: e.activation(out=cpad[i2][:, 2:2 + CTX], in_=pacc[i3][:, 0:CTX], func=AF.Copy),
                                  r=[b_pacc[i3]], w=[b_cpad[i2]])
                        else:
                            S.act(lambda e, i3=i3, i2=i2, lat=lat: e.activation(out=xpad[i2][:, 2 + lat:2 + lat + 512], in_=pacc[i3][:],
                                                                                   func=AF.Copy), r=[b_pacc[i3]], w=[b_xpad[i2]])
                    for (pad, bp, ac, ba, n, o0) in ((xpad[i2], b_xpad[i2], acc[i2], b_acc[i2], SEQ, CTX),
                                                     (cpad[i2], b_cpad[i2], cacc[i2], b_cacc[i2], CTX, 0)):
                        S.act(lambda e, pad=pad, ac=ac, n=n, cb=cb: e.activation(out=ac[:, 0:n], in_=pad[:, 0:n], func=AF.Copy,
                                                                                 scale=cw[:, cb, 0:1]), r=[bp, b_cw], w=[ba], c=0.3 + n * 0.0009)
                        for j in range(1, 5):
                            S.dve(lambda e, pad=pad, ac=ac, n=n, cb=cb, j=j: e.scalar_tensor_tensor(
                                out=ac[:, 0:n], in0=pad[:, j:j + n], scalar=cw[:, cb, j:j + 1], in1=ac[:, 0:n], op0=MUL, op1=ADD),
                                  r=[bp, b_cw, ba], w=[ba], c=0.2 + n * 0.00105)
                        S.act(lambda e, ac=ac, n=n, cb=cb, o0=o0, i2=i2: e.activation(out=cvo[i2][:, o0:o0 + n], in_=ac[:, 0:n], func=AF.Silu,
                                                                                      bias=cw[:, cb, 5:6]), r=[ba, b_cw], w=[b_cvo[i2]], c=0.3 + n * 0.0009)
                    if cb >= 16:
                        g = (cb - 16) % 4
                        dst = BT if cb < 20 else CT
                        S.dma("sp", dst[g], cvo[i2][:], r=[b_cvo[i2]], w=[bd["BT"] if cb < 20 else bd["CT"]])
                    if cb < 20:
                        for (t0, nt) in ((0, 8), (8, 8), (16, 2)):
                            p2 = ti % 2
                            ti += 1
                            for i in range(nt):
                                S.pe(lambda e, i=i, t0=t0, p2=p2, i2=i2: e.transpose(
                                    ptr[p2][:, i * 128:(i + 1) * 128], cvo[i2][:, (t0 + i) * 128:(t0 + i + 1) * 128], ident_b[:]),
                                     r=[b_cvo[i2], b_const], w=[b_ptr[p2]], c=0.1)
                            S.dve(lambda e, p2=p2, nt=nt: e.tensor_copy(out=tro[p2][:, 0:nt, :],
                                                                        in_=ptr[p2][:, 0:nt * 128].rearrange("p (i c) -> p i c", c=128)),
                                  r=[b_ptr[p2]], w=[b_tro[p2]])
                            if cb < 16:
                                d_ap = xs_tm[t0 * 128:(t0 + nt) * 128, cb * 128:(cb + 1) * 128].rearrange("(i p) c -> p i c", p=128)
                                S.dma("sp", d_ap, tro[p2][:, 0:nt, :], r=[b_tro[p2]], w=[bd["xs_tm"]])
                            else:
                                g = cb - 16
                                d_ap = b_tm[t0 * 128:(t0 + nt) * 128, g * 128:(g + 1) * 128].rearrange("(i p) c -> p i c", p=128)
                                S.dma("sp", d_ap, tro[p2][:, 0:nt, :], r=[b_tro[p2]], w=[bd["b_tm"]])
                S.flush()
        hctx.close()

        if want("ssd"):
            with contextlib.ExitStack() as st:
                tri = kb.sb(st, "tri", [128, 4, 128], F32)
                dtp_all = kb.sb(st, "dtp_all", [128, NT, 64], F32)
                dA_all = kb.sb(st, "dA_all", [128, NT, 64], F32)
                dskip = kb.sb(st, "dskip", [128, 32], F32)
                ssdn = kb.sb(st, "ssdn", [128, 2048], F32)
                b_sc = Buf()
                S.dma("sp", tri[:], I["tri"], w=[b_sc])
                S.dma("sp", dtp_all[:], dtp_d.rearrange("(t p) c -> p t c", p=128), r=[bd["dtp_d"]], w=[b_sc])
                S.dma("sp", dA_all[:], dA_d.rearrange("(t p) c -> p t c", p=128), r=[bd["dA_d"]], w=[b_sc])
                S.dma("sp", dskip[:], I["ev_d_skip"].partition_broadcast(128), w=[b_sc])
                S.dma("sp", ssdn[:], I["ev_ssd_norm"].partition_broadcast(128), w=[b_sc])
                Dsk = kb.sb(st, "Dsk", [128, 32, 128], BF16)
                b_dsk = Buf()
                for h in range(32):
                    S.dve(lambda e, h=h: e.tensor_scalar(out=Dsk[:, h, :], in0=ident_b[:], scalar1=dskip[:, h:h + 1], scalar2=None, op0=MUL),
                          r=[b_sc, b_const], w=[b_dsk], c=0.2)
                H = kb.sb(st, "H", [128, 4, 512], F32)
                Hbf = kb.sb(st, "Hbf", [128, 4, 512], BF16)
                b_H = [Buf() for _ in range(4)]
                b_Hbf = [Buf() for _ in range(4)]
                xs_t = [kb.sb(st, "xs_t%d" % i, [128, 2048], BF16) for i in range(3)]
                btm_t = [kb.sb(st, "btm_t%d" % i, [128, 512], BF16) for i in range(3)]
                BTc = [kb.sb(st, "BTc%d" % i, [128, 4, 128], BF16) for i in range(3)]
                CTc = [kb.sb(st, "CTc%d" % i, [128, 4, 128], BF16) for i in range(3)]
                b_ld = [Buf(), Buf(), Buf()]
                X = [kb.sb(st, "X%d" % i, [128, 2048], BF16) for i in range(2)]
                Xd = [kb.sb(st, "Xd%d" % i, [128, 2048], BF16) for i in range(2)]
                b_X, b_Xd = [Buf(), Buf()], [Buf(), Buf()]
                sm = [kb.sb(st, "sm%d" % i, [128, 4, 32], F32) for i in range(2)]
                b_sm = [Buf(), Buf()]
                rhsA = [kb.sb(st, "rhsA%d" % i, [128, 8, 128], F32) for i in range(2)]
                expd = [kb.sb(st, "expd%d" % i, [128, 8, 128], BF16) for i in range(2)]
                MT = [kb.sb(st, "MT%d" % i, [128, 8, 128], BF16) for i in range(2)]
                CBm = [kb.sb(st, "CBm%d" % i, [128, 128], BF16) for i in range(2)]
                dteb = [kb.sb(st, "dteb%d" % i, [128, 32], BF16) for i in range(2)]
                tmpy = [kb.sb(st, "tmpy%d" % i, [128, 512], F32) for i in range(2)]
                b_rhsA, b_expd, b_MT, b_CBm, b_tmpy = [[Buf(), Buf()] for _ in range(5)]
                yt = [kb.sb(st, "yt%d" % i, [128, 2048], F32) for i in range(2)]
                b_yt = [Buf(), Buf()]
                yf_t = kb.sb(st, "yf_t", [128, 2048], F32)
                sz_t = kb.sb(st, "sz_t", [128, 2048], BF16)
                yn = kb.sb(st, "yn", [128, 2048], BF16)
                junk = kb.sb(st, "sjunk", [128, 512], BF16)
                gss = kb.sb(st, "gss", [128, 12], F32)
                trs = kb.sb(st, "trs", [128, 16, 128], BF16)
                b_yf, b_szt, b_tmpf, b_yn, b_junk, b_gss, b_trs = [Buf() for _ in range(7)]
                pcb = kb.ps(st, "pcb")
                pdiff = [kb.ps(st, "pdiff%d" % i) for i in range(2)]
                pyd = kb.ps(st, "pyd")
                pyo = kb.ps(st, "pyo")
                pst = kb.ps(st, "pst")
                psm = kb.ps(st, "psm")
                ptr = kb.ps(st, "sptr", (128, 1024), BF16)
                b_pcb, b_pyd, b_pyo, b_pst, b_psm, b_ptr = [Buf() for _ in range(6)]
                b_pdiff = [Buf(), Buf()]
                A = ada_setup(st, psum=(ptr[:].bitcast(F32), b_ptr))
                ada_chunks(A, 0, range(4, 6))
                ada_gate(A, 0)
                ada_chunks(A, 1, range(0, 6))
                ada_mod(A, 1)
                ada_gate(A, 1)
                li = 0
                gi = 0
                for d in range(2):
                    for g in range(4):
                        S.pool(lambda e, g=g: e.memset(H[:, g, :], 0.0), w=[b_H[g]])
                        S.pool(lambda e, g=g: e.memset(Hbf[:, g, :], 0.0), w=[b_Hbf[g]])
                    order = ([0, 1] + list(range(2, NT))) if d == 0 else ([1, 0] + list(range(NT - 1, 1, -1)))
                    Td = tri[:, d, :]
                    Wd = tri[:, 2 + d, :]
                    for tt in order:
                        is_lat = tt >= 2
                        l2 = li % 2
                        l3 = li % 3
                        li += 1
                        tok0 = tt * 128
                        S.dma("sp", xs_t[l3][:], xs_tm[tok0:tok0 + 128, :], r=[bd["xs_tm"]], w=[b_ld[l3]])
                        S.dma("sp", btm_t[l3][:], b_tm[tok0:tok0 + 128, :], r=[bd["b_tm"]], w=[b_ld[l3]])
                        S.dma("act", BTc[l3][:], BT[:, :, tok0:tok0 + 128].rearrange("g n t -> n g t"), r=[bd["BT"]], w=[b_ld[l3]])
                        S.dma("act", CTc[l3][:], CT[:, :, tok0:tok0 + 128].rearrange("g n t -> n g t"), r=[bd["CT"]], w=[b_ld[l3]])
                        if d == 1 and is_lat:
                            S.dma("sp", yf_t[:], yf_d[(tt - 2) * 128:(tt - 1) * 128, :], r=[bd["yf_d"]], w=[b_yf])
                        a_ap = dA_all[:, tt, d * 32:(d + 1) * 32]
                        dtp_ap = dtp_all[:, tt, d * 32:(d + 1) * 32]
                        S.pe(lambda e, a_ap=a_ap, Td=Td: e.matmul(psm[:, 0:32], lhsT=Td, rhs=a_ap, start=True, stop=True),
                             r=[b_sc], w=[b_psm], c=0.15)
                        S.pe(lambda e, a_ap=a_ap: e.matmul(psm[:, 32:64], lhsT=ones_f[:], rhs=a_ap, start=True, stop=True),
                             r=[b_sc, b_const], w=[b_psm], c=0.15)
                        smt, bsm = sm[l2], b_sm[l2]
                        S.dve(lambda e, smt=smt: e.tensor_copy(out=smt[:, 0, :], in_=psm[:, 0:32]), r=[b_psm], w=[bsm])
                        S.act(lambda e, smt=smt: e.activation(out=smt[:, 1, :], in_=psm[:, 0:32], func=AF.Exp), r=[b_psm], w=[bsm])
                        S.act(lambda e, smt=smt: e.activation(out=smt[:, 3, :], in_=psm[:, 32:64], func=AF.Exp), r=[b_psm], w=[bsm])
                        S.dve(lambda e, smt=smt: e.tensor_tensor(out=smt[:, 2, :], in0=psm[:, 32:64], in1=smt[:, 0, :], op=SUB),
                              r=[b_psm, bsm], w=[bsm])
                        S.act(lambda e, smt=smt: e.activation(out=smt[:, 2, :], in_=smt[:, 2, :], func=AF.Exp), r=[bsm], w=[bsm])
                        Xv = X[l2][:].rearrange("p (h c) -> p h c", c=64)
                        Xdv = Xd[l2][:].rearrange("p (h c) -> p h c", c=64)
                        xsv = xs_t[l3][:].rearrange("p (h c) -> p h c", c=64)
                        S.pool(lambda e, Xv=Xv, xsv=xsv, dtp_ap=dtp_ap: e.tensor_tensor(
                            out=Xv, in0=xsv, in1=dtp_ap.unsqueeze(2).broadcast_to([128, 32, 64]), op=MUL),
                               r=[b_ld[l3], b_sc], w=[b_X[l2]], c=6.0)
                        S.pool(lambda e, Xv=Xv, Xdv=Xdv, smt=smt: e.tensor_tensor(
                            out=Xdv, in0=Xv, in1=smt[:, 2, :].unsqueeze(2).broadcast_to([128, 32, 64]), op=MUL),
                               r=[b_X[l2], bsm], w=[b_Xd[l2]], c=6.0)
                        y_t, y_b = yt[l2], b_yt[l2]
                        for g in range(4):
                            g2 = gi % 2
                            gi += 1
                            if is_lat:
                                S.pe(lambda e, g=g, l2=l2, l3=l3: e.matmul(pcb[:, 0:128], lhsT=BTc[l3][:, g, :], rhs=CTc[l3][:, g, :],
                                                                    start=True, stop=True), r=[b_ld[l3]], w=[b_pcb], c=0.1)
                                S.dve(lambda e, g2=g2, Td=Td: e.tensor_tensor(out=CBm[g2][:], in0=pcb[:, 0:128], in1=Td, op=MUL),
                                      r=[b_pcb, b_sc], w=[b_CBm[g2]], c=0.25)
                                for h in range(8):
                                    S.act(lambda e, g=g, g2=g2, h=h, a_ap=a_ap, Td=Td: e.activation(
                                        out=rhsA[g2][:, h, :], in_=Td, func=AF.Copy, scale=a_ap[:, g * 8 + h:g * 8 + h + 1]),
                                          r=[b_sc], w=[b_rhsA[g2]], c=0.25)
                                for hf in range(2):
                                    S.pe(lambda e, hf=hf, g2=g2, Wd=Wd: e.matmul(
                                        pdiff[hf][:], lhsT=Wd, rhs=rhsA[g2][:, hf * 4:(hf + 1) * 4, :].rearrange("p h l -> p (h l)"),
                                        start=True, stop=True), r=[b_rhsA[g2], b_sc], w=[b_pdiff[hf]], c=0.9)
                                    S.act(lambda e, hf=hf, g2=g2: e.activation(
                                        out=expd[g2][:, hf * 4:(hf + 1) * 4, :].rearrange("p h l -> p (h l)"), in_=pdiff[hf][:], func=AF.Exp),
                                          r=[b_pdiff[hf]], w=[b_expd[g2]])
                                S.dve(lambda e, g2=g2: e.tensor_tensor(out=MT[g2][:], in0=expd[g2][:],
                                                                       in1=CBm[g2][:].unsqueeze(1).broadcast_to([128, 8, 128]), op=MUL),
                                      r=[b_expd[g2], b_CBm[g2]], w=[b_MT[g2]], c=0.7)
                                if d == 1:
                                    S.pe(lambda e, g=g: e.matmul(pyd[:], lhsT=ident_f[:], rhs=yf_t[:, g * 512:(g + 1) * 512],
                                                                 start=True, stop=False), r=[b_yf, b_const], w=[b_pyd], c=0.9)
                                for h in range(8):
                                    c0 = (g * 8 + h) * 64
                                    S.pe(lambda e, h=h, g2=g2, l2=l2, l3=l3, c0=c0, d=d: e.matmul(
                                        pyd[:, h * 64:(h + 1) * 64], lhsT=MT[g2][:, h, :], rhs=X[l2][:, c0:c0 + 64],
                                        start=(d == 0), stop=(d == 1 and h == 7)),
                                         r=[b_MT[g2], b_X[l2]], w=[b_pyd], c=0.1)
                                    if d == 0:
                                        S.pe(lambda e, h=h, g=g, l3=l3, c0=c0: e.matmul(
                                            pyd[:, h * 64:(h + 1) * 64], lhsT=Dsk[:, g * 8 + h, :], rhs=xs_t[l3][:, c0:c0 + 64],
                                            start=False, stop=True), r=[b_dsk, b_ld[l3]], w=[b_pyd], c=0.1)
                                S.pe(lambda e, g=g, l2=l2, l3=l3: e.matmul(pyo[:], lhsT=CTc[l3][:, g, :], rhs=Hbf[:, g, :], start=True, stop=True),
                                     r=[b_ld[l3], b_Hbf[g]], w=[b_pyo])
                                S.dve(lambda e, g=g, g2=g2, smt=smt: e.tensor_tensor(
                                    out=tmpy[g2][:].rearrange("p (h c) -> p h c", c=64), in0=pyo[:].rearrange("p (h c) -> p h c", c=64),
                                    in1=smt[:, 1, g * 8:(g + 1) * 8].unsqueeze(2).broadcast_to([128, 8, 64]), op=MUL),
                                      r=[b_pyo, bsm], w=[b_tmpy[g2]])
                                S.dve(lambda e, g=g, g2=g2, y_t=y_t: e.tensor_tensor(out=y_t[:, g * 512:(g + 1) * 512], in0=pyd[:],
                                                                                    in1=tmpy[g2][:], op=ADD),
                                      r=[b_pyd, b_tmpy[g2]], w=[y_b])
                            S.pe(lambda e, g=g, l2=l2, l3=l3: e.matmul(pst[:], lhsT=btm_t[l3][:, g * 128:(g + 1) * 128],
                                                                rhs=Xd[l2][:, g * 512:(g + 1) * 512], start=True, stop=True),
                                 r=[b_ld[l3], b_Xd[l2]], w=[b_pst])
                            S.pool(lambda e, g=g, smt=smt: e.tensor_tensor(
                                out=H[:, g, :].rearrange("p (h c) -> p h c", c=64), in0=H[:, g, :].rearrange("p (h c) -> p h c", c=64),
                                in1=smt[:, 3, g * 8:(g + 1) * 8].unsqueeze(2).broadcast_to([128, 8, 64]), op=MUL),
                                   r=[b_H[g], bsm], w=[b_H[g]], c=2.0)
                            S.dve(lambda e, g=g: e.tensor_tensor(out=H[:, g, :], in0=H[:, g, :], in1=pst[:], op=ADD),
                                  r=[b_H[g], b_pst], w=[b_H[g]])
                            S.act(lambda e, g=g: e.activation(out=Hbf[:, g, :], in_=H[:, g, :], func=AF.Copy), r=[b_H[g]], w=[b_Hbf[g]])
                        if not is_lat:
                            continue
                        lt = tt - 2
                        if d == 0:
                            S.dma("sp", yf_d[lt * 128:(lt + 1) * 128, :], y_t[:], r=[y_b], w=[bd["yf_d"]])
                            continue
                        S.dma("act", sz_t[:], sz[lt * 128:(lt + 1) * 128, :], r=[bd["sz"]], w=[b_szt])
                        S.dve(lambda e, y_t=y_t: e.tensor_tensor(out=y_t[:], in0=y_t[:], in1=sz_t[:], op=MUL), r=[y_b, b_szt], w=[y_b], c=2.3)
                        for g in range(4):
                            S.act(lambda e, g=g, y_t=y_t: e.activation(out=junk[:], in_=y_t[:, g * 512:(g + 1) * 512], func=AF.Square,
                                                                        accum_out=gss[:, g:g + 1]), r=[y_b], w=[b_junk, b_gss])
                        S.act(lambda e: e.activation(out=gss[:, 4:8], in_=gss[:, 0:4], func=AF.Sqrt, scale=1.0 / 512, bias=eps_t[:, 0:1]),
                              r=[b_gss, b_const], w=[b_gss])
                        S.dve(lambda e: e.reciprocal(out=gss[:, 8:12], in_=gss[:, 4:8]), r=[b_gss], w=[b_gss])
                        for g in range(4):
                            S.dve(lambda e, g=g, y_t=y_t: e.scalar_tensor_tensor(
                                out=yn[:, g * 512:(g + 1) * 512], in0=y_t[:, g * 512:(g + 1) * 512], scalar=gss[:, 8 + g:9 + g],
                                in1=ssdn[:, g * 512:(g + 1) * 512], op0=MUL, op1=MUL), r=[y_b, b_gss, b_sc], w=[b_yn])
                        for half in range(2):
                            for i in range(8):
                                cbi = half * 8 + i
                                S.pe(lambda e, i=i, cbi=cbi: e.transpose(ptr[:, i * 128:(i + 1) * 128], yn[:, cbi * 128:(cbi + 1) * 128], ident_b[:]),
                                     r=[b_yn, b_const], w=[b_ptr], c=0.1)
                            S.act(lambda e, half=half: e.activation(out=trs[:, half * 8:(half + 1) * 8, :].rearrange("p i c -> p (i c)"),
                                                                    in_=ptr[:], func=AF.Copy), r=[b_ptr], w=[b_trs])
                        S.dma("sp", mixT[lt, :, 16:32, :], trs[:], r=[b_trs], w=[bd["mixT"]])
                S.flush()

        if want("att"):
            with contextlib.ExitStack() as st:
                kTg = [kb.sb(st, "kTg%d" % i, [128, NTOK], BF16) for i in range(2)]
                vg = [kb.sb(st, "vg%d" % i, [128, NT, 128], BF16) for i in range(2)]
                qh = [kb.sb(st, "qh%d" % i, [128, SEQ], BF16) for i in range(2)]
                sgh = [kb.sb(st, "sgh%d" % i, [128, SEQ], BF16) for i in range(2)]
                b_kv, b_qh = [Buf(), Buf()], [Buf(), Buf()]
                PT = [kb.sb(st, "PT%d" % i, [128, 512], BF16) for i in range(6)]
                b_PT = [Buf() for _ in range(6)]
                negone = kb.sb(st, "negone", [128, 512], F32)
                b_negone = Buf()
                S.pool(lambda e: e.memset(negone[:], -1.0), w=[b_negone])
                PS = [kb.sb(st, "PS%d" % i, [128, 512], BF16) for i in range(3)]
                b_PS = [Buf() for _ in range(3)]
                gsi = 0
                rden = [kb.sb(st, "rden%d" % i, [128, 512], F32) for i in range(2)]
                ao = [kb.sb(st, "ao%d" % i, [128, 512], F32) for i in range(2)]
                aob = [kb.sb(st, "aob%d" % i, [128, 512], BF16) for i in range(2)]
                b_rden, b_ao, b_aob = [[Buf(), Buf()] for _ in range(3)]
                pS = [kb.ps(st, "pS%d" % i) for i in range(3)]
                pO = [kb.ps(st, "pO%d" % i) for i in range(2)]
                pD = [kb.ps(st, "pD%d" % i) for i in range(2)]
                b_pS = [Buf() for _ in range(3)]
                b_pO, b_pD = [Buf(), Buf()], [Buf(), Buf()]

                si = 0
                oi = 0
                sc = 1.0 / math.sqrt(128.0)
                for g in range(4):
                    kv2 = g % 2
                    S.dma("sp", kTg[kv2][:], kT[g], r=[bd["kT"]], w=[b_kv[kv2]])
                    S.dma("act", vg[kv2][:], vtm[:, g * 128:(g + 1) * 128].rearrange("(t p) c -> p t c", p=128), r=[bd["vtm"]], w=[b_kv[kv2]])
                    for hh in range(4):
                        h = g * 4 + hh
                        q2 = h % 2
                        S.dma("sp", qh[q2][:], qT[h], r=[bd["qT"]], w=[b_qh[q2]])
                        S.dma("act", sgh[q2][:], sgT[h], r=[bd["sgT"]], w=[b_qh[q2]])
                        for qb in range(4):
                            o2 = oi % 2
                            oi += 1
                            for kt in range(NT):
                                s3 = si % 3
                                p6 = si % 6
                                si += 1
                                S.pe(lambda e, kt=kt, s3=s3, kv2=kv2, q2=q2, qb=qb: e.matmul(
                                    pS[s3][:], lhsT=kTg[kv2][:, kt * 128:(kt + 1) * 128], rhs=qh[q2][:, qb * 512:(qb + 1) * 512],
                                    start=True, stop=True), r=[b_kv[kv2], b_qh[q2]], w=[b_pS[s3]])
                                S.act(lambda e, s3=s3, p6=p6: e.activation(out=PT[p6][:], in_=pS[s3][:], func=AF.Exp, scale=sc),
                                      r=[b_pS[s3]], w=[b_PT[p6]], c=0.6)
                                S.pe(lambda e, kt=kt, p6=p6, kv2=kv2, o2=o2: e.matmul(
                                    pO[o2][:], lhsT=vg[kv2][:, kt, :], rhs=PT[p6][:], start=(kt == 0), stop=(kt == NT - 1)),
                                     r=[b_kv[kv2], b_PT[p6]], w=[b_pO[o2]])
                                if kt % 3 == 2:
                                    pa, pb, pc = (p6 - 2) % 6, (p6 - 1) % 6, p6
                                    g3 = gsi % 3
                                    gsi += 1
                                    S.dve(lambda e, pa=pa, pb=pb, g3=g3: e.tensor_tensor(out=PS[g3][:], in0=PT[pa][:], in1=PT[pb][:], op=ADD),
                                          r=[b_PT[pa], b_PT[pb]], w=[b_PS[g3]], c=0.45)
                                    S.dve(lambda e, pc=pc, g3=g3: e.tensor_tensor(out=PS[g3][:], in0=PS[g3][:], in1=PT[pc][:], op=ADD),
                                          r=[b_PS[g3], b_PT[pc]], w=[b_PS[g3]], c=0.45)
                                    S.pe(lambda e, kt=kt, g3=g3, o2=o2: e.matmul(
                                        pD[o2][:], lhsT=ones_b[:], rhs=PS[g3][:], start=(kt == 2), stop=(kt == NT - 1)),
                                         r=[b_const, b_PS[g3]], w=[b_pD[o2]])
                            S.act(lambda e, o2=o2: e.activation(out=rden[o2][:], in_=pD[o2][:], func=AF.Ln), r=[b_pD[o2]], w=[b_rden[o2]], c=0.6)
                            S.act(lambda e, o2=o2: e.activation(out=rden[o2][:], in_=rden[o2][:], func=AF.Exp, scale=-1.0),
                                  r=[b_rden[o2]], w=[b_rden[o2]], c=0.6)
                            S.dve(lambda e, o2=o2: e.tensor_tensor(out=ao[o2][:], in0=pO[o2][:], in1=rden[o2][:], op=MUL),
                                  r=[b_pO[o2], b_rden[o2]], w=[b_ao[o2]])
                            S.pool(lambda e, o2=o2, q2=q2, qb=qb: e.tensor_tensor(out=aob[o2][:], in0=ao[o2][:],
                                                                                   in1=sgh[q2][:, qb * 512:(qb + 1) * 512], op=MUL),
                                   r=[b_ao[o2], b_qh[q2]], w=[b_aob[o2]], c=2.0)
                            S.dma("sp", mixT[qb * 4:(qb + 1) * 4, :, h, :].rearrange("t p c -> p t c"),
                                  aob[o2][:].rearrange("p (t c) -> p t c", c=128), r=[b_aob[o2]], w=[bd["mixT"]])
                S.flush()

        def out_phase(layer, mix_d, mix_b, w_ap, xsrc, xsrc_b, dst, dst_b):
            with contextlib.ExitStack() as st:
                wo_t = [kb.sb(st, "wo%d" % i, [128, 32, 512], BF16) for i in range(2)]
                b_wo = [WB(), WB()]
                mx = [kb.sb(st, "mx%d" % i, [128, 32, 128], BF16) for i in range(2)]
                xr = [kb.sb(st, "xr%d" % i, [128, 512], F32) for i in range(2)]
                oo = [kb.sb(st, "oo%d" % i, [128, 512], F32) for i in range(2)]
                b_mx, b_xr, b_oo = [[Buf(), Buf()] for _ in range(3)]
                pacc = [kb.ps(st, "oacc%d" % i) for i in range(2)]
                b_pacc = [Buf(), Buf()]
                it = 0
                for cbk in range(4):
                    w_t, w_b = wo_t[cbk % 2], b_wo[cbk % 2]
                    src = w_ap[:, cbk * 512:(cbk + 1) * 512].rearrange("(kc p) n -> p kc n", p=128)
                    for k0 in range(0, 32, 4):
                        S.dma("pool", w_t[:, k0:k0 + 4, :], src[:, k0:k0 + 4, :], w=[w_b.p(k0)])
                    for tt in range(16):
                        i2 = it % 2
                        it += 1
                        S.dma("sp", mx[i2][:], mix_d[tt], r=[mix_b], w=[b_mx[i2]])
                        S.dma("act", xr[i2][:], xsrc[tt * 128:(tt + 1) * 128, cbk * 512:(cbk + 1) * 512], r=xsrc_b, w=[b_xr[i2]])
                        for kc in range(32):
                            S.pe(lambda e, kc=kc, i2=i2, w_t=w_t: e.matmul(pacc[i2][:], lhsT=mx[i2][:, kc, :], rhs=w_t[:, kc, :],
                                                                            start=(kc == 0), stop=(kc == 31)),
                                 r=[b_mx[i2], w_b.p(kc)], w=[b_pacc[i2]])
                        S.dve(lambda e, i2=i2, cbk=cbk: e.tensor_tensor(out=oo[i2][:], in0=pacc[i2][:],
                                                                        in1=gate_bc[:, layer, cbk * 512:(cbk + 1) * 512], op=MUL),
                              r=[b_pacc[i2], b_gate], w=[b_oo[i2]])
                        S.pool(lambda e, i2=i2: e.tensor_tensor(out=oo[i2][:], in0=oo[i2][:], in1=xr[i2][:], op=ADD),
                               r=[b_oo[i2], b_xr[i2]], w=[b_oo[i2]], c=2.0)
                        S.dma("sp", dst[tt * 128:(tt + 1) * 128, cbk * 512:(cbk + 1) * 512], oo[i2][:], r=[b_oo[i2]], w=[dst_b])
                S.flush()

        if want("out0"):
            out_phase(0, mixT, bd["mixT"], I["ev_w_out"], I["x"], [], x1, bd["x1"])

        if want("norm1"):
            hctx = contextlib.ExitStack()
            hT = kb.sb(hctx, "hT1", [128, 16, NTOK], BF16)
            srcs = []
            for tb in range(4):
                srcs.append((x1[tb * 512:(tb + 1) * 512, :], [bd["x1"]], CTX + tb * 512, 4, b_hT[1 + tb], 0))
            stn1 = contextlib.ExitStack()
            norm_phase(1, srcs, st=stn1)

        if want("proj1"):
            with contextlib.ExitStack() as st:
                wg = [kb.sb(st, "w1_%d" % i, [128, 16, 512], BF16) for i in range(2)]
                b_wg = [WB(), WB()]
                og = [kb.sb(st, "o1_%d" % i, [128, 512], BF16) for i in range(3)]
                b_og = [Buf() for _ in range(3)]
                pacc = [kb.ps(st, "p1acc%d" % i) for i in range(3)]
                b_pacc = [Buf() for _ in range(3)]
                it = 0
                for blk in range(16):
                    w_t, w_b = wg[blk % 2], b_wg[blk % 2]
                    load_w(w_t, w_b, I["od_w_in"], blk * 512, 512)
                    isu = blk < 8
                    for hh in range(4):
                        j = (blk % 8) * 4 + hh
                        for (tok0, ntok, hbuf, lat) in tblocks[1:]:
                            i3 = it % 3
                            it += 1
                            for kc in range(16):
                                S.pe(lambda e, kc=kc, w_t=w_t, i3=i3, tok0=tok0, hh=hh: e.matmul(
                                    pacc[i3][:], lhsT=w_t[:, kc, hh * 128:(hh + 1) * 128], rhs=hT[:, kc, tok0:tok0 + 512],
                                    start=(kc == 0), stop=(kc == 15)), r=[w_b.p(kc), hbuf], w=[b_pacc[i3]])
                            if isu:
                                S.dve(lambda e, i3=i3: e.tensor_copy(out=og[i3][:], in_=pacc[i3][:]), r=[b_pacc[i3]], w=[b_og[i3]])
                                S.dma("sp", uT[j, :, lat:lat + 512], og[i3][:], r=[b_og[i3]], w=[bd["uT"]])
                            else:
                                S.act(lambda e, i3=i3: e.activation(out=og[i3][:], in_=pacc[i3][:], func=AF.Silu),
                                      r=[b_pacc[i3]], w=[b_og[i3]])
                                S.dma("sp", szT[j, :, lat:lat + 512], og[i3][:], r=[b_og[i3]], w=[bd["szT"]])
                S.flush()
            stn1.close()
            hctx.close()

        if want("dft1"):
            with contextlib.ExitStack() as st:
                dftc = kb.sb(st, "dftc", [128, 2, 4, 512], BF16)
                b_dc = Buf()
                S.dma("sp", dftc[:], I["dftc"], w=[b_dc])
                ug = [kb.sb(st, "ug%d" % i, [128, 4, SEQ], BF16) for i in range(2)]
                b_ug = [Buf(), Buf()]
                uo = [[kb.sb(st, "uo%d_%d" % (cs, i), [128, 16, 512], BF16) for i in range(2)] for cs in range(2)]
                b_uo = [[Buf(), Buf()], [Buf(), Buf()]]
                pacc = [kb.ps(st, "dacc%d" % i) for i in range(3)]
                b_pacc = [Buf() for _ in range(3)]
                it = 0
                for g in range(8):
                    g2 = g % 2
                    S.dma("sp", ug[g2][:, 0:2, :], uT[g * 4:g * 4 + 2].rearrange("j p t -> p j t"), r=[bd["uT"]], w=[b_ug[g2]])
                    S.dma("act", ug[g2][:, 2:4, :], uT[g * 4 + 2:g * 4 + 4].rearrange("j p t -> p j t"), r=[bd["uT"]], w=[b_ug[g2]])
                    for tt in range(16):
                        t0 = (tt // 8) + 256 * (tt % 8)
                        for cs in range(2):
                            i3 = it % 3
                            it += 1
                            for cc in range(4):
                                S.pe(lambda e, cc=cc, cs=cs, t0=t0, g2=g2, i3=i3: e.matmul(
                                    pacc[i3][:], lhsT=ug[g2][:, cc, t0:t0 + 255:2], rhs=dftc[:, cs, cc, :],
                                    start=(cc == 0), stop=(cc == 3)), r=[b_ug[g2], b_dc], w=[b_pacc[i3]])
                            if cs == 0:
                                S.dve(lambda e, i3=i3, tt=tt, g2=g2: e.tensor_copy(out=uo[0][g2][:, tt, :], in_=pacc[i3][:]),
                                      r=[b_pacc[i3]], w=[b_uo[0][g2]])
                            else:
                                S.act(lambda e, i3=i3, tt=tt, g2=g2: e.activation(out=uo[1][g2][:, tt, :], in_=pacc[i3][:], func=AF.Copy),
                                      r=[b_pacc[i3]], w=[b_uo[1][g2]])
                    for cs in range(2):
                        S.dma("sp", UCd[cs, g], uo[cs][g2][:], r=[b_uo[cs][g2]], w=[bd["UCd"]], lat=12.0)
                S.flush()

        if want("dft2"):
            with contextlib.ExitStack() as st:
                dftl = kb.sb(st, "dftl", [128, 2, 2, 8, 1024], BF16)
                b_dl = [[Buf(), Buf()], [Buf(), Buf()]]
                for par in range(2):
                    for cs in range(2):
                        S.dma("sp" if cs == 0 else "act", dftl[:, par, cs, :, :], I["dftl"][:, par, cs, :, :], w=[b_dl[par][cs]])
                ucs = [kb.sb(st, "ucs%d" % i, [128, 2, 16, 512], BF16) for i in range(2)]
                szj = [kb.sb(st, "szj%d" % i, [128, SEQ], BF16) for i in range(2)]
                b_ucs, b_szj = [Buf(), Buf()], [Buf(), Buf()]
                osb = [kb.sb(st, "osb%d" % i, [128, 512], F32) for i in range(2)]
                fa = [kb.sb(st, "fa%d" % i, [128, 2, 512], F32) for i in range(2)]
                mo = [kb.sb(st, "mo%d" % i, [128, 2, 512], BF16) for i in range(2)]
                b_osb, b_fa, b_mo = [[Buf(), Buf()] for _ in range(3)]
                pE = [kb.ps(st, "pE%d" % i) for i in range(2)]
                pO = [kb.ps(st, "pOd%d" % i) for i in range(2)]
                b_pE, b_pO = [Buf(), Buf()], [Buf(), Buf()]
                it = 0
                for j in range(32):
                    j2 = j % 2
                    u2 = (j // 4) % 2
                    jl = j % 4
                    if jl == 0:
                        S.dma("sp", ucs[u2][:, 0, :, :], UCd[0, j // 4], r=[bd["UCd"]], w=[b_ucs[u2]])
                        S.dma("act", ucs[u2][:, 1, :, :], UCd[1, j // 4], r=[bd["UCd"]], w=[b_ucs[u2]])
                    S.dma("sp", szj[j2][:], szT[j], r=[bd["szT"]], w=[b_szj[j2]])
                    for kbk in range(2):
                        i2 = it % 2
                        it += 1
                        for par, (pp, bpp) in enumerate(((pE[i2], b_pE[i2]), (pO[i2], b_pO[i2]))):
                            n = 0
                            for tt8 in range(8):
                                for cs in range(2):
                                    S.pe(lambda e, tt8=tt8, cs=cs, u2=u2, jl=jl, pp=pp, kbk=kbk, n=n, par=par: e.matmul(
                                        pp[:], lhsT=ucs[u2][:, cs, par * 8 + tt8, jl * 128:(jl + 1) * 128],
                                        rhs=dftl[:, par, cs, tt8, kbk * 512:(kbk + 1) * 512],
                                        start=(n == 0), stop=(n == 15)), r=[b_ucs[u2], b_dl[par][cs]], w=[bpp])
                                    n += 1
                        S.act(lambda e, i2=i2: e.activation(out=osb[i2][:], in_=pO[i2][:], func=AF.Copy), r=[b_pO[i2]], w=[b_osb[i2]], c=0.6)
                        S.dve(lambda e, i2=i2: e.tensor_tensor(out=fa[i2][:, 0, :], in0=pE[i2][:], in1=osb[i2][:], op=ADD),
                              r=[b_pE[i2], b_osb[i2]], w=[b_fa[i2]])
                        S.dve(lambda e, i2=i2: e.tensor_tensor(out=fa[i2][:, 1, :], in0=pE[i2][:], in1=osb[i2][:], op=SUB),
                              r=[b_pE[i2], b_osb[i2]], w=[b_fa[i2]])
                        k0 = kbk * 512
                        S.pool(lambda e, i2=i2, j2=j2, k0=k0: e.tensor_tensor(out=mo[i2][:, 0, :], in0=fa[i2][:, 0, :],
                                                                              in1=szj[j2][:, k0:k0 + 512], op=MUL),
                               r=[b_fa[i2], b_szj[j2]], w=[b_mo[i2]], c=2.0)
                        S.dve(lambda e, i2=i2, j2=j2, k0=k0: e.tensor_tensor(out=mo[i2][:, 1, :], in0=fa[i2][:, 1, :],
                                                                             in1=szj[j2][:, 1024 + k0:1024 + k0 + 512], op=MUL),
                              r=[b_fa[i2], b_szj[j2]], w=[b_mo[i2]])
                        for hf in range(2):
                            tt0 = hf * 8 + kbk * 4
                            S.dma("sp", mixT1[tt0:tt0 + 4, :, j, :].rearrange("t p c -> p t c"),
                                  mo[i2][:, hf, :].rearrange("p (t c) -> p t c", c=128), r=[b_mo[i2]], w=[bd["mixT1"]])
                S.flush()

        if want("out1"):
            b_out = Buf("y")
            out_phase(1, mixT1, bd["mixT1"], I["od_w_out"], x1, [bd["x1"]], kb.out, b_out)

        S.flush()
    return kb


_CONSTS = None


def _prep_inputs(inputs, b):
    global _CONSTS
    if _CONSTS is None:
        _CONSTS = host_consts()
    m = {}
    for k, shp in IN_SHAPES.items():
        a = np.asarray(inputs[k])
        if k in ("x", "c", "ctx"):
            a = a[b]
        elif k == "c_ctx":
            pass
        else:
            a = a[0]
        m[k] = np.ascontiguousarray(a, dtype=np.float32).reshape(shp)
    m.update(_CONSTS)
    return m


def kernel(**inputs):
    kb = build()
    in_maps = [_prep_inputs(inputs, b) for b in range(8)]
    res = run_bass_kernel_spmd(kb.nc, in_maps, core_ids=list(range(8)))
    return np.stack([np.asarray(r["y"], dtype=np.float32) for r in res.results], axis=0)
```
